# Optimizing a Trainium2 kernel written in Bass

```python
import jax, jax.numpy as jnp
from jax import lax
import numpy as np

D_MODEL = 4096
BATCH = 1
SEQ = 8192
DEPTH = 1

N_META = 16
GLA_HEADS = 8
GLA_DK = 128
GLA_DV = 256
GLA_GATE_RANK = 16
GLA_GATE_NORM = 16.0
GLA_CHUNK = 64
MLA_HEADS = 16
MLA_Q_RANK = 1024
MLA_KV_RANK = 512
MLA_NOPE = 128
MLA_ROPE = 64
MLA_V = 128
ROPE_THETA = 10000.0
Q_BLOCK = 128
D_FF = 4 * D_MODEL
EPS = 1e-6

GLA_QK_W = GLA_HEADS * GLA_DK
GLA_V_W = GLA_HEADS * GLA_DV
MLA_QK_HEAD = MLA_NOPE + MLA_ROPE
MLA_V_W = MLA_HEADS * MLA_V
IN_SPLITS = (GLA_QK_W, GLA_QK_W, GLA_V_W, GLA_V_W, GLA_GATE_RANK, GLA_GATE_RANK,
             MLA_Q_RANK, MLA_KV_RANK, MLA_ROPE, D_MODEL, D_MODEL)
IN_WIDTH = sum(IN_SPLITS)

kernel_name = "bidir_gla_mla_gated_hybrid"


def rmsnorm(x, g):
    xf = x.astype(jnp.float32)
    y = xf * lax.rsqrt(jnp.mean(xf * xf, axis=-1, keepdims=True) + EPS)
    return (y * g.astype(jnp.float32)).astype(x.dtype)


def apply_rope(x, cos, sin):
    xf = x.astype(jnp.float32)
    x1, x2 = jnp.split(xf, 2, axis=-1)
    c = cos[:, None, :]
    s = sin[:, None, :]
    return jnp.concatenate([x1 * c - x2 * s, x1 * s + x2 * c], axis=-1).astype(x.dtype)


def gla_direction(q, k, v, logg):
    B, H, T, dk = q.shape
    dv = v.shape[-1]
    C = GLA_CHUNK
    n = T // C

    def chunks(a):
        return jnp.moveaxis(a.reshape(B, H, n, C, a.shape[-1]), 2, 0)

    qc, kc, vc = chunks(q), chunks(k), chunks(v)
    bc = jnp.cumsum(chunks(logg), axis=-2)
    mask = jnp.tril(jnp.ones((C, C), dtype=bool))[:, :, None]

    def step(S, inp):
        qi, ki, vi, bi = inp
        inter = jnp.einsum('bhtd,bhde->bhte', qi * jnp.exp(bi), S)
        diff = bi[:, :, :, None, :] - bi[:, :, None, :, :]
        decay = jnp.exp(jnp.where(mask, diff, -jnp.inf))
        att = jnp.einsum('bhtsd,bhsd->bhts', qi[:, :, :, None, :] * decay, ki)
        intra = jnp.einsum('bhts,bhse->bhte', att, vi)
        last = bi[:, :, -1:, :]
        S_new = jnp.exp(last[:, :, 0, :])[..., None] * S + jnp.einsum(
            'bhsd,bhse->bhde', ki * jnp.exp(last - bi), vi)
        return S_new, inter + intra

    S0 = jnp.zeros((B, H, dk, dv), jnp.float32)
    _, out = lax.scan(step, S0, (qc, kc, vc, bc))
    return jnp.moveaxis(out, 0, 2).reshape(B, H, T, dv)


def block_attention(q, k, v):
    B, L, H, Dq = q.shape
    nb = -(-L // Q_BLOCK)
    Lq = nb * Q_BLOCK
    scale = Dq ** -0.5
    qb = jnp.pad(q, ((0, 0), (0, Lq - L), (0, 0), (0, 0)))
    qb = qb.reshape(B, nb, Q_BLOCK, H, Dq).transpose(1, 0, 3, 2, 4)
    kt = k.transpose(0, 2, 1, 3)
    vt = v.transpose(0, 2, 1, 3)

    def one(qi):
        s = jnp.einsum('bhqd,bhkd->bhqk', qi, kt, preferred_element_type=jnp.float32) * scale
        p = jax.nn.softmax(s, axis=-1)
        return jnp.einsum('bhqk,bhkd->bhqd', p.astype(vt.dtype), vt)

    o = lax.map(one, qb)
    return o.transpose(1, 0, 3, 2, 4).reshape(B, Lq, H, -1)[:, :L]


def hybrid_layer(h, cos, sin, ln1, w_in, gla_wf, gla_bf, gla_wb, gla_bb, gla_norm,
                 q_norm, w_uq, kv_norm, w_ukv, w_gla_out, w_mla_out, w_o,
                 ln2, w_ff1, w_ff2):
    B, L, _ = h.shape
    f32 = jnp.float32
    xn = rmsnorm(h, ln1)
    proj = xn @ w_in
    split_idx = [int(i) for i in np.cumsum(IN_SPLITS)[:-1]]
    (gq, gk, gv, gog, glrf, glrb, cq, ckv, krope, gate_a, gate_b) = jnp.split(proj, split_idx, axis=-1)

    pad = GLA_CHUNK - N_META

    def heads(a, d):
        a = a.reshape(B, L, -1, d).transpose(0, 2, 1, 3).astype(f32)
        return jnp.pad(a, ((0, 0), (0, 0), (pad, 0), (0, 0)))

    q = heads(gq, GLA_DK) * (GLA_DK ** -0.5)
    k = heads(gk, GLA_DK)
    v = heads(gv, GLA_DV)
    lg_f = heads(jax.nn.log_sigmoid((glrf @ gla_wf + gla_bf).astype(f32)) / GLA_GATE_NORM, GLA_DK)
    lg_b = heads(jax.nn.log_sigmoid((glrb @ gla_wb + gla_bb).astype(f32)) / GLA_GATE_NORM, GLA_DK)
    flip = lambda a: jnp.flip(a, axis=2)
    o_f = gla_direction(q, k, v, lg_f)
    o_b = flip(gla_direction(flip(q), flip(k), flip(v), flip(lg_b)))
    o = (o_f + o_b)[:, :, pad:, :].transpose(0, 2, 1, 3)
    o = rmsnorm(o, gla_norm).reshape(B, L, GLA_V_W).astype(h.dtype)
    y_gla = (o * jax.nn.silu(gog)) @ w_gla_out

    qf = (rmsnorm(cq, q_norm) @ w_uq).reshape(B, L, MLA_HEADS, MLA_QK_HEAD)
    q_nope, q_pe = qf[..., :MLA_NOPE], qf[..., MLA_NOPE:]
    q_pe = apply_rope(q_pe, cos, sin)
    kv = (rmsnorm(ckv, kv_norm) @ w_ukv).reshape(B, L, MLA_HEADS, MLA_NOPE + MLA_V)
    k_nope, v_m = kv[..., :MLA_NOPE], kv[..., MLA_NOPE:]
    k_pe = apply_rope(krope[:, :, None, :], cos, sin)
    qh = jnp.concatenate([q_nope, q_pe], axis=-1)
    kh = jnp.concatenate([k_nope, jnp.broadcast_to(k_pe, (B, L, MLA_HEADS, MLA_ROPE))], axis=-1)
    y_mla = block_attention(qh, kh, v_m).reshape(B, L, MLA_V_W) @ w_mla_out

    merged = jax.nn.sigmoid(gate_a) * y_gla + jax.nn.sigmoid(gate_b) * y_mla
    h = h + merged @ w_o

    hn = rmsnorm(h, ln2)
    h = h + jnp.square(jax.nn.relu(hn @ w_ff1)) @ w_ff2
    return h


def setup_inputs(seed: int = 0) -> dict:
    key = jax.random.key(seed)
    ks = jax.random.split(key, 24)
    f32 = jnp.float32

    def nrm(k, shape, scale):
        return jax.random.normal(k, shape, f32) * scale

    def gain(k, shape):
        return 1.0 + 0.01 * jax.random.normal(k, shape, f32)

    return {
        "x": nrm(ks[0], (BATCH, SEQ, D_MODEL), 1.0),
        "meta_tokens": nrm(ks[1], (N_META, D_MODEL), 1.0),
        "ln1": gain(ks[2], (DEPTH, D_MODEL)),
        "w_in": nrm(ks[3], (DEPTH, D_MODEL, IN_WIDTH), D_MODEL ** -0.5),
        "gla_wf": nrm(ks[4], (DEPTH, GLA_GATE_RANK, GLA_QK_W), GLA_GATE_RANK ** -0.5),
        "gla_bf": nrm(ks[5], (DEPTH, GLA_QK_W), 0.01),
        "gla_wb": nrm(ks[6], (DEPTH, GLA_GATE_RANK, GLA_QK_W), GLA_GATE_RANK ** -0.5),
        "gla_bb": nrm(ks[7], (DEPTH, GLA_QK_W), 0.01),
        "gla_norm": gain(ks[8], (DEPTH, GLA_DV)),
        "q_norm": gain(ks[9], (DEPTH, MLA_Q_RANK)),
        "w_uq": nrm(ks[10], (DEPTH, MLA_Q_RANK, MLA_HEADS * MLA_QK_HEAD), MLA_Q_RANK ** -0.5),
        "kv_norm": gain(ks[11], (DEPTH, MLA_KV_RANK)),
        "w_ukv": nrm(ks[12], (DEPTH, MLA_KV_RANK, MLA_HEADS * (MLA_NOPE + MLA_V)), MLA_KV_RANK ** -0.5),
        "w_gla_out": nrm(ks[13], (DEPTH, GLA_V_W, D_MODEL), GLA_V_W ** -0.5),
        "w_mla_out": nrm(ks[14], (DEPTH, MLA_V_W, D_MODEL), MLA_V_W ** -0.5),
        "w_o": nrm(ks[15], (DEPTH, D_MODEL, D_MODEL), D_MODEL ** -0.5),
        "ln2": gain(ks[16], (DEPTH, D_MODEL)),
        "w_ff1": nrm(ks[17], (DEPTH, D_MODEL, D_FF), D_MODEL ** -0.5),
        "w_ff2": nrm(ks[18], (DEPTH, D_FF, D_MODEL), D_FF ** -0.5),
        "final_norm": gain(ks[19], (D_MODEL,)),
    }


def reference(x, meta_tokens, ln1, w_in, gla_wf, gla_bf, gla_wb, gla_bb, gla_norm,
              q_norm, w_uq, kv_norm, w_ukv, w_gla_out, w_mla_out, w_o,
              ln2, w_ff1, w_ff2, final_norm):
    B = x.shape[0]
    meta = jnp.broadcast_to(meta_tokens[None].astype(x.dtype), (B, N_META, D_MODEL))
    h = jnp.concatenate([meta, x], axis=1)
    L = h.shape[1]
    pos = jnp.arange(L, dtype=jnp.float32)
    inv_freq = ROPE_THETA ** (-jnp.arange(0, MLA_ROPE, 2, dtype=jnp.float32) / MLA_ROPE)
    ang = pos[:, None] * inv_freq[None, :]
    cos, sin = jnp.cos(ang), jnp.sin(ang)
    for i in range(DEPTH):
        h = hybrid_layer(h, cos, sin, ln1[i], w_in[i], gla_wf[i], gla_bf[i], gla_wb[i], gla_bb[i],
                         gla_norm[i], q_norm[i], w_uq[i], kv_norm[i], w_ukv[i], w_gla_out[i],
                         w_mla_out[i], w_o[i], ln2[i], w_ff1[i], w_ff2[i])
    return rmsnorm(h, final_norm)[:, N_META:]
```

```python
import numpy as np
import ml_dtypes
from contextlib import ExitStack
import concourse.bass as bass
import concourse.mybir as mybir
from concourse.bass_utils import run_bass_kernel_spmd

F32 = mybir.dt.float32
BF16 = mybir.dt.bfloat16
AF = mybir.ActivationFunctionType
ALU = mybir.AluOpType
AX = mybir.AxisListType

ENGS = ['pe', 'act', 'dve', 'pool', 'sp']
NDMA_SEM = 8
import os as _osx
POOL_COMPUTE = _osx.environ.get('POOLC', 'dve')
EPS = 1e-6


class Cfg:
    def __init__(s, D=4096, SEQ=8192, NCORES=8, NM=16, GH=8, GR=16, MH=16, QR=1024, KVR=512, DFF=16384):
        s.D, s.SEQ, s.NCORES, s.NM, s.GH, s.GR, s.MH, s.QR, s.KVR, s.DFF = D, SEQ, NCORES, NM, GH, GR, MH, QR, KVR, DFF
        s.DK, s.DV, s.NOPE, s.ROPE, s.VD = 128, 256, 128, 64, 128
        s.T = SEQ // NCORES
        s.QKW = GH * s.DK
        s.VW = GH * s.DV
        s.MVW = MH * s.VD
        names = ['gq', 'gk', 'gv', 'gog', 'glrf', 'glrb', 'cq', 'ckv', 'krope', 'ga', 'gb']
        sizes = [s.QKW, s.QKW, s.VW, s.VW, GR, GR, QR, KVR, s.ROPE, D, D]
        s.off = {}
        o = 0
        for n, z in zip(names, sizes):
            s.off[n] = o
            o += z
        s.INW = o


class _Op:
    __slots__ = ('eng', 'fn', 'dma', 'deps', 'flag', 'seq', 'dslot', 'dval', 'idx', 'bar', 'cc')


class Prog:
    def __init__(self, nc, st):
        self.nc = nc
        self.st = st
        self.all = []
        self.per = {e: [] for e in ENGS}
        self.last_w = {}
        self.readers = {}
        self.dma_n = {e: 0 for e in ENGS}
        self.sync_same = {'act': True, 'dve': True, 'pool': True, 'pe': False, 'sp': False}
        self.barriers = []
        self.dma_since_bar = []

    def sbuf(self, name, shape, dt):
        return self.st.enter_context(self.nc.sbuf_tensor(name, list(shape), dt))

    def psum(self, name, shape, dt):
        return self.st.enter_context(self.nc.psum_tensor(name, list(shape), dt))

    def op(self, eng, fn, reads=(), writes=(), dma=False, cc=False, nobar=False):
        if eng == 'pool' and not dma and not cc:
            eng = POOL_COMPUTE
        o = _Op()
        o.eng = eng; o.fn = fn; o.dma = dma or cc; o.flag = dma or cc; o.seq = None
        o.cc = cc
        o.idx = len(self.all)
        o.bar = 0 if nobar else len(self.barriers)
        psr_ = [k for k in reads if isinstance(k, tuple) and k[0] == 'ps']
        if psr_:
            reads = [k for k in reads if k not in psr_]
            writes = list(writes) + [k for k in psr_ if k not in writes]
        deps = set()
        for k in reads:
            w = self.last_w.get(k)
            if w is not None:
                deps.add(w)
        for k in writes:
            w = self.last_w.get(k)
            if w is not None:
                deps.add(w)
            for r in self.readers.get(k, ()):
                deps.add(r)
        o.deps = deps
        if cc:
            self.ccs = getattr(self, 'ccs', [])
            o.dslot = ('cc', len(self.ccs))
            o.dval = 1
            self.ccs.append(o.idx)
            self.dma_since_bar.append(o.idx)
        elif dma:
            n = self.dma_n[eng]
            o.dslot = n % NDMA_SEM
            o.dval = 16 * (n // NDMA_SEM + 1)
            self.dma_n[eng] = n + 1
            self.dma_since_bar.append(o.idx)
        for k in writes:
            self.last_w[k] = o.idx
            self.readers[k] = []
        for k in reads:
            self.readers.setdefault(k, []).append(o.idx)
        self.all.append(o)
        self.per[eng].append(o)
        if getattr(self, 'max_ops', None) and len(self.all) >= self.max_ops:
            raise _Stop()
        return o

    def barrier(self):
        deps = list(self.dma_since_bar)
        for e in ENGS:
            for o in reversed(self.per[e]):
                if not o.dma:
                    deps.append(o.idx)
                    break
        self.barriers.append(deps)
        self.dma_since_bar = []
        if getattr(self, 'stop_at', None) is not None and len(self.barriers) == self.stop_at:
            raise _Stop()

    def dma(self, eng, out, in_, reads=(), writes=(), nobar=False, **kw):
        return self.op(eng, lambda e: e.dma_start(out=out, in_=in_, **kw), reads, writes, dma=True, nobar=nobar)

    def emit(self):
        nc = self.nc
        ops = self.all
        for o in ops:
            for d in o.deps:
                od = ops[d]
                if od.dma:
                    continue
                if od.eng != o.eng or o.dma or self.sync_same[o.eng]:
                    od.flag = True
        for bl in self.barriers:
            for d in bl:
                if not ops[d].dma:
                    ops[d].flag = True
        for e in ENGS:
            c = 0
            for o in self.per[e]:
                if not o.dma and o.flag:
                    c += 1
                    o.seq = c
        csem = {e: self.st.enter_context(nc.semaphore("c_" + e)) for e in ENGS}
        dsem = {e: [self.st.enter_context(nc.semaphore("d_%s%d" % (e, i))) for i in range(NDMA_SEM)]
                for e in ENGS if self.dma_n[e] > 0}
        ccsem = [self.st.enter_context(nc.semaphore("cc%d" % i)) for i in range(len(getattr(self, 'ccs', [])))]
        final_waits = [(sm, 1) for sm in ccsem]
        for e in dsem:
            n = self.dma_n[e]
            for s in range(min(n, NDMA_SEM)):
                cnt = (n - 1 - s) // NDMA_SEM + 1
                final_waits.append((dsem[e][s], 16 * cnt))
        self.n_waits = 0

        def make(e):
            def f(engobj):
                waited = {}
                nbar = [0]

                def wait(sem, val, key):
                    if waited.get(key, 0) >= val:
                        return
                    waited[key] = val
                    engobj.wait_ge(sem, val)
                    self.n_waits += 1

                def wait_op(od, o):
                    if od.cc:
                        wait(ccsem[od.dslot[1]], 1, ('cc', od.dslot[1]))
                    elif od.dma:
                        wait(dsem[od.eng][od.dslot], od.dval, ('d', od.eng, od.dslot))
                    else:
                        if od.eng == e and o is not None and not (o.dma or self.sync_same[e]):
                            return
                        wait(csem[od.eng], od.seq, ('c', od.eng))

                for o in self.per[e]:
                    while nbar[0] < o.bar:
                        mx = {}
                        for d in self.barriers[nbar[0]]:
                            od = ops[d]
                            if od.cc:
                                mx[('cc', od.dslot[1])] = (ccsem[od.dslot[1]], 1)
                            elif od.dma:
                                key = ('d', od.eng, od.dslot)
                                if mx.get(key, (None, 0))[1] < od.dval:
                                    mx[key] = (dsem[od.eng][od.dslot], od.dval)
                            elif od.eng != e or e != 'pe':
                                key = ('c', od.eng)
                                if mx.get(key, (None, 0))[1] < od.seq:
                                    mx[key] = (csem[od.eng], od.seq)
                        for key, (sem, val) in mx.items():
                            wait(sem, val, key)
                        nbar[0] += 1
                    for d in sorted(o.deps):
                        wait_op(ops[d], o)
                    if o.dma and not o.cc and o.dval > 16:
                        wait(dsem[e][o.dslot], o.dval - 16, ('d', e, o.dslot))
                    ins = o.fn(engobj)
                    if o.cc:
                        ins.then_inc(ccsem[o.dslot[1]])
                    elif o.dma:
                        ins.then_inc(dsem[e][o.dslot], 16)
                    elif o.flag:
                        ins.then_inc(csem[e], 1)
                if e == 'sp':
                    for (sem, val) in final_waits:
                        engobj.wait_ge(sem, val)
            return f

        with nc.Block() as block:
            block.sync(make('sp'))
            block.tensor(make('pe'))
            block.scalar(make('act'))
            block.vector(make('dve'))
            block.gpsimd(make('pool'))


class Arena:
    def __init__(self, P, nbytes):
        self.t = P.sbuf("arena", [128, nbytes // 2], BF16)
        self.n = nbytes
        self.top = 0
        self.peak = 0

    def alloc(self, shape, dt):
        ne = int(np.prod(shape))
        nb = ne * (4 if dt == F32 else 2)
        off = self.top
        self.top += (nb + 63) // 64 * 64
        assert self.top <= self.n, ("SBUF arena overflow", self.top, self.n)
        self.peak = max(self.peak, self.top)
        v = self.t[:, off // 2: off // 2 + nb // 2]
        if dt == F32:
            v = v.bitcast(F32)
        if len(shape) == 2:
            v = v.rearrange("p (a b) -> p a b", b=shape[1])
        elif len(shape) == 3:
            v = v.rearrange("p (a b c) -> p a b c", b=shape[1], c=shape[2])
        return v

    def mark(self):
        return self.top

    def reset(self, m):
        self.top = m


class _Stop(Exception):
    pass


def build_program(c, dbg=False, stop=None):
    nc = bass.Bass("TRN2", target_bir_lowering=False)
    D, T, NM, NC_ = c.D, c.T, c.NM, c.NCORES
    TT = T + NM
    KC = D // 128
    NT = T // 128
    GH, MH, QR, KVR, DFF = c.GH, c.MH, c.QR, c.KVR, c.DFF
    QKW, VW, MVW = c.QKW, c.VW, c.MVW
    QC, KVC, FC = QR // 128, KVR // 128, DFF // 128
    TS = [(h * 512, min(512, T - h * 512)) for h in range((T + 511) // 512)]

    def din(name, shape, dt=F32):
        return nc.dram_tensor(name, list(shape), dt, kind="ExternalInput").ap()

    def dscr(name, shape, dt):
        return nc.dram_tensor(name, list(shape), dt).ap()

    x = din("x", [T, D]); meta = din("meta", [NM, D])
    x_full = din("x_full", [NC_ * T, D])
    cosA = din("cosA", [64, NC_ * T + NM]); sinA = din("sinA", [64, NC_ * T + NM])
    w_in = din("w_in", [D, c.INW])
    wfa = din("wfa", [c.GR + 1, QKW]); wba = din("wba", [c.GR + 1, QKW])
    w_uq = din("w_uq", [QR, MH * 192]); w_ukv = din("w_ukv", [KVR, MH * 256])
    w_gla_out = din("w_gla_out", [VW, D]); w_mla_out = din("w_mla_out", [MVW, D])
    w_o = din("w_o", [D, D]); w_ff1 = din("w_ff1", [D, DFF]); w_ff2 = din("w_ff2", [DFF, D])
    NF = 128 * 5 + 2 * TT + 4 * NC_ + 2 * KC + QC + KVC
    cf_in = din("cf32", [128, NF])
    cb_in = din("cbf", [128, 320], BF16)
    glan_in = din("glan", [128, VW])
    fin_in = din("finn", [128, D])
    out = nc.dram_tensor("out", [T, D], F32, kind="ExternalOutput").ap()

    R1 = MH * 128 + 64 + MH * 128
    recv1 = nc.dram_tensor("recv1", [NC_ * R1, T], BF16)
    metaK = dscr("metaK", [MH * 128 + 64, NM], BF16)
    metaV = dscr("metaV", [NM, MVW], BF16)
    C2 = 2 * GH * 257
    GdA = [dscr("GdA%d" % i, [c.GR + 1, NC_ * T + NM], BF16) for i in range(2)]
    kall = dscr("kall", [NC_ * T + NM, QKW], BF16); vall = dscr("vall", [NC_ * T + NM, VW], BF16)
    QnT = dscr("QnT", [MH * 128, T], BF16); QpT = dscr("QpT", [MH * 64, T], BF16)
    q_s = dscr("q_s", [T, QKW], BF16); k_s = dscr("k_s", [TT, QKW], BF16); v_s = dscr("v_s", [TT, VW], BF16)
    gg_s = dscr("gg_s", [T, VW], BF16)
    sga = dscr("sga", [D, T], BF16); sgb = dscr("sgb", [D, T], BF16)
    o_f = dscr("o_f", [T, VW], F32); o_b = dscr("o_b", [T, VW], F32)
    qbf = dscr("qbf", [QKW, T], BF16); qbb = dscr("qbb", [QKW, T], BF16)
    OT = dscr("OT", [MVW, T], BF16)
    h1 = dscr("h1", [T, D], F32)
    uT = dscr("uT", [DFF, T], BF16)

    with ExitStack() as st:
        P = Prog(nc, st)
        A = Arena(P, 211968)
        P.stop_at = stop if isinstance(stop, int) else None
        import os as _os2
        P.max_ops = int(_os2.environ.get('MAXOPS', '0'))

        def chk(name):
            if stop == name:
                raise _Stop()
        try:
            PS = P.psum("ps", [128, 4096], F32)

            def bank(b, n=512):
                return PS[:, 512 * b: 512 * b + n]

            def bankbf(b):
                return PS[:, 512 * b: 512 * b + 512].bitcast(BF16)

            psr = [0]
            ukc = [0]

            def uk():
                ukc[0] += 1
                return ('uk', ukc[0])

            def nextbank(n=1):
                b = psr[0]
                psr[0] = (psr[0] + n) % 8
                return b

            def mm(out_, lhsT, rhs, start, stop, reads, writes):
                P.op('pe', lambda e: e.matmul(out_, lhsT=lhsT, rhs=rhs, start=start, stop=stop), reads, writes)

            def tr(out_, in_, ident_, reads, writes):
                P.op('pe', lambda e: e.transpose(out_, in_, ident_), reads, writes)

            def act(out_, in_, func, reads, writes, scale=1.0, bias=0.0, accum_out=None, eng='act'):
                if accum_out is None:
                    P.op('act', lambda e: e.activation(out=out_, in_=in_, func=func, bias=bias, scale=scale), reads, writes)
                else:
                    P.op('act', lambda e: e.activation(out=out_, in_=in_, func=func, bias=bias, scale=scale,
                                                       accum_out=accum_out), reads, writes)

            def amul(out_, in_, mulv, reads, writes):
                P.op('act', lambda e: e.mul(out=out_, in_=in_, mul=mulv), reads, writes)

            def tt(eng, out_, in0, in1, op, reads, writes):
                P.op(eng, lambda e: e.tensor_tensor(out=out_, in0=in0, in1=in1, op=op), reads, writes)

            def ts(eng, out_, in0, s1, s2, op0, op1, reads, writes):
                if op1 is None:
                    P.op(eng, lambda e: e.tensor_scalar(out=out_, in0=in0, scalar1=s1, scalar2=None, op0=op0), reads, writes)
                else:
                    P.op(eng, lambda e: e.tensor_scalar(out=out_, in0=in0, scalar1=s1, scalar2=s2, op0=op0, op1=op1), reads, writes)

            def stt(eng, out_, in0, scalar, in1, op0, op1, reads, writes):
                P.op(eng, lambda e: e.scalar_tensor_tensor(out=out_, in0=in0, scalar=scalar, in1=in1, op0=op0, op1=op1),
                     reads, writes)

            def cp(eng, out_, in_, reads, writes):
                if eng == 'act':
                    P.op('act', lambda e: e.copy(out=out_, in_=in_), reads, writes)
                else:
                    P.op(eng, lambda e: e.tensor_copy(out=out_, in_=in_), reads, writes)

            def memset(eng, ap, val, writes):
                P.op(eng, lambda e: e.memset(ap, val), (), writes)

            def rstd_from_ss(eng, dst, ss, n, reads, writes):
                ts(eng, dst, ss, 1.0 / n, EPS, ALU.mult, ALU.add, reads, writes)
                P.op('act', lambda e: e.sqrt(out=dst, in_=dst), writes, writes)
                P.op('dve', lambda e: e.reciprocal(out=dst, in_=dst), writes, writes)

            cf = A.alloc([NF], F32)
            cb = A.alloc([320], BF16)
            P.dma('sp', cf, cf_in, writes=['cf'])
            P.dma('sp', cb, cb_in, writes=['cb'])
            o_ = [0]

            def take(n):
                v = cf[:, o_[0]: o_[0] + n]
                o_[0] += n
                return v
            ones_f = take(128); M_le = take(128); M_gt = take(128); M_ge = take(128); M_lt = take(128)
            cos2 = take(TT); sin2 = take(TT)
            oh = take(NC_); _u1 = take(NC_); _u2 = take(NC_); _u3 = take(NC_)
            ln1T = take(KC); ln2T = take(KC); qnT = take(QC); kvnT = take(KVC)
            ident = cb[:, 0:128]; ones_b = cb[:, 128:256]; rot = cb[:, 256:320]
            CK = ['cf', 'cb']
            GR = c.GR
            Gd = [A.alloc([TT], BF16) for _ in range(2)]
            wga = [A.alloc([QKW], BF16) for _ in range(2)]
            for di in range(2):
                P.dma('pool', wga[di][:GR + 1, :], (wfa, wba)[di], writes=[('wga', di)])
            Sst = [A.alloc([GH, 256], BF16) for _ in range(2)]
            pers = A.mark()

            class WStream:
                def __init__(self, kcmax, exempt=False):
                    self.buf = [A.alloc([kcmax, 512], BF16) for _ in range(2)]
                    self.n = 0
                    self.exempt = exempt

                def load(self, Wap, c0, nb, kc):
                    s = self.n % 2
                    self.n += 1
                    Wv = Wap.rearrange("(c p) n -> p c n", p=128)
                    g = 8
                    for q in range(0, kc, g):
                        qe = min(kc, q + g)
                        P.dma('pool', self.buf[s][:, q:qe, 0:nb], Wv[:, q:qe, c0:c0 + nb], writes=[('W', s, q // g)], nobar=self.exempt)
                    return s

                def rkeys(self, s, k):
                    return [('W', s, k // 8)]

            def linear_a(ws, Wap, kc, blocks, actT, akeys, segs, epi):
                nxt = ws.load(Wap, blocks[0][0], blocks[0][1], kc)
                for bi, (c0, nb) in enumerate(blocks):
                    s = nxt
                    if bi + 1 < len(blocks):
                        nxt = ws.load(Wap, blocks[bi + 1][0], blocks[bi + 1][1], kc)
                    for j0 in range(0, nb, 128):
                        m = min(128, nb - j0)
                        b0 = nextbank(len(segs))
                        for si, (t0, n) in enumerate(segs):
                            b = (b0 + si) % 8
                            for k in range(kc):
                                mm(bank(b)[:m, :n], ws.buf[s][:, k, j0:j0 + m], actT[:, k, t0:t0 + n], k == 0, k == kc - 1,
                                   ws.rkeys(s, k) + akeys, [('ps', b)])
                            epi(c0 + j0, m, si, b, bank(b)[:m, :n])

            def linear_b(ws, Wap, kc, blocks, actT, akeys, tiles, epi):
                nxt = ws.load(Wap, blocks[0][0], blocks[0][1], kc)
                for bi, (c0, nb) in enumerate(blocks):
                    s = nxt
                    if bi + 1 < len(blocks):
                        nxt = ws.load(Wap, blocks[bi + 1][0], blocks[bi + 1][1], kc)
                    for ti, (t0, np_) in enumerate(tiles):
                        b = nextbank()
                        for k in range(kc):
                            mm(bank(b)[:np_, :nb], actT[:, k, t0:t0 + np_], ws.buf[s][:, k, 0:nb], k == 0, k == kc - 1,
                               ws.rkeys(s, k) + akeys, [('ps', b)])
                        epi(c0, nb, ti, b, bank(b)[:np_, :nb])

            def blocks_of(c0, n, step=512):
                return [(c0 + i, min(step, n - i)) for i in range(0, n, step)]

            def stage_norm(srcs, gT, dstT):
                m0 = A.mark()
                xin = [A.alloc([D], F32) for _ in range(2)]
                xs = [A.alloc([D], BF16) for _ in range(2)]
                stt_ = A.alloc([8], F32)
                for i, (src, np_, tok0) in enumerate(srcs):
                    b = i % 2
                    ss = stt_[:, 4 * b: 4 * b + 1]
                    rs = stt_[:, 4 * b + 1: 4 * b + 2]
                    P.dma('sp', xin[b][:np_], src, writes=[('xin', b)])
                    memset('dve', ss, 0.0, [('nst', b)])
                    act(xs[b][:np_], xin[b][:np_], AF.Square, [('xin', b), ('nst', b)], [('xs', b), ('nst', b)], accum_out=ss[:np_])
                    rstd_from_ss('dve', rs[:np_], ss[:np_], D, [('nst', b)], [('nrs', b)])
                    ts('dve', xs[b][:np_], xin[b][:np_], rs[:np_], None, ALU.mult, None, [('xin', b), ('nrs', b)], [('xs', b)])
                    for q0 in range(0, KC, 8):
                        pb = nextbank()
                        pv = bankbf(pb)
                        for kk in range(min(8, KC - q0)):
                            k = q0 + kk
                            tr(pv[:, 128 * kk: 128 * kk + np_], xs[b][:np_, 128 * k: 128 * k + 128], ident[:np_, :np_],
                               [('xs', b)] + CK, [('ps', pb)])
                        for kk in range(min(8, KC - q0)):
                            k = q0 + kk
                            src_ = pv[:, 128 * kk: 128 * kk + np_]
                            dst_ = dstT[:, k, tok0: tok0 + np_]
                            if kk % 2 == 0:
                                amul(dst_, src_, gT[:, k: k + 1], [('ps', pb)] + CK, [('nT', tok0)])
                            else:
                                ts('dve', dst_, src_, gT[:, k: k + 1], None, ALU.mult, None, [('ps', pb)] + CK, [('nT', tok0)])
                A.reset(m0)

            def make_gla(G):
                class _NS:
                    pass
                g = _NS()
                qk_in = [A.alloc([2, QKW], BF16) for _ in range(2)]
                v_in = [A.alloc([VW], BF16) for _ in range(2)]
                xg_e = A.alloc([QKW], F32)
                sp_t = A.alloc([QKW], F32)
                Rsum = A.alloc([QKW], F32)
                Ep = A.alloc([QKW], F32); Em = A.alloc([QKW], F32); Gk = A.alloc([QKW], F32); Pb = A.alloc([QKW], F32)
                qt_ = A.alloc([QKW], BF16); kt_ = A.alloc([QKW], BF16); kh_ = A.alloc([QKW], BF16); qb_ = A.alloc([QKW], BF16)
                qtT = A.alloc([GH, 128], BF16); ktT = A.alloc([GH, 128], BF16); qbT = A.alloc([GH, 128], BF16)
                attm = [A.alloc([128], BF16) for _ in range(2)]
                Sloc = A.alloc([GH, 256], F32)
                Sbf = A.alloc([GH, 256], BF16)
                a_h = A.alloc([GH], F32)
                o_st = [A.alloc([VW], F32) for _ in range(2)]
                ocnt = [0]

                def gla_gates(di, t0, np_, first, state_only=False):
                    Mi, Mc = ((M_le, M_gt), (M_ge, M_lt))[di]
                    gap_, gkeys_ = G(di, t0, np_)
                    for h0 in range(0, QKW, 512):
                        nb = min(512, QKW - h0)
                        b = nextbank()
                        mm(bank(b)[:np_, :nb], gap_, wga[di][:GR + 1, h0:h0 + nb], True, True,
                           gkeys_ + [('wga', di)], [('ps', b)])
                        act(xg_e[:np_, h0:h0 + nb], bank(b)[:np_, :nb], AF.Exp, [('ps', b)], ['xg_e'], scale=-1.0)
                    act(sp_t[:np_, :], xg_e[:np_, :], AF.Ln, ['xg_e'], ['sp_t'], bias=1.0)
                    for h0 in range(0, QKW, 512):
                        nb = min(512, QKW - h0)
                        b = nextbank()
                        if not state_only:
                            mm(bank(b)[:np_, :nb], Mi[:np_, :np_], sp_t[:np_, h0:h0 + nb], True, True, ['sp_t'] + CK, [('ps', b)])
                            act(Ep[:np_, h0:h0 + nb], bank(b)[:np_, :nb], AF.Exp, [('ps', b)], ['Ep'], scale=-1.0 / 16)
                            act(Em[:np_, h0:h0 + nb], bank(b)[:np_, :nb], AF.Exp, [('ps', b)], ['Em'], scale=1.0 / 16)
                            b = nextbank()
                        mm(bank(b)[:np_, :nb], Mc[:np_, :np_], sp_t[:np_, h0:h0 + nb], True, True, ['sp_t'] + CK, [('ps', b)])
                        act(Gk[:np_, h0:h0 + nb], bank(b)[:np_, :nb], AF.Exp, [('ps', b)], ['Gk'], scale=-1.0 / 16)
                        if not first:
                            b = nextbank()
                            mm(bank(b)[:np_, :nb], ones_f[:, :np_], Rsum[:, h0:h0 + nb], True, True, ['Rsum'] + CK, [('ps', b)])
                            act(Pb[:np_, h0:h0 + nb], bank(b)[:np_, :nb], AF.Exp, [('ps', b)], ['Pb'], scale=-1.0 / 16)

                def gla_state_update(di, np_, kin, vin, kkeys, vkeys, bf=True):
                    tt('dve', kh_[:np_, :], kin, Gk[:np_, :], ALU.mult, kkeys + ['Gk'], ['kh_'])
                    ba = nextbank()
                    for h in range(GH):
                        mm(bank(ba)[:, h:h + 1], sp_t[:np_, 128 * h:128 * h + 128], ones_f[:np_, 0:1], True, True, ['sp_t'] + CK, [('ps', ba)])
                    act(a_h, bank(ba)[:, :GH], AF.Exp, [('ps', ba)], ['a_h'], scale=-1.0 / 16)
                    for h in range(GH):
                        b = nextbank()
                        mm(bank(b)[:, :256], kh_[:np_, 128 * h:128 * h + 128], vin[:np_, 256 * h:256 * h + 256], True, True,
                           ['kh_'] + vkeys, [('ps', b)])
                        stt('dve', Sloc[:, h, :], Sloc[:, h, :], a_h[:, h:h + 1], bank(b)[:, :256], ALU.mult, ALU.add,
                            [('ps', b), 'a_h', ('Sloc', h)], [('Sloc', h)])
                        if bf:
                            cp('act', Sbf[:, h, :], Sloc[:, h, :], [('Sloc', h)], [('Sbf', h)])

                for _n, _v in (('qk_in', qk_in), ('v_in', v_in), ('xg_e', xg_e), ('sp_t', sp_t), ('Rsum', Rsum), ('Ep', Ep), ('Em', Em), ('Gk', Gk), ('Pb', Pb), ('qt_', qt_), ('kt_', kt_), ('kh_', kh_), ('qb_', qb_), ('qtT', qtT), ('ktT', ktT), ('qbT', qbT), ('attm', attm), ('Sloc', Sloc), ('Sbf', Sbf), ('a_h', a_h), ('o_st', o_st), ('ocnt', ocnt)):
                    setattr(g, _n, _v)
                g.gates = gla_gates
                g.state_update = gla_state_update
                return g

            own_tiles = [(x[128 * i: 128 * i + 128, :], 128, 128 * i) for i in range(NT)]
            NTOK = NC_ * T
            r1 = recv1.ap()
            mB = A.mark()
            ws = WStream(KC, exempt=True)

            mG = A.mark()
            gstg = [A.alloc([512], BF16) for _ in range(2)]
            for i_ in range(2):
                memset('dve', gstg[i_][:GR + 1, :], 1.0, [('gstg', i_)])
            gsc = [0]
            xg = A.alloc([KC, TT], BF16)
            mG2 = A.mark()

            def stage_kv(r, with_meta):
                segsM = TS + ([(T, NM)] if with_meta else [])
                tilesM = [(128 * i, 128) for i in range(NT)] + ([(T, NM)] if with_meta else [])
                nsg = len(segsM)
                XK_ = [('nT', 128 * i) for i in range(NT)] + [('nT', T)]
                ckvg = A.alloc([KVC, TT], BF16)
                sqkv = A.alloc([KVC, TT], BF16)
                krp = A.alloc([TT], F32)
                krb = A.alloc([TT], BF16)
                kro = A.alloc([TT], BF16)
                csr = A.alloc([TT], F32)
                snr = A.alloc([TT], F32)
                tmp = A.alloc([512], F32)
                P.dma('sp', csr[:64, 0:T], cosA[:, NM + r * T: NM + (r + 1) * T], writes=['csr'])
                P.dma('sp', snr[:64, 0:T], sinA[:, NM + r * T: NM + (r + 1) * T], writes=['snr'])
                if with_meta:
                    P.dma('sp', csr[:64, T:TT], cosA[:, 0:NM], writes=['csr2'])
                    P.dma('sp', snr[:64, T:TT], sinA[:, 0:NM], writes=['snr2'])
                CS = ['csr', 'snr', 'csr2', 'snr2']

                def epi_ckv(cc, m, si, b, ps):
                    t0, n = segsM[si]
                    if cc < KVR:
                        k = cc // 128
                        amul(ckvg[:, k, t0:t0 + n], ps, kvnT[:, k:k + 1], [('ps', b)] + CK, [('ckvg', si)])
                        act(sqkv[:, k, t0:t0 + n], ps, AF.Square, [('ps', b)], [('sqkv', si)])
                    else:
                        cp('act', krp[:64, t0:t0 + n], ps, [('ps', b)], [('krp', si)])
                        cp('dve', krb[:64, t0:t0 + n], ps, [('ps', b)], [('krb', si)])
                linear_a(ws, w_in[:, c.off['ckv']: c.off['ckv'] + KVR + 64], KC,
                         ([(0, 512), (512, 64)] if KVR == 512 else blocks_of(0, KVR + 64)), xg, XK_, segsM, epi_ckv)
                rkv_bc = A.alloc([TT], F32)
                rkv_tm = A.alloc([NT + 1], F32)
                for si, (t0, n) in enumerate(segsM):
                    b = nextbank()
                    for k in range(KVC):
                        mm(bank(b)[:, :n], ones_b, sqkv[:, k, t0:t0 + n], k == 0, k == KVC - 1, [('sqkv', si)] + CK, [('ps', b)])
                    rstd_from_ss('dve', rkv_bc[:, t0:t0 + n], bank(b)[:, :n], KVR, [('ps', b)], [('rkv_bc', si)])
                bq = nextbank()
                for ti, (t0, np_) in enumerate(tilesM):
                    si = min(t0 // 512, len(TS) - 1) if t0 < T else len(TS)
                    for k in range(KVC):
                        mm(bank(bq)[:np_, ti:ti + 1], sqkv[:, k, t0:t0 + np_], ones_b[:, 0:1], k == 0, k == KVC - 1,
                           [('sqkv', si)] + CK, [('ps', bq)])
                rstd_from_ss('dve', rkv_tm[:, :NT], bank(bq)[:, :NT], KVR, [('ps', bq)], ['rkv_tm'])
                if with_meta:
                    rstd_from_ss('dve', rkv_tm[:NM, NT:NT + 1], bank(bq)[:NM, NT:NT + 1], KVR, [('ps', bq)], ['rkv_tm2'])
                for si, (t0, n) in enumerate(segsM):
                    b = nextbank()
                    mm(bank(b)[:64, :n], rot[:64, :], krb[:64, t0:t0 + n], True, True, [('krb', si)] + CK, [('ps', b)])
                    tt('dve', tmp[:64, :n], bank(b)[:64, :n], snr[:64, t0:t0 + n], ALU.mult, [('ps', b)] + CS, ['rtmp'])
                    tt('pool', krp[:64, t0:t0 + n], krp[:64, t0:t0 + n], csr[:64, t0:t0 + n], ALU.mult, [('krp', si)] + CS, [('krp', si)])
                    tt('dve', kro[:64, t0:t0 + n], tmp[:64, :n], krp[:64, t0:t0 + n], ALU.add, ['rtmp', ('krp', si)], [('kro', si), 'rtmp'])
                for si, (t0, n) in enumerate(TS):
                    P.dma('sp', r1[r * R1 + MH * 128: r * R1 + MH * 128 + 64, t0:t0 + n], kro[:64, t0:t0 + n], reads=[('kro', si)], writes=[uk()])
                if with_meta:
                    P.dma('sp', metaK[MH * 128: MH * 128 + 64, :], kro[:64, T:TT], reads=[('kro', len(TS))], writes=[uk()])
                kst = [A.alloc([TT], BF16) for _ in range(2)]
                kcnt = [0]

                def epi_kn(cc, m, si, b, ps):
                    t0, n = segsM[si]
                    h = cc // 256
                    sb = kcnt[0] % 2
                    tt('dve', kst[sb][:, t0:t0 + n], ps, rkv_bc[:, t0:t0 + n], ALU.mult, [('ps', b), ('rkv_bc', si)], [('kst', sb, si)])
                    if si < len(TS):
                        P.dma('sp', r1[r * R1 + h * 128: r * R1 + h * 128 + 128, t0:t0 + n], kst[sb][:, t0:t0 + n], reads=[('kst', sb, si)], writes=[uk()])
                    else:
                        P.dma('sp', metaK[h * 128: h * 128 + 128, :], kst[sb][:, T:TT], reads=[('kst', sb, si)], writes=[uk()])
                    if si == nsg - 1:
                        kcnt[0] += 1
                linear_a(ws, w_ukv, KVC, [(256 * h, 128) for h in range(MH)], ckvg, [('ckvg', i) for i in range(nsg)], segsM, epi_kn)
                vst = [A.alloc([512], BF16) for _ in range(2)]
                vcnt = [0]
                w_ukv_v = w_ukv.rearrange("k (h two d) -> k h two d", two=2, d=128)
                sendV = r1[r * R1 + MH * 128 + 64: (r + 1) * R1, :].rearrange("(h p) (kt d) -> p h kt d", p=128, d=128)
                for h0 in range(0, MH, 4):
                    nh = min(4, MH - h0)
                    s = ws.n % 2
                    ws.n += 1
                    for hh in range(nh):
                        P.dma('pool', ws.buf[s][:, 0:KVC, 128 * hh: 128 * hh + 128],
                              w_ukv_v[:, h0 + hh, 1, :].rearrange("(c p) d -> p c d", p=128), writes=[('W', s, 0)], nobar=True)
                    for ti, (t0, np_) in enumerate(tilesM):
                        b = nextbank()
                        for k in range(KVC):
                            mm(bank(b)[:np_, :128 * nh], ckvg[:, k, t0:t0 + np_], ws.buf[s][:, k, 0:128 * nh], k == 0, k == KVC - 1,
                               [('W', s, 0)] + [('ckvg', i) for i in range(nsg)], [('ps', b)])
                        sb = vcnt[0] % 2
                        vcnt[0] += 1
                        ts('dve', vst[sb][:np_, :128 * nh], bank(b)[:np_, :128 * nh], rkv_tm[:np_, ti:ti + 1], None, ALU.mult, None,
                           [('ps', b), 'rkv_tm', 'rkv_tm2'], [('vst', sb)])
                        if t0 < T:
                            P.dma('sp', sendV[:, h0:h0 + nh, ti, :], vst[sb][:, :128 * nh].rearrange("p (h d) -> p h d", d=128),
                                  reads=[('vst', sb)], writes=[uk()])
                        else:
                            P.dma('sp', metaV[:, 128 * h0: 128 * (h0 + nh)], vst[sb][:NM, :128 * nh], reads=[('vst', sb)], writes=[uk()])

            def stage_kvg(r, with_meta):
                tiles_ = [(128 * i, 128) for i in range(NT)] + ([(T, NM)] if with_meta else [])
                segs_ = TS + ([(T, NM)] if with_meta else [])
                XK_ = [('nT', 128 * i) for i in range(NT)] + [('nT', T)]
                pst_ = [A.alloc([512], BF16) for _ in range(2)]
                pc_ = [0]

                def gcol(t0):
                    return r * T + t0 if t0 < T else NTOK

                def epi_kvg(c0, nb, ti, b, ps):
                    sb = pc_[0] % 2
                    pc_[0] += 1
                    t0, np_ = tiles_[ti]
                    g0 = gcol(t0)
                    cp('act' if pc_[0] % 2 else 'dve', pst_[sb][:np_, :nb], ps, [('ps', b)], [('pstg', sb)])
                    if c0 < QKW:
                        P.dma('sp', kall[g0:g0 + np_, c0:c0 + nb], pst_[sb][:np_, :nb], reads=[('pstg', sb)], writes=[uk()])
                    else:
                        P.dma('sp', vall[g0:g0 + np_, c0 - QKW:c0 - QKW + nb], pst_[sb][:np_, :nb], reads=[('pstg', sb)], writes=[uk()])
                linear_b(ws, w_in[:, QKW: 2 * QKW + VW], KC, blocks_of(0, QKW) + blocks_of(QKW, VW), xg, XK_, tiles_, epi_kvg)
                for di, nm_ in enumerate(('glrf', 'glrb')):
                    def epi_gg(cc, m, si, b, ps, di=di):
                        t0, n = segs_[si]
                        g0 = gcol(t0)
                        sb = gsc[0] % 2
                        gsc[0] += 1
                        cp('act', gstg[sb][:GR, :n], ps, [('ps', b)], [('gstg', sb)])
                        P.dma('sp', GdA[di][:, g0:g0 + n], gstg[sb][:GR + 1, :n], reads=[('gstg', sb)], writes=[uk()])
                    linear_a(ws, w_in[:, c.off[nm_]: c.off[nm_] + GR], KC, [(0, GR)], xg, XK_, segs_, epi_gg)

            for r in range(NC_):
                wm = (r == 0)
                stage_norm([(x_full[r * T + 128 * i: r * T + 128 * i + 128, :], 128, 128 * i) for i in range(NT)]
                           + ([(meta[:, :], NM, T)] if wm else []), ln1T, xg)
                P.barrier()
                stage_kv(r, wm)
                stage_kvg(r, wm)
                A.reset(mG2)
                P.barrier()

            A.reset(mG)
            gtl = [A.alloc([128], BF16) for _ in range(2)]
            gtc = [0]

            def G_glob(di, t0, np_):
                ib = gtc[0] % 2
                gtc[0] += 1
                P.dma('sp', gtl[ib][:GR + 1, :np_], GdA[di][:, t0:t0 + np_], writes=[('gtl', ib)])
                return gtl[ib][:GR + 1, :np_], [('gtl', ib)]
            gl_ = make_gla(G_glob)
            Sacc = A.alloc([2, GH, 256], F32)
            memset('dve', Sacc, 0.0, ['Sacc'])
            kv_ld = [0]

            def load_kv(g0, np_):
                ib = kv_ld[0] % 2
                kv_ld[0] += 1
                P.dma('sp', gl_.qk_in[ib][:np_, 1, :], kall[g0:g0 + np_, :], writes=[('qk_in', ib, 1)])
                P.dma('sp', gl_.v_in[ib][:np_, :], vall[g0:g0 + np_, :], writes=[('v_in', ib)])
                return ib
            for di in range(2):
                memset('dve', gl_.Sloc, 0.0, [('Sloc', h) for h in range(GH)])
                if di == 0:
                    ib = load_kv(NTOK, NM)
                    gl_.gates(0, NTOK, NM, True, state_only=True)
                    gl_.state_update(0, NM, gl_.qk_in[ib][:NM, 1, :], gl_.v_in[ib], [('qk_in', ib, 1)], [('v_in', ib)], bf=False)
                for r in (range(NC_) if di == 0 else range(NC_ - 1, -1, -1)):
                    stt('dve', Sacc[:, di].rearrange("p h c -> p (h c)"), gl_.Sloc.rearrange("p h c -> p (h c)"), oh[:, r:r + 1],
                        Sacc[:, di].rearrange("p h c -> p (h c)"), ALU.mult, ALU.add, [('Sloc', h) for h in range(GH)] + ['Sacc'] + CK, ['Sacc'])
                    if (di == 0 and r == NC_ - 1) or (di == 1 and r == 0):
                        break
                    for i in (range(NT) if di == 0 else range(NT - 1, -1, -1)):
                        g0 = r * T + 128 * i
                        ib = load_kv(g0, 128)
                        gl_.gates(di, g0, 128, True, state_only=True)
                        gl_.state_update(di, 128, gl_.qk_in[ib][:, 1, :], gl_.v_in[ib], [('qk_in', ib, 1)], [('v_in', ib)], bf=False)
            for di in range(2):
                cp('dve', Sst[di], Sacc[:, di], ['Sacc'], [('Sst', di)])
            P.barrier()
            A.reset(mG)

            xnT = A.alloc([KC, TT], BF16)
            stage_norm(own_tiles + [(meta[:, :], NM, T)], ln1T, xnT)
            XK = [('nT', 128 * i) for i in range(NT)] + [('nT', T)]
            segsM = TS + [(T, NM)]
            P.barrier()

            m1 = A.mark()
            cqg = A.alloc([QC, T], BF16)
            sqq = A.alloc([QC, T], BF16)

            def epi_cq(cc, m, si, b, ps):
                t0, n = TS[si]
                k = cc // 128
                amul(cqg[:, k, t0:t0 + n], ps, qnT[:, k:k + 1], [('ps', b)] + CK, [('cqg', si)])
                act(sqq[:, k, t0:t0 + n], ps, AF.Square, [('ps', b)], [('sqq', si)])
            linear_a(ws, w_in[:, c.off['cq']: c.off['cq'] + QR], KC, blocks_of(0, QR), xnT, XK, TS, epi_cq)
            rq_bc = A.alloc([T], F32)
            for si, (t0, n) in enumerate(TS):
                b = nextbank()
                for k in range(QC):
                    mm(bank(b)[:, :n], ones_b, sqq[:, k, t0:t0 + n], k == 0, k == QC - 1, [('sqq', si)] + CK, [('ps', b)])
                rstd_from_ss('dve', rq_bc[:, t0:t0 + n], bank(b)[:, :n], QR, [('ps', b)], [('rq_bc', si)])
            qst = [A.alloc([T], BF16) for _ in range(2)]
            qpf = [A.alloc([T], F32)]
            qpb = [A.alloc([T], BF16)]
            qtm = [A.alloc([T], F32)]
            qcnt = [0]
            SC = 192.0 ** -0.5

            def epi_q(cc, m, si, b, ps):
                t0, n = TS[si]
                h = cc // 192
                sb = qcnt[0] % 2
                CQ = [('cqg', i) for i in range(len(TS))]
                if m == 128:
                    stt('dve', qst[sb][:, t0:t0 + n], ps, SC, rq_bc[:, t0:t0 + n], ALU.mult, ALU.mult,
                        [('ps', b), ('rq_bc', si)], [('qst', sb, si)])
                    P.dma('sp', QnT[h * 128: h * 128 + 128, t0:t0 + n], qst[sb][:, t0:t0 + n], reads=[('qst', sb, si)], writes=[uk()])
                else:
                    stt('dve', qpf[0][:64, t0:t0 + n], ps, SC, rq_bc[:64, t0:t0 + n], ALU.mult, ALU.mult,
                        [('ps', b), ('rq_bc', si)], [('qpf', 0, si)])
                    cp('act', qpb[0][:64, t0:t0 + n], qpf[0][:64, t0:t0 + n], [('qpf', 0, si)], [('qpb', 0, si)])
                    b2 = nextbank()
                    mm(bank(b2)[:64, :n], rot[:64, :], qpb[0][:64, t0:t0 + n], True, True, [('qpb', 0, si)] + CK, [('ps', b2)])
                    tt('dve', qtm[0][:64, t0:t0 + n], bank(b2)[:64, :n], sin2[:64, t0:t0 + n], ALU.mult, [('ps', b2)] + CK, [('qtm', 0, si)])
                    tt('pool', qpf[0][:64, t0:t0 + n], qpf[0][:64, t0:t0 + n], cos2[:64, t0:t0 + n], ALU.mult,
                       [('qpf', 0, si)] + CK, [('qpf', 0, si)])
                    tt('dve', qpb[0][:64, t0:t0 + n], qtm[0][:64, t0:t0 + n], qpf[0][:64, t0:t0 + n], ALU.add,
                       [('qtm', 0, si), ('qpf', 0, si)], [('qpb', 0, si)])
                    P.dma('sp', QpT[h * 64: h * 64 + 64, t0:t0 + n], qpb[0][:64, t0:t0 + n], reads=[('qpb', 0, si)], writes=[uk()])
                    if si == len(TS) - 1:
                        qcnt[0] += 1
            qblocks = []
            for h in range(MH):
                qblocks += [(192 * h, 128), (192 * h + 128, 64)]
            linear_a(ws, w_uq, QC, qblocks, cqg, [('cqg', i) for i in range(len(TS))], TS, epi_q)
            A.reset(m1)
            P.barrier()

            m1 = A.mark()
            glan = A.alloc([VW], F32)
            P.dma('sp', glan, glan_in, writes=['glan'])
            tilesO = [(128 * i, 128) for i in range(NT)]
            gst = [A.alloc([512], F32) for _ in range(2)]
            gsb = [A.alloc([512], BF16) for _ in range(2)]
            gcnt = [0]

            def epi_gog(c0, nb, ti, b, ps):
                sb = gcnt[0] % 2
                gcnt[0] += 1
                act(gst[sb][:, :nb], ps, AF.Silu, [('ps', b)], [('gst', sb)])
                tt('dve', gsb[sb][:, :nb], gst[sb][:, :nb], glan[:, c0:c0 + nb], ALU.mult, [('gst', sb), 'glan'], [('gsb', sb)])
                P.dma('sp', gg_s[128 * ti: 128 * ti + 128, c0:c0 + nb], gsb[sb][:, :nb], reads=[('gsb', sb)], writes=[uk()])
            linear_b(ws, w_in[:, c.off['gog']: c.off['gog'] + VW], KC, blocks_of(0, VW), xnT, XK, tilesO, epi_gog)
            A.reset(m1)
            P.barrier()

            m1 = A.mark()
            sst = [A.alloc([T], BF16) for _ in range(2)]
            scnt = [0]
            for nm_, dst in (('ga', sga), ('gb', sgb)):
                def epi_sg(cc, m, si, b, ps, dst=dst):
                    t0, n = TS[si]
                    sb = scnt[0] % 2
                    act(sst[sb][:, t0:t0 + n], ps, AF.Sigmoid, [('ps', b)], [('sst', sb, si)])
                    P.dma('sp', dst[cc:cc + 128, t0:t0 + n], sst[sb][:, t0:t0 + n], reads=[('sst', sb, si)], writes=[uk()])
                    if si == len(TS) - 1:
                        scnt[0] += 1
                linear_a(ws, w_in[:, c.off[nm_]: c.off[nm_] + D], KC, blocks_of(0, D), xnT, XK, TS, epi_sg)
            A.reset(m1)
            P.barrier()

            m1 = A.mark()
            pst = [A.alloc([512], BF16) for _ in range(2)]
            pcnt = [0]
            QSC = 128.0 ** -0.5
            tilesAll = tilesO + [(T, NM)]

            def epi_qkv(c0, nb, ti, b, ps):
                sb = pcnt[0] % 2
                pcnt[0] += 1
                t0, np_ = tilesAll[ti]
                if c0 < QKW:
                    if t0 >= T:
                        return
                    amul(pst[sb][:np_, :nb], ps, QSC, [('ps', b)], [('pst', sb)])
                    P.dma('sp', q_s[t0:t0 + np_, c0:c0 + nb], pst[sb][:np_, :nb], reads=[('pst', sb)], writes=[uk()])
                elif c0 < 2 * QKW:
                    cp('act' if pcnt[0] % 2 else 'dve', pst[sb][:np_, :nb], ps, [('ps', b)], [('pst', sb)])
                    P.dma('sp', k_s[t0:t0 + np_, c0 - QKW:c0 - QKW + nb], pst[sb][:np_, :nb], reads=[('pst', sb)], writes=[uk()])
                else:
                    cp('act' if pcnt[0] % 2 else 'dve', pst[sb][:np_, :nb], ps, [('ps', b)], [('pst', sb)])
                    P.dma('sp', v_s[t0:t0 + np_, c0 - 2 * QKW:c0 - 2 * QKW + nb], pst[sb][:np_, :nb], reads=[('pst', sb)], writes=[uk()])
            linear_b(ws, w_in[:, 0: 2 * QKW + VW], KC, blocks_of(0, QKW) + blocks_of(QKW, QKW) + blocks_of(2 * QKW, VW), xnT, XK, tilesAll, epi_qkv)
            for di, nm_ in enumerate(('glrf', 'glrb')):
                memset('dve', Gd[di][:GR + 1, :], 1.0, [('Gd', di)])

                def epi_g(cc, m, si, b, ps, di=di):
                    t0, n = segsM[si]
                    cp('act', Gd[di][:GR, t0:t0 + n], ps, [('ps', b)], [('Gd', di)])
                linear_a(ws, w_in[:, c.off[nm_]: c.off[nm_] + GR], KC, [(0, GR)], xnT, XK, segsM, epi_g)
            P.barrier()
            A.reset(pers)

            m1 = A.mark()
            NTG = NT
            gl = make_gla(lambda di, t0, np_: (Gd[di][:GR + 1, t0:t0 + np_], [('Gd', di)]))
            qk_in = gl.qk_in; v_in = gl.v_in; xg_e = gl.xg_e; sp_t = gl.sp_t; Rsum = gl.Rsum; Ep = gl.Ep; Em = gl.Em; Gk = gl.Gk; Pb = gl.Pb; qt_ = gl.qt_; kt_ = gl.kt_; kh_ = gl.kh_; qb_ = gl.qb_; qtT = gl.qtT; ktT = gl.ktT; qbT = gl.qbT; attm = gl.attm; Sloc = gl.Sloc; Sbf = gl.Sbf; a_h = gl.a_h; o_st = gl.o_st; ocnt = gl.ocnt
            gla_gates = gl.gates; gla_state_update = gl.state_update
            for di in range(2):
                memset('dve', Sloc, 0.0, [('Sloc', h) for h in range(GH)])
                memset('pool', Sbf, 0.0, [('Sbf', h) for h in range(GH)])
                memset('pool', Rsum, 0.0, ['Rsum'])
                order = list(range(NTG)) if di == 0 else list(range(NTG - 1, -1, -1))
                mask = (M_le, M_ge)[di]
                qbdst = (qbf, qbb)[di]
                odst = (o_f, o_b)[di]
                for oi, i in enumerate(order):
                    t0 = 128 * i
                    ib = oi % 2
                    first = (oi == 0)
                    P.dma('sp', qk_in[ib][:, 0, :], q_s[t0:t0 + 128, :], reads=['q_s'], writes=[('qk_in', ib, 0)])
                    P.dma('sp', qk_in[ib][:, 1, :], k_s[t0:t0 + 128, :], reads=['k_s'], writes=[('qk_in', ib, 1)])
                    P.dma('sp', v_in[ib], v_s[t0:t0 + 128, :], reads=['v_s'], writes=[('v_in', ib)])
                    gla_gates(di, t0, 128, first)
                    qin = qk_in[ib][:, 0, :]
                    kin = qk_in[ib][:, 1, :]
                    tt('dve', qt_, qin, Ep, ALU.mult, [('qk_in', ib, 0), 'Ep'], ['qt_'])
                    tt('pool', kt_, kin, Em, ALU.mult, [('qk_in', ib, 1), 'Em'], ['kt_'])
                    if first:
                        cp('pool', qb_, qt_, ['qt_'], ['qb_'])
                    else:
                        tt('pool', qb_, qt_, Pb, ALU.mult, ['qt_', 'Pb'], ['qb_'])
                    for (src, dstT_, key) in ((qt_, qtT, 'qtT'), (kt_, ktT, 'ktT'), (qb_, qbT, 'qbT')):
                        pb = nextbank()
                        pv = bankbf(pb)
                        for h in range(GH):
                            tr(pv[:, 128 * h:128 * h + 128], src[:, 128 * h:128 * h + 128], ident, [key[:-1] + '_'] + CK, [('ps', pb)])
                        cp('act', dstT_, pv[:, :128 * GH].rearrange("p (h t) -> p h t", t=128), [('ps', pb)], [key])
                    P.dma('sp', qbdst.rearrange("(h p) t -> p h t", p=128)[:, :, t0:t0 + 128], qbT, reads=['qbT'], writes=[uk()])
                    ob0 = nextbank(VW // 512)
                    osb = ocnt[0] % 2
                    ocnt[0] += 1
                    nob = VW // 512
                    for h in range(GH):
                        b = (ob0 + nob + (h % (8 - nob))) % 8
                        mm(bank(b)[:, :128], ktT[:, h, :], qtT[:, h, :], True, True, ['ktT', 'qtT'], [('ps', b)])
                        ab = h % 2
                        tt('dve', attm[ab], bank(b)[:, :128], mask, ALU.mult, [('ps', b)] + CK, [('attm', ab)])
                        ob = (ob0 + (256 * h) // 512) % 8
                        oc = (256 * h) % 512
                        oap = bank(ob)[:, oc:oc + 256]
                        mm(oap, attm[ab], v_in[ib][:, 256 * h:256 * h + 256], True, False, [('attm', ab), ('v_in', ib)], [('ps', ob)])
                        mm(oap, qtT[:, h, :], Sbf[:, h, :], False, True, ['qtT', ('Sbf', h)], [('ps', ob)])
                    for j in range(VW // 512):
                        ob = (ob0 + j) % 8
                        cp('act' if j % 2 else 'dve', o_st[osb][:, 512 * j:512 * j + 512], bank(ob), [('ps', ob)], [('o_st', osb)])
                    P.dma('sp', odst[t0:t0 + 128, :], o_st[osb], reads=[('o_st', osb)], writes=[uk()])
                    gla_state_update(di, 128, kin, v_in[ib], [('qk_in', ib, 1)], [('v_in', ib)])
                    tt('pool', Rsum, Rsum, sp_t, ALU.add, ['Rsum', 'sp_t'], ['Rsum'])
            A.reset(m1)
            P.barrier()

            A.reset(pers)
            mM = A.mark()
            Qn = A.alloc([MH, T], BF16)
            Qp = A.alloc([MH, T], BF16)
            P.dma('sp', Qn, QnT.rearrange("(h p) t -> p h t", p=128), reads=['QnT'], writes=['Qn'])
            P.dma('sp', Qp[:64], QpT.rearrange("(h p) t -> p h t", p=64), reads=['QpT'], writes=['Qp'])
            NKT = NC_ * NT
            NK = NKT * 128 + NM
            Kpe = A.alloc([NK], BF16)
            r1 = recv1.ap()
            for r in range(NC_):
                P.dma('sp', Kpe[:64, r * T:(r + 1) * T], r1[r * R1 + MH * 128: r * R1 + MH * 128 + 64, :], reads=['recv1'], writes=['Kpe'])
            P.dma('sp', Kpe[:64, NKT * 128:NK], metaK[MH * 128:MH * 128 + 64, :], reads=['metaK'], writes=['Kpe'])
            Kh = [A.alloc([NK], BF16) for _ in range(2)]
            Vh = [A.alloc([NKT + 1, 128], BF16) for _ in range(2)]
            Pt = [A.alloc([T], BF16) for _ in range(3)]
            rec = A.alloc([T], F32)
            ost = [A.alloc([T], BF16) for _ in range(2)]
            nseg = len(TS)
            assert nseg <= 2
            for h in range(MH):
                hb = h % 2
                for r in range(NC_):
                    P.dma('sp', Kh[hb][:, r * T:(r + 1) * T], r1[r * R1 + h * 128: r * R1 + h * 128 + 128, :], reads=['recv1'], writes=[('Kh', hb)])
                    vsrc = r1[r * R1 + MH * 128 + 64 + h * 128: r * R1 + MH * 128 + 64 + (h + 1) * 128, :]
                    P.dma('sp', Vh[hb][:, r * NT:(r + 1) * NT, :], vsrc.rearrange("p (kt d) -> p kt d", d=128), reads=['recv1'], writes=[('Vh', hb)])
                P.dma('sp', Kh[hb][:, NKT * 128:NK], metaK[h * 128:h * 128 + 128, :], reads=['metaK'], writes=[('Kh', hb)])
                P.dma('sp', Vh[hb][:NM, NKT, :], metaV[:, h * 128:h * 128 + 128], reads=['metaV'], writes=[('Vh', hb)])
                for kt in range(NKT + 1):
                    nk = 128 if kt < NKT else NM
                    k0 = kt * 128
                    sbk = 4 + 2 * (kt % 2)
                    pi = kt % 3
                    for si, (t0, n) in enumerate(TS):
                        b = sbk + si
                        mm(bank(b)[:nk, :n], Kh[hb][:, k0:k0 + nk], Qn[:, h, t0:t0 + n], True, False, [('Kh', hb), 'Qn'], [('ps', b)])
                        mm(bank(b)[:nk, :n], Kpe[:64, k0:k0 + nk], Qp[:64, h, t0:t0 + n], False, True, ['Kpe', 'Qp'], [('ps', b)])
                    P.op('act', (lambda e, o_=Pt[pi][:nk, :T], i_=PS[:nk, 512 * sbk:512 * sbk + T]:
                                 e.activation(out=o_, in_=i_, func=AF.Exp, bias=0.0, scale=1.0)),
                         [('ps', sbk + si) for si in range(nseg)], [('Pt', pi)])
                    for si, (t0, n) in enumerate(TS):
                        mm(bank(si)[:, :n], Vh[hb][:nk, kt, :], Pt[pi][:nk, t0:t0 + n], kt == 0, kt == NKT, [('Vh', hb), ('Pt', pi)], [('ps', si)])
                        mm(bank(2 + si)[:, :n], ones_b[:nk, :], Pt[pi][:nk, t0:t0 + n], kt == 0, kt == NKT, [('Pt', pi)] + CK, [('ps', 2 + si)])
                for si, (t0, n) in enumerate(TS):
                    P.op('dve', (lambda e, o_=rec[:, t0:t0 + n], i_=bank(2 + si)[:, :n]: e.reciprocal(out=o_, in_=i_)),
                         [('ps', 2 + si)], [('rec', si)])
                    tt('dve', ost[hb][:, t0:t0 + n], bank(si)[:, :n], rec[:, t0:t0 + n], ALU.mult, [('ps', si), ('rec', si)], [('ost', hb, si)])
                    P.dma('sp', OT[h * 128:h * 128 + 128, t0:t0 + n], ost[hb][:, t0:t0 + n], reads=[('ost', hb, si)], writes=[uk()])
            A.reset(mM)
            P.barrier()

            A.reset(pers)
            VC = VW // 128
            mgT = A.alloc([KC, T], BF16)
            mWO = A.mark()
            ogT = A.alloc([VC, T], BF16)
            mF2 = A.mark()
            qbl = [A.alloc([2, GH, 128], BF16) for _ in range(2)]
            ofl = [A.alloc([2, VW], F32) for _ in range(2)]
            ggl = [A.alloc([VW], BF16) for _ in range(2)]
            osum = A.alloc([VW], F32)
            osq = A.alloc([VW], F32)
            ssh = A.alloc([GH], F32)
            rsh = A.alloc([GH], F32)
            ogb = A.alloc([VW], BF16)
            for i in range(NT):
                t0 = 128 * i
                ib = i % 2
                P.dma('sp', qbl[ib][:, 0], qbf.rearrange("(h p) t -> p h t", p=128)[:, :, t0:t0 + 128], reads=['qb_d'], writes=[('qbl', ib)])
                P.dma('sp', qbl[ib][:, 1], qbb.rearrange("(h p) t -> p h t", p=128)[:, :, t0:t0 + 128], reads=['qb_d'], writes=[('qbl', ib)])
                P.dma('sp', ofl[ib][:, 0, :], o_f[t0:t0 + 128, :], reads=['o_d'], writes=[('ofl', ib)])
                P.dma('sp', ofl[ib][:, 1, :], o_b[t0:t0 + 128, :], reads=['o_d'], writes=[('ofl', ib)])
                P.dma('sp', ggl[ib], gg_s[t0:t0 + 128, :], reads=['gg_s'], writes=[('ggl', ib)])
                tt('pool', osum, ofl[ib][:, 0, :], ofl[ib][:, 1, :], ALU.add, [('ofl', ib)], ['osum'])
                ob0 = nextbank(VW // 512)
                for h in range(GH):
                    ob = (ob0 + (256 * h) // 512) % 8
                    oc = (256 * h) % 512
                    oap = bank(ob)[:, oc:oc + 256]
                    mm(oap, qbl[ib][:, 0, h, :], Sst[0][:, h, :], True, False, [('qbl', ib), ('Sst', 0)], [('ps', ob)])
                    mm(oap, qbl[ib][:, 1, h, :], Sst[1][:, h, :], False, True, [('qbl', ib), ('Sst', 1)], [('ps', ob)])
                for j in range(VW // 512):
                    ob = (ob0 + j) % 8
                    tt('dve', osum[:, 512 * j:512 * j + 512], osum[:, 512 * j:512 * j + 512], bank(ob), ALU.add, ['osum', ('ps', ob)], ['osum'])
                tt('pool', osq, osum, osum, ALU.mult, ['osum'], ['osq'])
                P.op('dve', lambda e: e.tensor_reduce(out=ssh, in_=osq.rearrange("p (h d) -> p h d", d=256), axis=AX.X, op=ALU.add),
                     ['osq'], ['ssh'])
                rstd_from_ss('dve', rsh, ssh, 256, ['ssh'], ['rsh'])
                for h in range(GH):
                    stt('dve', ogb[:, 256 * h:256 * h + 256], osum[:, 256 * h:256 * h + 256], rsh[:, h:h + 1],
                        ggl[ib][:, 256 * h:256 * h + 256], ALU.mult, ALU.mult, ['osum', 'rsh', ('ggl', ib)], [('ogb', h)])
                for q0 in range(0, VC, 8):
                    pb = nextbank()
                    pv = bankbf(pb)
                    nq = min(8, VC - q0)
                    for kk in range(nq):
                        k = q0 + kk
                        tr(pv[:, 128 * kk:128 * kk + 128], ogb[:, 128 * k:128 * k + 128], ident, [('ogb', k // 2)] + CK, [('ps', pb)])
                    cp('act', ogT[:, q0:q0 + nq, t0:t0 + 128], pv[:, :128 * nq].rearrange("p (k t) -> p k t", t=128), [('ps', pb)], [('ogT', i)])
            A.reset(mF2)
            P.barrier()

            OTs = A.alloc([MVW // 128, T], BF16)
            P.dma('sp', OTs, OT.rearrange("(k p) t -> p k t", p=128), reads=['OT'], writes=['OTs'])
            ws2 = WStream(max(VC, MVW // 128))
            gl = [A.alloc([2, T], BF16) for _ in range(2)]
            t1 = A.alloc([T], F32)
            mcnt = [0]
            OGK = [('ogT', i) for i in range(NT)]
            for j0 in range(0, D, 512):
                nb = min(512, D - j0)
                sA = ws2.load(w_gla_out, j0, nb, VC)
                sB = ws2.load(w_mla_out, j0, nb, MVW // 128)
                for jj in range(0, nb, 128):
                    cc = j0 + jj
                    gbf = mcnt[0] % 2
                    mcnt[0] += 1
                    P.dma('sp', gl[gbf][:, 0, :], sga[cc:cc + 128, :], reads=['ga'], writes=[('gl', gbf)])
                    P.dma('sp', gl[gbf][:, 1, :], sgb[cc:cc + 128, :], reads=['gb'], writes=[('gl', gbf)])
                    b0 = nextbank(nseg)
                    for si, (t0, n) in enumerate(TS):
                        b = (b0 + si) % 8
                        for k in range(VC):
                            mm(bank(b)[:, :n], ws2.buf[sA][:, k, jj:jj + 128], ogT[:, k, t0:t0 + n], k == 0, k == VC - 1,
                               ws2.rkeys(sA, k) + OGK, [('ps', b)])
                        tt('dve', t1[:, t0:t0 + n], bank(b)[:, :n], gl[gbf][:, 0, t0:t0 + n], ALU.mult, [('ps', b), ('gl', gbf)], [('t1', si)])
                    b0 = nextbank(nseg)
                    for si, (t0, n) in enumerate(TS):
                        b = (b0 + si) % 8
                        for k in range(MVW // 128):
                            mm(bank(b)[:, :n], ws2.buf[sB][:, k, jj:jj + 128], OTs[:, k, t0:t0 + n], k == 0, k == MVW // 128 - 1,
                               ws2.rkeys(sB, k) + ['OTs'], [('ps', b)])
                        tt('dve', gl[gbf][:, 1, t0:t0 + n], bank(b)[:, :n], gl[gbf][:, 1, t0:t0 + n], ALU.mult, [('ps', b), ('gl', gbf)], [('gl', gbf)])
                        tt('pool', mgT[:, cc // 128, t0:t0 + n], t1[:, t0:t0 + n], gl[gbf][:, 1, t0:t0 + n], ALU.add,
                           [('t1', si), ('gl', gbf)], [('mgT', si)])
            P.barrier()
            A.reset(mWO)
            ws3 = WStream(KC)
            xl = [A.alloc([512], F32) for _ in range(3)]
            wcnt = [0]

            def epi_wo(c0, nb, ti, b, ps):
                sb = wcnt[0] % 3
                wcnt[0] += 1
                t0 = 128 * ti
                P.dma('sp', xl[sb][:, :nb], x[t0:t0 + 128, c0:c0 + nb], writes=[('xl', sb)])
                tt('dve', xl[sb][:, :nb], xl[sb][:, :nb], ps, ALU.add, [('xl', sb), ('ps', b)], [('xl', sb)])
                P.dma('sp', h1[t0:t0 + 128, c0:c0 + nb], xl[sb][:, :nb], reads=[('xl', sb)], writes=[uk()])
            linear_b(ws3, w_o, KC, blocks_of(0, D), mgT, [('mgT', si) for si in range(nseg)], tilesO, epi_wo)
            P.barrier()

            A.reset(pers)
            hnT = A.alloc([KC, T], BF16)
            stage_norm([(h1[128 * i:128 * i + 128, :], 128, 128 * i) for i in range(NT)], ln2T, hnT)
            P.barrier()
            HK = [('nT', 128 * i) for i in range(NT)]

            mf1 = A.mark()
            ws4 = WStream(KC)
            ur = [A.alloc([T], F32) for _ in range(2)]
            ub = [A.alloc([T], BF16) for _ in range(2)]
            ucnt = [0]

            def epi_f1(cc, m, si, b, ps):
                t0, n = TS[si]
                sb = ucnt[0] % 2
                act(ur[sb][:, t0:t0 + n], ps, AF.Relu, [('ps', b)], [('ur', sb, si)])
                tt('dve', ub[sb][:, t0:t0 + n], ur[sb][:, t0:t0 + n], ur[sb][:, t0:t0 + n], ALU.mult, [('ur', sb, si)], [('ub', sb, si)])
                P.dma('sp', uT[cc:cc + 128, t0:t0 + n], ub[sb][:, t0:t0 + n], reads=[('ub', sb, si)], writes=[uk()])
                if si == nseg - 1:
                    ucnt[0] += 1
            linear_a(ws4, w_ff1, KC, blocks_of(0, DFF), hnT, HK, TS, epi_f1)
            P.barrier()

            A.reset(pers)
            yacc = A.alloc([NT, D], F32)
            for i in range(NT):
                P.dma('sp', yacc[:, i, :], h1[128 * i:128 * i + 128, :], reads=['h1'], writes=[('yacc', i)])
            FB = 8
            mUL = A.mark()
            ul = [A.alloc([FB, T], BF16) for _ in range(2)]
            w2b = [A.alloc([FB, 512], BF16) for _ in range(2)]
            w2n = [0]
            uTv = uT.rearrange("(f p) t -> p f t", p=128)
            w2v = w_ff2.rearrange("(f p) n -> p f n", p=128)
            for fb in range(0, FC, FB):
                nf = min(FB, FC - fb)
                us = (fb // FB) % 2
                P.dma('sp', ul[us][:, :nf, :], uTv[:, fb:fb + nf, :], reads=['uT'], writes=[('ul', us)])
                for c0 in range(0, D, 512):
                    nb = min(512, D - c0)
                    s = w2n[0] % 2
                    w2n[0] += 1
                    P.dma('pool', w2b[s][:, :nf, :nb], w2v[:, fb:fb + nf, c0:c0 + nb], writes=[('w2b', s)])
                    for i in range(NT):
                        b = nextbank()
                        for f in range(nf):
                            mm(bank(b)[:, :nb], ul[us][:, f, 128 * i:128 * i + 128], w2b[s][:, f, :nb], f == 0, f == nf - 1,
                               [('ul', us), ('w2b', s)], [('ps', b)])
                        tt('dve', yacc[:, i, c0:c0 + nb], yacc[:, i, c0:c0 + nb], bank(b)[:, :nb], ALU.add, [('yacc', i), ('ps', b)], [('yacc', i)])
            P.barrier()
            A.reset(mUL)
            finn = A.alloc([D], F32)
            P.dma('sp', finn, fin_in, writes=['finn'])
            fj = A.alloc([D], BF16)
            fst = A.alloc([2 * NT], F32)
            for i in range(NT):
                ss = fst[:, 2 * i:2 * i + 1]
                rs = fst[:, 2 * i + 1:2 * i + 2]
                memset('pool', ss, 0.0, [('fst', i)])
                act(fj, yacc[:, i, :], AF.Square, [('yacc', i), ('fst', i)], ['fj', ('fst', i)], accum_out=ss)
                rstd_from_ss('dve', rs, ss, D, [('fst', i)], [('frs', i)])
                stt('dve', yacc[:, i, :], yacc[:, i, :], rs, finn, ALU.mult, ALU.mult, [('yacc', i), ('frs', i), 'finn'], [('yacc', i)])
                P.dma('sp', out[128 * i:128 * i + 128, :], yacc[:, i, :], reads=[('yacc', i)], writes=[uk()])
        except _Stop:
            print('[kernel] stopped at barrier', stop)
        P.emit()
        print("[kernel] ops=%d waits=%d arena_peak=%d" % (len(P.all), P.n_waits, A.peak), flush=True)
        import os as _os3
        if _os3.environ.get('PRINTOPS'):
            for o in P.all[int(_os3.environ['PRINTOPS']):]:
                print('OP', o.idx, o.eng, 'dma' if o.dma else '', 'line', o.fn.__code__.co_firstlineno, sorted(o.deps), o.seq, flush=True)
    return nc


def _host_consts(c, core):
    T, NM, NC_ = c.T, c.NM, c.NCORES
    TT = T + NM
    idx = np.arange(128)
    s_ = idx[:, None]; t_ = idx[None, :]
    ones = np.ones((128, 128), np.float32)
    M_le = (s_ <= t_).astype(np.float32); M_gt = (s_ > t_).astype(np.float32)
    M_ge = (s_ >= t_).astype(np.float32); M_lt = (s_ < t_).astype(np.float32)
    pos = np.concatenate([NM + core * T + np.arange(T), np.arange(NM)]).astype(np.float32)
    inv_freq = (np.float32(10000.0) ** (-np.arange(0, 64, 2, dtype=np.float32) / np.float32(64))).astype(np.float32)
    ang = (pos[:, None] * inv_freq[None, :]).astype(np.float32)
    cos = np.cos(ang).astype(np.float32).T
    sin = np.sin(ang).astype(np.float32).T
    cos2 = np.zeros((128, TT), np.float32); sin2 = np.zeros((128, TT), np.float32)
    cos2[0:32] = cos; cos2[32:64] = cos; sin2[0:32] = sin; sin2[32:64] = sin
    r = np.arange(NC_)
    mf = np.broadcast_to((r == core).astype(np.float32)[None, :], (128, NC_))
    mb = np.broadcast_to((r > core).astype(np.float32)[None, :], (128, NC_))
    rot = np.zeros((128, 64), np.float32)
    for i in range(32):
        rot[i + 32, i] = -1.0
        rot[i, i + 32] = 1.0
    cbf = np.concatenate([np.eye(128, dtype=np.float32), ones, rot], axis=1).astype(ml_dtypes.bfloat16)
    return [ones, M_le, M_gt, M_ge, M_lt, cos2, sin2, mf, 1.0 - mf, mb, 1.0 - mb], cbf


def _run(c, inputs, dbg=False, stop=None):
    T, NM, NC_ = c.T, c.NM, c.NCORES
    f = lambda a: np.ascontiguousarray(np.asarray(a, dtype=np.float32))
    x = f(inputs["x"])[0]
    gT = lambda v: f(v).reshape(-1, 128).T
    shared = {
        "meta": f(inputs["meta_tokens"]),
        "w_in": f(inputs["w_in"])[0],
        "wfa": np.concatenate([f(inputs["gla_wf"])[0], f(inputs["gla_bf"])], axis=0),
        "wba": np.concatenate([f(inputs["gla_wb"])[0], f(inputs["gla_bb"])], axis=0),
        "w_uq": f(inputs["w_uq"])[0], "w_ukv": f(inputs["w_ukv"])[0],
        "w_gla_out": f(inputs["w_gla_out"])[0], "w_mla_out": f(inputs["w_mla_out"])[0],
        "w_o": f(inputs["w_o"])[0], "w_ff1": f(inputs["w_ff1"])[0], "w_ff2": f(inputs["w_ff2"])[0],
        "glan": np.ascontiguousarray(np.broadcast_to(np.tile(f(inputs["gla_norm"])[0], c.GH)[None, :], (128, c.VW))),
        "finn": np.ascontiguousarray(np.broadcast_to(f(inputs["final_norm"])[None, :], (128, c.D))),
        "x_full": x,
    }
    posA = np.arange(NM + NC_ * T, dtype=np.float32)
    inv_freq = (np.float32(10000.0) ** (-np.arange(0, 64, 2, dtype=np.float32) / np.float32(64))).astype(np.float32)
    angA = (posA[:, None] * inv_freq[None, :]).astype(np.float32)
    cA = np.cos(angA).astype(np.float32).T
    sA = np.sin(angA).astype(np.float32).T
    shared["cosA"] = np.ascontiguousarray(np.concatenate([cA, cA], axis=0))
    shared["sinA"] = np.ascontiguousarray(np.concatenate([sA, sA], axis=0))
    tails = [gT(inputs["ln1"][0]), gT(inputs["ln2"][0]), gT(inputs["q_norm"][0]), gT(inputs["kv_norm"][0])]
    in_maps = []
    for core in range(NC_):
        parts, cbf = _host_consts(c, core)
        m = dict(shared)
        m["x"] = np.ascontiguousarray(x[core * T:(core + 1) * T])
        m["cf32"] = np.ascontiguousarray(np.concatenate(parts + tails, axis=1).astype(np.float32))
        m["cbf"] = np.ascontiguousarray(cbf)
        in_maps.append(m)
    nc = build_program(c, dbg, stop)
    res = run_bass_kernel_spmd(nc, in_maps, core_ids=list(range(NC_)))
    outs = [np.asarray(res.results[i]["out"], dtype=np.float32) for i in range(NC_)]
    return np.concatenate(outs, axis=0)[None], res


def kernel(**inputs):
    c = Cfg()
    out, _ = _run(c, inputs)
    return out
```

```python
import numpy as np
import ml_dtypes
from contextlib import ExitStack
import concourse.bass as bass
import concourse.mybir as mybir
from concourse.bass_utils import run_bass_kernel_spmd

F32 = mybir.dt.float32
BF16 = mybir.dt.bfloat16
AF = mybir.ActivationFunctionType
ALU = mybir.AluOpType
AX = mybir.AxisListType

ENGS = ['pe', 'act', 'dve', 'pool', 'sp']
NDMA_SEM = 8
import os as _osx
POOL_COMPUTE = _osx.environ.get('POOLC', 'dve')
PROFILE = bool(_osx.environ.get('KPROFILE'))
EPS = 1e-6


class Cfg:
    def __init__(s, D=4096, SEQ=8192, NCORES=8, NM=16, GH=8, GR=16, MH=16, QR=1024, KVR=512, DFF=16384):
        s.D, s.SEQ, s.NCORES, s.NM, s.GH, s.GR, s.MH, s.QR, s.KVR, s.DFF = D, SEQ, NCORES, NM, GH, GR, MH, QR, KVR, DFF
        s.DK, s.DV, s.NOPE, s.ROPE, s.VD = 128, 256, 128, 64, 128
        s.T = SEQ // NCORES
        s.QKW = GH * s.DK
        s.VW = GH * s.DV
        s.MVW = MH * s.VD
        names = ['gq', 'gk', 'gv', 'gog', 'glrf', 'glrb', 'cq', 'ckv', 'krope', 'ga', 'gb']
        sizes = [s.QKW, s.QKW, s.VW, s.VW, GR, GR, QR, KVR, s.ROPE, D, D]
        s.off = {}
        o = 0
        for n, z in zip(names, sizes):
            s.off[n] = o
            o += z
        s.INW = o


class _Op:
    __slots__ = ('eng', 'fn', 'dma', 'deps', 'flag', 'seq', 'dslot', 'dval', 'idx', 'bar', 'cc', 'stage')


class Prog:
    def __init__(self, nc, st):
        self.nc = nc
        self.st = st
        self.all = []
        self.per = {e: [] for e in ENGS}
        self.last_w = {}
        self.readers = {}
        self.dma_n = {e: 0 for e in ENGS}
        self.sync_same = {'act': True, 'dve': True, 'pool': True, 'pe': False, 'sp': False}
        self.barriers = []
        self.dma_since_bar = []

    def sbuf(self, name, shape, dt):
        return self.st.enter_context(self.nc.sbuf_tensor(name, list(shape), dt))

    def psum(self, name, shape, dt):
        return self.st.enter_context(self.nc.psum_tensor(name, list(shape), dt))

    def op(self, eng, fn, reads=(), writes=(), dma=False, cc=False, nobar=False):
        if eng == 'pool' and not dma and not cc:
            eng = POOL_COMPUTE
        o = _Op()
        o.eng = eng; o.fn = fn; o.dma = dma or cc; o.flag = dma or cc; o.seq = None
        o.cc = cc
        o.stage = getattr(self, 'stage', 'S')
        o.idx = len(self.all)
        o.bar = 0 if nobar else len(self.barriers)
        psr_ = [k for k in reads if isinstance(k, tuple) and k[0] == 'ps']
        if psr_:
            reads = [k for k in reads if k not in psr_]
            writes = list(writes) + [k for k in psr_ if k not in writes]
        deps = set()
        for k in reads:
            w = self.last_w.get(k)
            if w is not None:
                deps.add(w)
        for k in writes:
            w = self.last_w.get(k)
            if w is not None:
                deps.add(w)
            for r in self.readers.get(k, ()):
                deps.add(r)
        o.deps = deps
        if cc:
            self.ccs = getattr(self, 'ccs', [])
            o.dslot = ('cc', len(self.ccs))
            o.dval = 1
            self.ccs.append(o.idx)
            self.dma_since_bar.append(o.idx)
        elif dma:
            n = self.dma_n[eng]
            o.dslot = n % NDMA_SEM
            o.dval = 16 * (n // NDMA_SEM + 1)
            self.dma_n[eng] = n + 1
            self.dma_since_bar.append(o.idx)
        for k in writes:
            self.last_w[k] = o.idx
            self.readers[k] = []
        for k in reads:
            self.readers.setdefault(k, []).append(o.idx)
        self.all.append(o)
        self.per[eng].append(o)
        if getattr(self, 'max_ops', None) and len(self.all) >= self.max_ops:
            raise _Stop()
        return o

    def barrier(self):
        deps = list(self.dma_since_bar)
        for e in ENGS:
            for o in reversed(self.per[e]):
                if not o.dma:
                    deps.append(o.idx)
                    break
        self.barriers.append(deps)
        self.dma_since_bar = []
        if getattr(self, 'stop_at', None) is not None and len(self.barriers) == self.stop_at:
            raise _Stop()

    def dma(self, eng, out, in_, reads=(), writes=(), nobar=False, **kw):
        return self.op(eng, lambda e: e.dma_start(out=out, in_=in_, **kw), reads, writes, dma=True, nobar=nobar)

    def emit(self):
        nc = self.nc
        ops = self.all
        for o in ops:
            for d in o.deps:
                od = ops[d]
                if od.dma:
                    continue
                if od.eng != o.eng or o.dma or self.sync_same[o.eng]:
                    od.flag = True
        for bl in self.barriers:
            for d in bl:
                if not ops[d].dma:
                    ops[d].flag = True
        for e in ENGS:
            c = 0
            for o in self.per[e]:
                if not o.dma and o.flag:
                    c += 1
                    o.seq = c
        csem = {e: self.st.enter_context(nc.semaphore("c_" + e)) for e in ENGS}
        dsem = {e: [self.st.enter_context(nc.semaphore("d_%s%d" % (e, i))) for i in range(NDMA_SEM)]
                for e in ENGS if self.dma_n[e] > 0}
        ccsem = [self.st.enter_context(nc.semaphore("cc%d" % i)) for i in range(len(getattr(self, 'ccs', [])))]
        final_waits = [(sm, 1) for sm in ccsem]
        for e in dsem:
            n = self.dma_n[e]
            for s in range(min(n, NDMA_SEM)):
                cnt = (n - 1 - s) // NDMA_SEM + 1
                final_waits.append((dsem[e][s], 16 * cnt))
        self.n_waits = 0

        def make(e):
            def f(engobj):
                waited = {}
                nbar = [0]

                def wait(sem, val, key):
                    if waited.get(key, 0) >= val:
                        return
                    waited[key] = val
                    engobj.wait_ge(sem, val)
                    self.n_waits += 1

                def wait_op(od, o):
                    if od.cc:
                        wait(ccsem[od.dslot[1]], 1, ('cc', od.dslot[1]))
                    elif od.dma:
                        wait(dsem[od.eng][od.dslot], od.dval, ('d', od.eng, od.dslot))
                    else:
                        if od.eng == e and o is not None and not (o.dma or self.sync_same[e]):
                            return
                        wait(csem[od.eng], od.seq, ('c', od.eng))

                for o in self.per[e]:
                    while nbar[0] < o.bar:
                        mx = {}
                        for d in self.barriers[nbar[0]]:
                            od = ops[d]
                            if od.cc:
                                mx[('cc', od.dslot[1])] = (ccsem[od.dslot[1]], 1)
                            elif od.dma:
                                key = ('d', od.eng, od.dslot)
                                if mx.get(key, (None, 0))[1] < od.dval:
                                    mx[key] = (dsem[od.eng][od.dslot], od.dval)
                            elif od.eng != e or e != 'pe':
                                key = ('c', od.eng)
                                if mx.get(key, (None, 0))[1] < od.seq:
                                    mx[key] = (csem[od.eng], od.seq)
                        for key, (sem, val) in mx.items():
                            wait(sem, val, key)
                        nbar[0] += 1
                    for d in sorted(o.deps):
                        wait_op(ops[d], o)
                    if o.dma and not o.cc and o.dval > 16:
                        wait(dsem[e][o.dslot], o.dval - 16, ('d', e, o.dslot))
                    if PROFILE:
                        with nc.named_scope(o.stage):
                            ins = o.fn(engobj)
                    else:
                        ins = o.fn(engobj)
                    if o.cc:
                        ins.then_inc(ccsem[o.dslot[1]])
                    elif o.dma:
                        ins.then_inc(dsem[e][o.dslot], 16)
                    elif o.flag:
                        ins.then_inc(csem[e], 1)
                if e == 'sp':
                    for (sem, val) in final_waits:
                        engobj.wait_ge(sem, val)
            return f

        with nc.Block() as block:
            block.sync(make('sp'))
            block.tensor(make('pe'))
            block.scalar(make('act'))
            block.vector(make('dve'))
            block.gpsimd(make('pool'))


class Arena:
    def __init__(self, P, nbytes):
        self.t = P.sbuf("arena", [128, nbytes // 2], BF16)
        self.n = nbytes
        self.top = 0
        self.peak = 0

    def alloc(self, shape, dt):
        ne = int(np.prod(shape))
        nb = ne * (4 if dt == F32 else 2)
        off = self.top
        self.top += (nb + 63) // 64 * 64
        assert self.top <= self.n, ("SBUF arena overflow", self.top, self.n)
        self.peak = max(self.peak, self.top)
        v = self.t[:, off // 2: off // 2 + nb // 2]
        if dt == F32:
            v = v.bitcast(F32)
        if len(shape) == 2:
            v = v.rearrange("p (a b) -> p a b", b=shape[1])
        elif len(shape) == 3:
            v = v.rearrange("p (a b c) -> p a b c", b=shape[1], c=shape[2])
        return v

    def mark(self):
        return self.top

    def reset(self, m):
        self.top = m


class _Stop(Exception):
    pass


def build_program(c, dbg=False, stop=None):
    nc = bass.Bass("TRN2", target_bir_lowering=False)
    D, T, NM, NC_ = c.D, c.T, c.NM, c.NCORES
    TT = T + NM
    KC = D // 128
    NT = T // 128
    GH, MH, QR, KVR, DFF = c.GH, c.MH, c.QR, c.KVR, c.DFF
    QKW, VW, MVW = c.QKW, c.VW, c.MVW
    QC, KVC, FC = QR // 128, KVR // 128, DFF // 128
    TS = [(h * 512, min(512, T - h * 512)) for h in range((T + 511) // 512)]

    def din(name, shape, dt=F32):
        return nc.dram_tensor(name, list(shape), dt, kind="ExternalInput").ap()

    def dscr(name, shape, dt):
        return nc.dram_tensor(name, list(shape), dt).ap()

    x = din("x", [T, D]); meta = din("meta", [NM, D])
    x_full = din("x_full", [NC_ * T, D])
    cosA = din("cosA", [64, NC_ * T + NM]); sinA = din("sinA", [64, NC_ * T + NM])
    w_in = din("w_in", [D, c.INW])
    wfa = din("wfa", [c.GR + 1, QKW]); wba = din("wba", [c.GR + 1, QKW])
    w_uq = din("w_uq", [QR, MH * 192]); w_ukv = din("w_ukv", [KVR, MH * 256])
    w_gla_out = din("w_gla_out", [VW, D]); w_mla_out = din("w_mla_out", [MVW, D])
    w_o = din("w_o", [D, D]); w_ff1 = din("w_ff1", [D, DFF]); w_ff2 = din("w_ff2", [DFF, D])
    NF = 128 * 5 + 2 * TT + 4 * NC_ + 2 * KC + QC + KVC
    cf_in = din("cf32", [128, NF])
    cb_in = din("cbf", [128, 320], BF16)
    glan_in = din("glan", [128, VW])
    fin_in = din("finn", [128, D])
    out = nc.dram_tensor("out", [T, D], F32, kind="ExternalOutput").ap()

    R1 = MH * 128 + 64 + MH * 128
    recv1 = nc.dram_tensor("recv1", [NC_ * R1, T], BF16)
    metaK = dscr("metaK", [MH * 128 + 64, NM], BF16)
    metaV = dscr("metaV", [NM, MVW], BF16)
    C2 = 2 * GH * 257
    GdA = [dscr("GdA%d" % i, [c.GR + 1, NC_ * T + NM], BF16) for i in range(2)]
    kall = dscr("kall", [NC_ * T + NM, QKW], BF16); vall = dscr("vall", [NC_ * T + NM, VW], BF16)
    QnT = dscr("QnT", [MH * 128, T], BF16); QpT = dscr("QpT", [MH * 64, T], BF16)
    q_s = dscr("q_s", [T, QKW], BF16); k_s = dscr("k_s", [TT, QKW], BF16); v_s = dscr("v_s", [TT, VW], BF16)
    gg_s = dscr("gg_s", [T, VW], BF16)
    sga = dscr("sga", [D, T], BF16); sgb = dscr("sgb", [D, T], BF16)
    o_f = dscr("o_f", [T, VW], F32); o_b = dscr("o_b", [T, VW], F32)
    qbf = dscr("qbf", [QKW, T], BF16); qbb = dscr("qbb", [QKW, T], BF16)
    OT = dscr("OT", [MVW, T], BF16)
    h1 = dscr("h1", [T, D], F32)
    uT = dscr("uT", [DFF, T], BF16)

    with ExitStack() as st:
        P = Prog(nc, st)
        A = Arena(P, 211968)
        P.stop_at = stop if isinstance(stop, int) else None
        import os as _os2
        P.max_ops = int(_os2.environ.get('MAXOPS', '0'))

        def chk(name):
            if stop == name:
                raise _Stop()
        try:
            PS = P.psum("ps", [128, 4096], F32)

            def bank(b, n=512):
                return PS[:, 512 * b: 512 * b + n]

            def bankbf(b):
                return PS[:, 512 * b: 512 * b + 512].bitcast(BF16)

            psr = [0]
            ukc = [0]

            def uk():
                ukc[0] += 1
                return ('uk', ukc[0])

            def nextbank(n=1):
                b = psr[0]
                psr[0] = (psr[0] + n) % 8
                return b

            def mm(out_, lhsT, rhs, start, stop, reads, writes):
                P.op('pe', lambda e: e.matmul(out_, lhsT=lhsT, rhs=rhs, start=start, stop=stop), reads, writes)

            def tr(out_, in_, ident_, reads, writes):
                P.op('pe', lambda e: e.transpose(out_, in_, ident_), reads, writes)

            def act(out_, in_, func, reads, writes, scale=1.0, bias=0.0, accum_out=None, eng='act'):
                if accum_out is None:
                    P.op('act', lambda e: e.activation(out=out_, in_=in_, func=func, bias=bias, scale=scale), reads, writes)
                else:
                    P.op('act', lambda e: e.activation(out=out_, in_=in_, func=func, bias=bias, scale=scale,
                                                       accum_out=accum_out), reads, writes)

            def amul(out_, in_, mulv, reads, writes):
                P.op('act', lambda e: e.mul(out=out_, in_=in_, mul=mulv), reads, writes)

            def tt(eng, out_, in0, in1, op, reads, writes):
                P.op(eng, lambda e: e.tensor_tensor(out=out_, in0=in0, in1=in1, op=op), reads, writes)

            def ts(eng, out_, in0, s1, s2, op0, op1, reads, writes):
                if op1 is None:
                    P.op(eng, lambda e: e.tensor_scalar(out=out_, in0=in0, scalar1=s1, scalar2=None, op0=op0), reads, writes)
                else:
                    P.op(eng, lambda e: e.tensor_scalar(out=out_, in0=in0, scalar1=s1, scalar2=s2, op0=op0, op1=op1), reads, writes)

            def stt(eng, out_, in0, scalar, in1, op0, op1, reads, writes):
                P.op(eng, lambda e: e.scalar_tensor_tensor(out=out_, in0=in0, scalar=scalar, in1=in1, op0=op0, op1=op1),
                     reads, writes)

            def cp(eng, out_, in_, reads, writes):
                if eng == 'act':
                    P.op('act', lambda e: e.copy(out=out_, in_=in_), reads, writes)
                else:
                    P.op(eng, lambda e: e.tensor_copy(out=out_, in_=in_), reads, writes)

            def memset(eng, ap, val, writes):
                P.op(eng, lambda e: e.memset(ap, val), (), writes)

            def rstd_from_ss(eng, dst, ss, n, reads, writes):
                ts(eng, dst, ss, 1.0 / n, EPS, ALU.mult, ALU.add, reads, writes)
                P.op('act', lambda e: e.sqrt(out=dst, in_=dst), writes, writes)
                P.op('dve', lambda e: e.reciprocal(out=dst, in_=dst), writes, writes)

            cf = A.alloc([NF], F32)
            cb = A.alloc([320], BF16)
            P.dma('sp', cf, cf_in, writes=['cf'])
            P.dma('sp', cb, cb_in, writes=['cb'])
            o_ = [0]

            def take(n):
                v = cf[:, o_[0]: o_[0] + n]
                o_[0] += n
                return v
            ones_f = take(128); M_le = take(128); M_gt = take(128); M_ge = take(128); M_lt = take(128)
            cos2 = take(TT); sin2 = take(TT)
            oh = take(NC_); _u1 = take(NC_); _u2 = take(NC_); _u3 = take(NC_)
            ln1T = take(KC); ln2T = take(KC); qnT = take(QC); kvnT = take(KVC)
            ident = cb[:, 0:128]; ones_b = cb[:, 128:256]; rot = cb[:, 256:320]
            CK = ['cf', 'cb']
            GR = c.GR
            Gd = [A.alloc([TT], BF16) for _ in range(2)]
            wga = [A.alloc([QKW], BF16) for _ in range(2)]
            for di in range(2):
                P.dma('pool', wga[di][:GR + 1, :], (wfa, wba)[di], writes=[('wga', di)])
            Sst = [A.alloc([GH, 256], BF16) for _ in range(2)]
            pers = A.mark()

            class WStream:
                def __init__(self, kcmax, exempt=False):
                    self.buf = [A.alloc([kcmax, 512], BF16) for _ in range(2)]
                    self.n = 0
                    self.exempt = exempt

                def load(self, Wap, c0, nb, kc):
                    s = self.n % 2
                    self.n += 1
                    Wv = Wap.rearrange("(c p) n -> p c n", p=128)
                    g = 8
                    for q in range(0, kc, g):
                        qe = min(kc, q + g)
                        P.dma('pool', self.buf[s][:, q:qe, 0:nb], Wv[:, q:qe, c0:c0 + nb], writes=[('W', s, q // g)], nobar=self.exempt)
                    return s

                def rkeys(self, s, k):
                    return [('W', s, k // 8)]

            def linear_a(ws, Wap, kc, blocks, actT, akeys, segs, epi):
                nxt = ws.load(Wap, blocks[0][0], blocks[0][1], kc)
                for bi, (c0, nb) in enumerate(blocks):
                    s = nxt
                    if bi + 1 < len(blocks):
                        nxt = ws.load(Wap, blocks[bi + 1][0], blocks[bi + 1][1], kc)
                    for j0 in range(0, nb, 128):
                        m = min(128, nb - j0)
                        b0 = nextbank(len(segs))
                        for si, (t0, n) in enumerate(segs):
                            b = (b0 + si) % 8
                            for k in range(kc):
                                mm(bank(b)[:m, :n], ws.buf[s][:, k, j0:j0 + m], actT[:, k, t0:t0 + n], k == 0, k == kc - 1,
                                   ws.rkeys(s, k) + akeys, [('ps', b)])
                            epi(c0 + j0, m, si, b, bank(b)[:m, :n])

            def linear_b(ws, Wap, kc, blocks, actT, akeys, tiles, epi):
                nxt = ws.load(Wap, blocks[0][0], blocks[0][1], kc)
                for bi, (c0, nb) in enumerate(blocks):
                    s = nxt
                    if bi + 1 < len(blocks):
                        nxt = ws.load(Wap, blocks[bi + 1][0], blocks[bi + 1][1], kc)
                    for ti, (t0, np_) in enumerate(tiles):
                        b = nextbank()
                        for k in range(kc):
                            mm(bank(b)[:np_, :nb], actT[:, k, t0:t0 + np_], ws.buf[s][:, k, 0:nb], k == 0, k == kc - 1,
                               ws.rkeys(s, k) + akeys, [('ps', b)])
                        epi(c0, nb, ti, b, bank(b)[:np_, :nb])

            def blocks_of(c0, n, step=512):
                return [(c0 + i, min(step, n - i)) for i in range(0, n, step)]

            def stage_norm(srcs, gT, dstT):
                m0 = A.mark()
                xin = [A.alloc([D], F32) for _ in range(2)]
                xs = [A.alloc([D], BF16) for _ in range(2)]
                stt_ = A.alloc([8], F32)
                for i, (src, np_, tok0) in enumerate(srcs):
                    b = i % 2
                    ss = stt_[:, 4 * b: 4 * b + 1]
                    rs = stt_[:, 4 * b + 1: 4 * b + 2]
                    P.dma('sp', xin[b][:np_], src, writes=[('xin', b)])
                    memset('dve', ss, 0.0, [('nst', b)])
                    act(xs[b][:np_], xin[b][:np_], AF.Square, [('xin', b), ('nst', b)], [('xs', b), ('nst', b)], accum_out=ss[:np_])
                    rstd_from_ss('dve', rs[:np_], ss[:np_], D, [('nst', b)], [('nrs', b)])
                    ts('dve', xs[b][:np_], xin[b][:np_], rs[:np_], None, ALU.mult, None, [('xin', b), ('nrs', b)], [('xs', b)])
                    for q0 in range(0, KC, 8):
                        pb = nextbank()
                        pv = bankbf(pb)
                        for kk in range(min(8, KC - q0)):
                            k = q0 + kk
                            tr(pv[:, 128 * kk: 128 * kk + np_], xs[b][:np_, 128 * k: 128 * k + 128], ident[:np_, :np_],
                               [('xs', b)] + CK, [('ps', pb)])
                        for kk in range(min(8, KC - q0)):
                            k = q0 + kk
                            src_ = pv[:, 128 * kk: 128 * kk + np_]
                            dst_ = dstT[:, k, tok0: tok0 + np_]
                            if kk % 2 == 0:
                                amul(dst_, src_, gT[:, k: k + 1], [('ps', pb)] + CK, [('nT', tok0)])
                            else:
                                ts('dve', dst_, src_, gT[:, k: k + 1], None, ALU.mult, None, [('ps', pb)] + CK, [('nT', tok0)])
                A.reset(m0)

            def make_gla(G):
                class _NS:
                    pass
                g = _NS()
                qk_in = [A.alloc([2, QKW], BF16) for _ in range(2)]
                v_in = [A.alloc([VW], BF16) for _ in range(2)]
                xg_e = A.alloc([QKW], F32)
                sp_t = A.alloc([QKW], F32)
                Rsum = A.alloc([QKW], F32)
                Ep = A.alloc([QKW], F32); Em = A.alloc([QKW], F32); Gk = A.alloc([QKW], F32); Pb = A.alloc([QKW], F32)
                qt_ = A.alloc([QKW], BF16); kt_ = A.alloc([QKW], BF16); kh_ = A.alloc([QKW], BF16); qb_ = A.alloc([QKW], BF16)
                qtT = A.alloc([GH, 128], BF16); ktT = A.alloc([GH, 128], BF16); qbT = A.alloc([GH, 128], BF16)
                attm = [A.alloc([128], BF16) for _ in range(2)]
                Sloc = A.alloc([GH, 256], F32)
                Sbf = A.alloc([GH, 256], BF16)
                a_h = A.alloc([GH], F32)
                o_st = [A.alloc([VW], F32) for _ in range(2)]
                ocnt = [0]

                def gla_gates(di, t0, np_, first, state_only=False):
                    Mi, Mc = ((M_le, M_gt), (M_ge, M_lt))[di]
                    gap_, gkeys_ = G(di, t0, np_)
                    for h0 in range(0, QKW, 512):
                        nb = min(512, QKW - h0)
                        b = nextbank()
                        mm(bank(b)[:np_, :nb], gap_, wga[di][:GR + 1, h0:h0 + nb], True, True,
                           gkeys_ + [('wga', di)], [('ps', b)])
                        act(xg_e[:np_, h0:h0 + nb], bank(b)[:np_, :nb], AF.Exp, [('ps', b)], ['xg_e'], scale=-1.0)
                    act(sp_t[:np_, :], xg_e[:np_, :], AF.Ln, ['xg_e'], ['sp_t'], bias=1.0)
                    for h0 in range(0, QKW, 512):
                        nb = min(512, QKW - h0)
                        b = nextbank()
                        if not state_only:
                            mm(bank(b)[:np_, :nb], Mi[:np_, :np_], sp_t[:np_, h0:h0 + nb], True, True, ['sp_t'] + CK, [('ps', b)])
                            act(Ep[:np_, h0:h0 + nb], bank(b)[:np_, :nb], AF.Exp, [('ps', b)], ['Ep'], scale=-1.0 / 16)
                            act(Em[:np_, h0:h0 + nb], bank(b)[:np_, :nb], AF.Exp, [('ps', b)], ['Em'], scale=1.0 / 16)
                            b = nextbank()
                        mm(bank(b)[:np_, :nb], Mc[:np_, :np_], sp_t[:np_, h0:h0 + nb], True, True, ['sp_t'] + CK, [('ps', b)])
                        act(Gk[:np_, h0:h0 + nb], bank(b)[:np_, :nb], AF.Exp, [('ps', b)], ['Gk'], scale=-1.0 / 16)
                        if not first:
                            b = nextbank()
                            mm(bank(b)[:np_, :nb], ones_f[:, :np_], Rsum[:, h0:h0 + nb], True, True, ['Rsum'] + CK, [('ps', b)])
                            act(Pb[:np_, h0:h0 + nb], bank(b)[:np_, :nb], AF.Exp, [('ps', b)], ['Pb'], scale=-1.0 / 16)

                def gla_state_update(di, np_, kin, vin, kkeys, vkeys, bf=True):
                    tt('dve', kh_[:np_, :], kin, Gk[:np_, :], ALU.mult, kkeys + ['Gk'], ['kh_'])
                    ba = nextbank()
                    for h in range(GH):
                        mm(bank(ba)[:, h:h + 1], sp_t[:np_, 128 * h:128 * h + 128], ones_f[:np_, 0:1], True, True, ['sp_t'] + CK, [('ps', ba)])
                    act(a_h, bank(ba)[:, :GH], AF.Exp, [('ps', ba)], ['a_h'], scale=-1.0 / 16)
                    for h in range(GH):
                        b = nextbank()
                        mm(bank(b)[:, :256], kh_[:np_, 128 * h:128 * h + 128], vin[:np_, 256 * h:256 * h + 256], True, True,
                           ['kh_'] + vkeys, [('ps', b)])
                        stt('dve', Sloc[:, h, :], Sloc[:, h, :], a_h[:, h:h + 1], bank(b)[:, :256], ALU.mult, ALU.add,
                            [('ps', b), 'a_h', ('Sloc', h)], [('Sloc', h)])
                        if bf:
                            cp('act', Sbf[:, h, :], Sloc[:, h, :], [('Sloc', h)], [('Sbf', h)])

                for _n, _v in (('qk_in', qk_in), ('v_in', v_in), ('xg_e', xg_e), ('sp_t', sp_t), ('Rsum', Rsum), ('Ep', Ep), ('Em', Em), ('Gk', Gk), ('Pb', Pb), ('qt_', qt_), ('kt_', kt_), ('kh_', kh_), ('qb_', qb_), ('qtT', qtT), ('ktT', ktT), ('qbT', qbT), ('attm', attm), ('Sloc', Sloc), ('Sbf', Sbf), ('a_h', a_h), ('o_st', o_st), ('ocnt', ocnt)):
                    setattr(g, _n, _v)
                g.gates = gla_gates
                g.state_update = gla_state_update
                return g

            own_tiles = [(x[128 * i: 128 * i + 128, :], 128, 128 * i) for i in range(NT)]
            NTOK = NC_ * T
            r1 = recv1.ap()
            mB = A.mark()
            ws = WStream(KC, exempt=True)

            P.stage = 'glob'
            mG = A.mark()
            gstg = [A.alloc([512], BF16) for _ in range(2)]
            for i_ in range(2):
                memset('dve', gstg[i_][:GR + 1, :], 1.0, [('gstg', i_)])
            gsc = [0]
            xg = A.alloc([KC, TT], BF16)
            mG2 = A.mark()

            def stage_kv(r, with_meta):
                segsM = TS + ([(T, NM)] if with_meta else [])
                tilesM = [(128 * i, 128) for i in range(NT)] + ([(T, NM)] if with_meta else [])
                nsg = len(segsM)
                XK_ = [('nT', 128 * i) for i in range(NT)] + [('nT', T)]
                ckvg = A.alloc([KVC, TT], BF16)
                sqkv = A.alloc([KVC, TT], BF16)
                krp = A.alloc([TT], F32)
                krb = A.alloc([TT], BF16)
                kro = A.alloc([TT], BF16)
                csr = A.alloc([TT], F32)
                snr = A.alloc([TT], F32)
                tmp = A.alloc([512], F32)
                P.dma('sp', csr[:64, 0:T], cosA[:, NM + r * T: NM + (r + 1) * T], writes=['csr'])
                P.dma('sp', snr[:64, 0:T], sinA[:, NM + r * T: NM + (r + 1) * T], writes=['snr'])
                if with_meta:
                    P.dma('sp', csr[:64, T:TT], cosA[:, 0:NM], writes=['csr2'])
                    P.dma('sp', snr[:64, T:TT], sinA[:, 0:NM], writes=['snr2'])
                CS = ['csr', 'snr', 'csr2', 'snr2']

                def epi_ckv(cc, m, si, b, ps):
                    t0, n = segsM[si]
                    if cc < KVR:
                        k = cc // 128
                        amul(ckvg[:, k, t0:t0 + n], ps, kvnT[:, k:k + 1], [('ps', b)] + CK, [('ckvg', si)])
                        act(sqkv[:, k, t0:t0 + n], ps, AF.Square, [('ps', b)], [('sqkv', si)])
                    else:
                        cp('act', krp[:64, t0:t0 + n], ps, [('ps', b)], [('krp', si)])
                        cp('dve', krb[:64, t0:t0 + n], ps, [('ps', b)], [('krb', si)])
                linear_a(ws, w_in[:, c.off['ckv']: c.off['ckv'] + KVR + 64], KC,
                         ([(0, 512), (512, 64)] if KVR == 512 else blocks_of(0, KVR + 64)), xg, XK_, segsM, epi_ckv)
                rkv_bc = A.alloc([TT], F32)
                rkv_tm = A.alloc([NT + 1], F32)
                for si, (t0, n) in enumerate(segsM):
                    b = nextbank()
                    for k in range(KVC):
                        mm(bank(b)[:, :n], ones_b, sqkv[:, k, t0:t0 + n], k == 0, k == KVC - 1, [('sqkv', si)] + CK, [('ps', b)])
                    rstd_from_ss('dve', rkv_bc[:, t0:t0 + n], bank(b)[:, :n], KVR, [('ps', b)], [('rkv_bc', si)])
                bq = nextbank()
                for ti, (t0, np_) in enumerate(tilesM):
                    si = min(t0 // 512, len(TS) - 1) if t0 < T else len(TS)
                    for k in range(KVC):
                        mm(bank(bq)[:np_, ti:ti + 1], sqkv[:, k, t0:t0 + np_], ones_b[:, 0:1], k == 0, k == KVC - 1,
                           [('sqkv', si)] + CK, [('ps', bq)])
                rstd_from_ss('dve', rkv_tm[:, :NT], bank(bq)[:, :NT], KVR, [('ps', bq)], ['rkv_tm'])
                if with_meta:
                    rstd_from_ss('dve', rkv_tm[:NM, NT:NT + 1], bank(bq)[:NM, NT:NT + 1], KVR, [('ps', bq)], ['rkv_tm2'])
                for si, (t0, n) in enumerate(segsM):
                    b = nextbank()
                    mm(bank(b)[:64, :n], rot[:64, :], krb[:64, t0:t0 + n], True, True, [('krb', si)] + CK, [('ps', b)])
                    tt('dve', tmp[:64, :n], bank(b)[:64, :n], snr[:64, t0:t0 + n], ALU.mult, [('ps', b)] + CS, ['rtmp'])
                    tt('pool', krp[:64, t0:t0 + n], krp[:64, t0:t0 + n], csr[:64, t0:t0 + n], ALU.mult, [('krp', si)] + CS, [('krp', si)])
                    tt('dve', kro[:64, t0:t0 + n], tmp[:64, :n], krp[:64, t0:t0 + n], ALU.add, ['rtmp', ('krp', si)], [('kro', si), 'rtmp'])
                for si, (t0, n) in enumerate(TS):
                    P.dma('sp', r1[r * R1 + MH * 128: r * R1 + MH * 128 + 64, t0:t0 + n], kro[:64, t0:t0 + n], reads=[('kro', si)], writes=[uk()])
                if with_meta:
                    P.dma('sp', metaK[MH * 128: MH * 128 + 64, :], kro[:64, T:TT], reads=[('kro', len(TS))], writes=[uk()])
                kst = [A.alloc([TT], BF16) for _ in range(2)]
                kcnt = [0]

                def epi_kn(cc, m, si, b, ps):
                    t0, n = segsM[si]
                    h = cc // 256
                    sb = kcnt[0] % 2
                    tt('dve', kst[sb][:, t0:t0 + n], ps, rkv_bc[:, t0:t0 + n], ALU.mult, [('ps', b), ('rkv_bc', si)], [('kst', sb, si)])
                    if si < len(TS):
                        P.dma('sp', r1[r * R1 + h * 128: r * R1 + h * 128 + 128, t0:t0 + n], kst[sb][:, t0:t0 + n], reads=[('kst', sb, si)], writes=[uk()])
                    else:
                        P.dma('sp', metaK[h * 128: h * 128 + 128, :], kst[sb][:, T:TT], reads=[('kst', sb, si)], writes=[uk()])
                    if si == nsg - 1:
                        kcnt[0] += 1
                linear_a(ws, w_ukv, KVC, [(256 * h, 128) for h in range(MH)], ckvg, [('ckvg', i) for i in range(nsg)], segsM, epi_kn)
                vst = [A.alloc([512], BF16) for _ in range(2)]
                vcnt = [0]
                w_ukv_v = w_ukv.rearrange("k (h two d) -> k h two d", two=2, d=128)
                sendV = r1[r * R1 + MH * 128 + 64: (r + 1) * R1, :].rearrange("(h p) (kt d) -> p h kt d", p=128, d=128)
                for h0 in range(0, MH, 4):
                    nh = min(4, MH - h0)
                    s = ws.n % 2
                    ws.n += 1
                    for hh in range(nh):
                        P.dma('pool', ws.buf[s][:, 0:KVC, 128 * hh: 128 * hh + 128],
                              w_ukv_v[:, h0 + hh, 1, :].rearrange("(c p) d -> p c d", p=128), writes=[('W', s, 0)], nobar=True)
                    for ti, (t0, np_) in enumerate(tilesM):
                        b = nextbank()
                        for k in range(KVC):
                            mm(bank(b)[:np_, :128 * nh], ckvg[:, k, t0:t0 + np_], ws.buf[s][:, k, 0:128 * nh], k == 0, k == KVC - 1,
                               [('W', s, 0)] + [('ckvg', i) for i in range(nsg)], [('ps', b)])
                        sb = vcnt[0] % 2
                        vcnt[0] += 1
                        ts('dve', vst[sb][:np_, :128 * nh], bank(b)[:np_, :128 * nh], rkv_tm[:np_, ti:ti + 1], None, ALU.mult, None,
                           [('ps', b), 'rkv_tm', 'rkv_tm2'], [('vst', sb)])
                        if t0 < T:
                            P.dma('sp', sendV[:, h0:h0 + nh, ti, :], vst[sb][:, :128 * nh].rearrange("p (h d) -> p h d", d=128),
                                  reads=[('vst', sb)], writes=[uk()])
                        else:
                            P.dma('sp', metaV[:, 128 * h0: 128 * (h0 + nh)], vst[sb][:NM, :128 * nh], reads=[('vst', sb)], writes=[uk()])

            def stage_kvg(r, with_meta):
                tiles_ = [(128 * i, 128) for i in range(NT)] + ([(T, NM)] if with_meta else [])
                segs_ = TS + ([(T, NM)] if with_meta else [])
                XK_ = [('nT', 128 * i) for i in range(NT)] + [('nT', T)]
                pst_ = [A.alloc([512], BF16) for _ in range(2)]
                pc_ = [0]

                def gcol(t0):
                    return r * T + t0 if t0 < T else NTOK

                def epi_kvg(c0, nb, ti, b, ps):
                    sb = pc_[0] % 2
                    pc_[0] += 1
                    t0, np_ = tiles_[ti]
                    g0 = gcol(t0)
                    cp('act' if pc_[0] % 2 else 'dve', pst_[sb][:np_, :nb], ps, [('ps', b)], [('pstg', sb)])
                    if c0 < QKW:
                        P.dma('sp', kall[g0:g0 + np_, c0:c0 + nb], pst_[sb][:np_, :nb], reads=[('pstg', sb)], writes=[uk()])
                    else:
                        P.dma('sp', vall[g0:g0 + np_, c0 - QKW:c0 - QKW + nb], pst_[sb][:np_, :nb], reads=[('pstg', sb)], writes=[uk()])
                linear_b(ws, w_in[:, QKW: 2 * QKW + VW], KC, blocks_of(0, QKW) + blocks_of(QKW, VW), xg, XK_, tiles_, epi_kvg)
                for di, nm_ in enumerate(('glrf', 'glrb')):
                    def epi_gg(cc, m, si, b, ps, di=di):
                        t0, n = segs_[si]
                        g0 = gcol(t0)
                        sb = gsc[0] % 2
                        gsc[0] += 1
                        cp('act', gstg[sb][:GR, :n], ps, [('ps', b)], [('gstg', sb)])
                        P.dma('sp', GdA[di][:, g0:g0 + n], gstg[sb][:GR + 1, :n], reads=[('gstg', sb)], writes=[uk()])
                    linear_a(ws, w_in[:, c.off[nm_]: c.off[nm_] + GR], KC, [(0, GR)], xg, XK_, segs_, epi_gg)

            for r in range(NC_):
                wm = (r == 0)
                stage_norm([(x_full[r * T + 128 * i: r * T + 128 * i + 128, :], 128, 128 * i) for i in range(NT)]
                           + ([(meta[:, :], NM, T)] if wm else []), ln1T, xg)
                P.barrier()
                P.stage = 'glob_kv'
                stage_kv(r, wm)
                P.stage = 'glob_kvg'
                stage_kvg(r, wm)
                P.stage = 'glob_norm'
                A.reset(mG2)
                P.barrier()

            P.stage = 'gstate'
            A.reset(mG)
            gtl = [A.alloc([128], BF16) for _ in range(2)]
            gtc = [0]

            def G_glob(di, t0, np_):
                ib = gtc[0] % 2
                gtc[0] += 1
                P.dma('sp', gtl[ib][:GR + 1, :np_], GdA[di][:, t0:t0 + np_], writes=[('gtl', ib)])
                return gtl[ib][:GR + 1, :np_], [('gtl', ib)]
            gl_ = make_gla(G_glob)
            Sacc = A.alloc([2, GH, 256], F32)
            memset('dve', Sacc, 0.0, ['Sacc'])
            kv_ld = [0]

            def load_kv(g0, np_):
                ib = kv_ld[0] % 2
                kv_ld[0] += 1
                P.dma('sp', gl_.qk_in[ib][:np_, 1, :], kall[g0:g0 + np_, :], writes=[('qk_in', ib, 1)])
                P.dma('sp', gl_.v_in[ib][:np_, :], vall[g0:g0 + np_, :], writes=[('v_in', ib)])
                return ib
            for di in range(2):
                memset('dve', gl_.Sloc, 0.0, [('Sloc', h) for h in range(GH)])
                if di == 0:
                    ib = load_kv(NTOK, NM)
                    gl_.gates(0, NTOK, NM, True, state_only=True)
                    gl_.state_update(0, NM, gl_.qk_in[ib][:NM, 1, :], gl_.v_in[ib], [('qk_in', ib, 1)], [('v_in', ib)], bf=False)
                for r in (range(NC_) if di == 0 else range(NC_ - 1, -1, -1)):
                    stt('dve', Sacc[:, di].rearrange("p h c -> p (h c)"), gl_.Sloc.rearrange("p h c -> p (h c)"), oh[:, r:r + 1],
                        Sacc[:, di].rearrange("p h c -> p (h c)"), ALU.mult, ALU.add, [('Sloc', h) for h in range(GH)] + ['Sacc'] + CK, ['Sacc'])
                    if (di == 0 and r == NC_ - 1) or (di == 1 and r == 0):
                        break
                    for i in (range(NT) if di == 0 else range(NT - 1, -1, -1)):
                        g0 = r * T + 128 * i
                        ib = load_kv(g0, 128)
                        gl_.gates(di, g0, 128, True, state_only=True)
                        gl_.state_update(di, 128, gl_.qk_in[ib][:, 1, :], gl_.v_in[ib], [('qk_in', ib, 1)], [('v_in', ib)], bf=False)
            for di in range(2):
                cp('dve', Sst[di], Sacc[:, di], ['Sacc'], [('Sst', di)])
            P.barrier()
            A.reset(mG)

            P.stage = 'A'
            xnT = A.alloc([KC, TT], BF16)
            stage_norm(own_tiles + [(meta[:, :], NM, T)], ln1T, xnT)
            XK = [('nT', 128 * i) for i in range(NT)] + [('nT', T)]
            segsM = TS + [(T, NM)]
            P.barrier()

            P.stage = 'B2'
            m1 = A.mark()
            cqg = A.alloc([QC, T], BF16)
            sqq = A.alloc([QC, T], BF16)

            def epi_cq(cc, m, si, b, ps):
                t0, n = TS[si]
                k = cc // 128
                amul(cqg[:, k, t0:t0 + n], ps, qnT[:, k:k + 1], [('ps', b)] + CK, [('cqg', si)])
                act(sqq[:, k, t0:t0 + n], ps, AF.Square, [('ps', b)], [('sqq', si)])
            linear_a(ws, w_in[:, c.off['cq']: c.off['cq'] + QR], KC, blocks_of(0, QR), xnT, XK, TS, epi_cq)
            rq_bc = A.alloc([T], F32)
            for si, (t0, n) in enumerate(TS):
                b = nextbank()
                for k in range(QC):
                    mm(bank(b)[:, :n], ones_b, sqq[:, k, t0:t0 + n], k == 0, k == QC - 1, [('sqq', si)] + CK, [('ps', b)])
                rstd_from_ss('dve', rq_bc[:, t0:t0 + n], bank(b)[:, :n], QR, [('ps', b)], [('rq_bc', si)])
            qst = [A.alloc([T], BF16) for _ in range(2)]
            qpf = [A.alloc([T], F32)]
            qpb = [A.alloc([T], BF16)]
            qtm = [A.alloc([T], F32)]
            qcnt = [0]
            SC = 192.0 ** -0.5

            def epi_q(cc, m, si, b, ps):
                t0, n = TS[si]
                h = cc // 192
                sb = qcnt[0] % 2
                CQ = [('cqg', i) for i in range(len(TS))]
                if m == 128:
                    stt('dve', qst[sb][:, t0:t0 + n], ps, SC, rq_bc[:, t0:t0 + n], ALU.mult, ALU.mult,
                        [('ps', b), ('rq_bc', si)], [('qst', sb, si)])
                    P.dma('sp', QnT[h * 128: h * 128 + 128, t0:t0 + n], qst[sb][:, t0:t0 + n], reads=[('qst', sb, si)], writes=[uk()])
                else:
                    stt('dve', qpf[0][:64, t0:t0 + n], ps, SC, rq_bc[:64, t0:t0 + n], ALU.mult, ALU.mult,
                        [('ps', b), ('rq_bc', si)], [('qpf', 0, si)])
                    cp('act', qpb[0][:64, t0:t0 + n], qpf[0][:64, t0:t0 + n], [('qpf', 0, si)], [('qpb', 0, si)])
                    b2 = nextbank()
                    mm(bank(b2)[:64, :n], rot[:64, :], qpb[0][:64, t0:t0 + n], True, True, [('qpb', 0, si)] + CK, [('ps', b2)])
                    tt('dve', qtm[0][:64, t0:t0 + n], bank(b2)[:64, :n], sin2[:64, t0:t0 + n], ALU.mult, [('ps', b2)] + CK, [('qtm', 0, si)])
                    tt('pool', qpf[0][:64, t0:t0 + n], qpf[0][:64, t0:t0 + n], cos2[:64, t0:t0 + n], ALU.mult,
                       [('qpf', 0, si)] + CK, [('qpf', 0, si)])
                    tt('dve', qpb[0][:64, t0:t0 + n], qtm[0][:64, t0:t0 + n], qpf[0][:64, t0:t0 + n], ALU.add,
                       [('qtm', 0, si), ('qpf', 0, si)], [('qpb', 0, si)])
                    P.dma('sp', QpT[h * 64: h * 64 + 64, t0:t0 + n], qpb[0][:64, t0:t0 + n], reads=[('qpb', 0, si)], writes=[uk()])
                    if si == len(TS) - 1:
                        qcnt[0] += 1
            qblocks = []
            for h in range(MH):
                qblocks += [(192 * h, 128), (192 * h + 128, 64)]
            linear_a(ws, w_uq, QC, qblocks, cqg, [('cqg', i) for i in range(len(TS))], TS, epi_q)
            A.reset(m1)
            P.barrier()

            P.stage = 'B4'
            m1 = A.mark()
            glan = A.alloc([VW], F32)
            P.dma('sp', glan, glan_in, writes=['glan'])
            tilesO = [(128 * i, 128) for i in range(NT)]
            gst = [A.alloc([512], F32) for _ in range(2)]
            gsb = [A.alloc([512], BF16) for _ in range(2)]
            gcnt = [0]

            def epi_gog(c0, nb, ti, b, ps):
                sb = gcnt[0] % 2
                gcnt[0] += 1
                act(gst[sb][:, :nb], ps, AF.Silu, [('ps', b)], [('gst', sb)])
                tt('dve', gsb[sb][:, :nb], gst[sb][:, :nb], glan[:, c0:c0 + nb], ALU.mult, [('gst', sb), 'glan'], [('gsb', sb)])
                P.dma('sp', gg_s[128 * ti: 128 * ti + 128, c0:c0 + nb], gsb[sb][:, :nb], reads=[('gsb', sb)], writes=[uk()])
            linear_b(ws, w_in[:, c.off['gog']: c.off['gog'] + VW], KC, blocks_of(0, VW), xnT, XK, tilesO, epi_gog)
            A.reset(m1)
            P.barrier()

            P.stage = 'B5'
            m1 = A.mark()
            sst = [A.alloc([T], BF16) for _ in range(2)]
            scnt = [0]
            for nm_, dst in (('ga', sga), ('gb', sgb)):
                def epi_sg(cc, m, si, b, ps, dst=dst):
                    t0, n = TS[si]
                    sb = scnt[0] % 2
                    act(sst[sb][:, t0:t0 + n], ps, AF.Sigmoid, [('ps', b)], [('sst', sb, si)])
                    P.dma('sp', dst[cc:cc + 128, t0:t0 + n], sst[sb][:, t0:t0 + n], reads=[('sst', sb, si)], writes=[uk()])
                    if si == len(TS) - 1:
                        scnt[0] += 1
                linear_a(ws, w_in[:, c.off[nm_]: c.off[nm_] + D], KC, blocks_of(0, D), xnT, XK, TS, epi_sg)
            A.reset(m1)
            P.barrier()

            P.stage = 'B3'
            m1 = A.mark()
            pst = [A.alloc([512], BF16) for _ in range(2)]
            pcnt = [0]
            QSC = 128.0 ** -0.5
            tilesAll = tilesO + [(T, NM)]

            def epi_qkv(c0, nb, ti, b, ps):
                sb = pcnt[0] % 2
                pcnt[0] += 1
                t0, np_ = tilesAll[ti]
                if c0 < QKW:
                    if t0 >= T:
                        return
                    amul(pst[sb][:np_, :nb], ps, QSC, [('ps', b)], [('pst', sb)])
                    P.dma('sp', q_s[t0:t0 + np_, c0:c0 + nb], pst[sb][:np_, :nb], reads=[('pst', sb)], writes=[uk()])
                elif c0 < 2 * QKW:
                    cp('act' if pcnt[0] % 2 else 'dve', pst[sb][:np_, :nb], ps, [('ps', b)], [('pst', sb)])
                    P.dma('sp', k_s[t0:t0 + np_, c0 - QKW:c0 - QKW + nb], pst[sb][:np_, :nb], reads=[('pst', sb)], writes=[uk()])
                else:
                    cp('act' if pcnt[0] % 2 else 'dve', pst[sb][:np_, :nb], ps, [('ps', b)], [('pst', sb)])
                    P.dma('sp', v_s[t0:t0 + np_, c0 - 2 * QKW:c0 - 2 * QKW + nb], pst[sb][:np_, :nb], reads=[('pst', sb)], writes=[uk()])
            linear_b(ws, w_in[:, 0: 2 * QKW + VW], KC, blocks_of(0, QKW) + blocks_of(QKW, QKW) + blocks_of(2 * QKW, VW), xnT, XK, tilesAll, epi_qkv)
            for di, nm_ in enumerate(('glrf', 'glrb')):
                memset('dve', Gd[di][:GR + 1, :], 1.0, [('Gd', di)])

                def epi_g(cc, m, si, b, ps, di=di):
                    t0, n = segsM[si]
                    cp('act', Gd[di][:GR, t0:t0 + n], ps, [('ps', b)], [('Gd', di)])
                linear_a(ws, w_in[:, c.off[nm_]: c.off[nm_] + GR], KC, [(0, GR)], xnT, XK, segsM, epi_g)
            P.barrier()
            A.reset(pers)

            P.stage = 'G1'
            m1 = A.mark()
            NTG = NT
            gl = make_gla(lambda di, t0, np_: (Gd[di][:GR + 1, t0:t0 + np_], [('Gd', di)]))
            qk_in = gl.qk_in; v_in = gl.v_in; xg_e = gl.xg_e; sp_t = gl.sp_t; Rsum = gl.Rsum; Ep = gl.Ep; Em = gl.Em; Gk = gl.Gk; Pb = gl.Pb; qt_ = gl.qt_; kt_ = gl.kt_; kh_ = gl.kh_; qb_ = gl.qb_; qtT = gl.qtT; ktT = gl.ktT; qbT = gl.qbT; attm = gl.attm; Sloc = gl.Sloc; Sbf = gl.Sbf; a_h = gl.a_h; o_st = gl.o_st; ocnt = gl.ocnt
            gla_gates = gl.gates; gla_state_update = gl.state_update
            for di in range(2):
                memset('dve', Sloc, 0.0, [('Sloc', h) for h in range(GH)])
                memset('pool', Sbf, 0.0, [('Sbf', h) for h in range(GH)])
                memset('pool', Rsum, 0.0, ['Rsum'])
                order = list(range(NTG)) if di == 0 else list(range(NTG - 1, -1, -1))
                mask = (M_le, M_ge)[di]
                qbdst = (qbf, qbb)[di]
                odst = (o_f, o_b)[di]
                for oi, i in enumerate(order):
                    t0 = 128 * i
                    ib = oi % 2
                    first = (oi == 0)
                    P.dma('sp', qk_in[ib][:, 0, :], q_s[t0:t0 + 128, :], reads=['q_s'], writes=[('qk_in', ib, 0)])
                    P.dma('sp', qk_in[ib][:, 1, :], k_s[t0:t0 + 128, :], reads=['k_s'], writes=[('qk_in', ib, 1)])
                    P.dma('sp', v_in[ib], v_s[t0:t0 + 128, :], reads=['v_s'], writes=[('v_in', ib)])
                    gla_gates(di, t0, 128, first)
                    qin = qk_in[ib][:, 0, :]
                    kin = qk_in[ib][:, 1, :]
                    tt('dve', qt_, qin, Ep, ALU.mult, [('qk_in', ib, 0), 'Ep'], ['qt_'])
                    tt('pool', kt_, kin, Em, ALU.mult, [('qk_in', ib, 1), 'Em'], ['kt_'])
                    if first:
                        cp('pool', qb_, qt_, ['qt_'], ['qb_'])
                    else:
                        tt('pool', qb_, qt_, Pb, ALU.mult, ['qt_', 'Pb'], ['qb_'])
                    for (src, dstT_, key) in ((qt_, qtT, 'qtT'), (kt_, ktT, 'ktT'), (qb_, qbT, 'qbT')):
                        pb = nextbank()
                        pv = bankbf(pb)
                        for h in range(GH):
                            tr(pv[:, 128 * h:128 * h + 128], src[:, 128 * h:128 * h + 128], ident, [key[:-1] + '_'] + CK, [('ps', pb)])
                        cp('act', dstT_, pv[:, :128 * GH].rearrange("p (h t) -> p h t", t=128), [('ps', pb)], [key])
                    P.dma('sp', qbdst.rearrange("(h p) t -> p h t", p=128)[:, :, t0:t0 + 128], qbT, reads=['qbT'], writes=[uk()])
                    ob0 = nextbank(VW // 512)
                    osb = ocnt[0] % 2
                    ocnt[0] += 1
                    nob = VW // 512
                    for h in range(GH):
                        b = (ob0 + nob + (h % (8 - nob))) % 8
                        mm(bank(b)[:, :128], ktT[:, h, :], qtT[:, h, :], True, True, ['ktT', 'qtT'], [('ps', b)])
                        ab = h % 2
                        tt('dve', attm[ab], bank(b)[:, :128], mask, ALU.mult, [('ps', b)] + CK, [('attm', ab)])
                        ob = (ob0 + (256 * h) // 512) % 8
                        oc = (256 * h) % 512
                        oap = bank(ob)[:, oc:oc + 256]
                        mm(oap, attm[ab], v_in[ib][:, 256 * h:256 * h + 256], True, False, [('attm', ab), ('v_in', ib)], [('ps', ob)])
                        mm(oap, qtT[:, h, :], Sbf[:, h, :], False, True, ['qtT', ('Sbf', h)], [('ps', ob)])
                    for j in range(VW // 512):
                        ob = (ob0 + j) % 8
                        cp('act' if j % 2 else 'dve', o_st[osb][:, 512 * j:512 * j + 512], bank(ob), [('ps', ob)], [('o_st', osb)])
                    P.dma('sp', odst[t0:t0 + 128, :], o_st[osb], reads=[('o_st', osb)], writes=[uk()])
                    gla_state_update(di, 128, kin, v_in[ib], [('qk_in', ib, 1)], [('v_in', ib)])
                    tt('pool', Rsum, Rsum, sp_t, ALU.add, ['Rsum', 'sp_t'], ['Rsum'])
            A.reset(m1)
            P.barrier()

            P.stage = 'M'
            A.reset(pers)
            mM = A.mark()
            Qn = A.alloc([MH, T], BF16)
            Qp = A.alloc([MH, T], BF16)
            P.dma('sp', Qn, QnT.rearrange("(h p) t -> p h t", p=128), reads=['QnT'], writes=['Qn'])
            P.dma('sp', Qp[:64], QpT.rearrange("(h p) t -> p h t", p=64), reads=['QpT'], writes=['Qp'])
            NKT = NC_ * NT
            NK = NKT * 128 + NM
            Kpe = A.alloc([NK], BF16)
            r1 = recv1.ap()
            for r in range(NC_):
                P.dma('sp', Kpe[:64, r * T:(r + 1) * T], r1[r * R1 + MH * 128: r * R1 + MH * 128 + 64, :], reads=['recv1'], writes=['Kpe'])
            P.dma('sp', Kpe[:64, NKT * 128:NK], metaK[MH * 128:MH * 128 + 64, :], reads=['metaK'], writes=['Kpe'])
            Kh = [A.alloc([NK], BF16) for _ in range(2)]
            Vh = [A.alloc([NKT + 1, 128], BF16) for _ in range(2)]
            Pt = [A.alloc([T], BF16) for _ in range(3)]
            rec = A.alloc([T], F32)
            ost = [A.alloc([T], BF16) for _ in range(2)]
            nseg = len(TS)
            assert nseg <= 2
            for h in range(MH):
                hb = h % 2
                for r in range(NC_):
                    P.dma('sp', Kh[hb][:, r * T:(r + 1) * T], r1[r * R1 + h * 128: r * R1 + h * 128 + 128, :], reads=['recv1'], writes=[('Kh', hb)])
                    vsrc = r1[r * R1 + MH * 128 + 64 + h * 128: r * R1 + MH * 128 + 64 + (h + 1) * 128, :]
                    P.dma('sp', Vh[hb][:, r * NT:(r + 1) * NT, :], vsrc.rearrange("p (kt d) -> p kt d", d=128), reads=['recv1'], writes=[('Vh', hb)])
                P.dma('sp', Kh[hb][:, NKT * 128:NK], metaK[h * 128:h * 128 + 128, :], reads=['metaK'], writes=[('Kh', hb)])
                P.dma('sp', Vh[hb][:NM, NKT, :], metaV[:, h * 128:h * 128 + 128], reads=['metaV'], writes=[('Vh', hb)])
                def emit_S(kt):
                    nk = 128 if kt < NKT else NM
                    k0 = kt * 128
                    sbk = 4 + 2 * (kt % 2)
                    for si, (t0, n) in enumerate(TS):
                        b = sbk + si
                        mm(bank(b)[:nk, :n], Kh[hb][:, k0:k0 + nk], Qn[:, h, t0:t0 + n], True, False, [('Kh', hb), 'Qn'], [('ps', b)])
                        mm(bank(b)[:nk, :n], Kpe[:64, k0:k0 + nk], Qp[:64, h, t0:t0 + n], False, True, ['Kpe', 'Qp'], [('ps', b)])

                emit_S(0)
                for kt in range(NKT + 1):
                    nk = 128 if kt < NKT else NM
                    sbk = 4 + 2 * (kt % 2)
                    pi = kt % 3
                    if kt + 1 <= NKT:
                        emit_S(kt + 1)
                    P.op('act', (lambda e, o_=Pt[pi][:nk, :T], i_=PS[:nk, 512 * sbk:512 * sbk + T]:
                                 e.activation(out=o_, in_=i_, func=AF.Exp, bias=0.0, scale=1.0)),
                         [('ps', sbk + si) for si in range(nseg)], [('Pt', pi)])
                    for si, (t0, n) in enumerate(TS):
                        mm(bank(si)[:, :n], Vh[hb][:nk, kt, :], Pt[pi][:nk, t0:t0 + n], kt == 0, kt == NKT, [('Vh', hb), ('Pt', pi)], [('ps', si)])
                        mm(bank(2 + si)[:, :n], ones_b[:nk, :], Pt[pi][:nk, t0:t0 + n], kt == 0, kt == NKT, [('Pt', pi)] + CK, [('ps', 2 + si)])
                for si, (t0, n) in enumerate(TS):
                    P.op('dve', (lambda e, o_=rec[:, t0:t0 + n], i_=bank(2 + si)[:, :n]: e.reciprocal(out=o_, in_=i_)),
                         [('ps', 2 + si)], [('rec', si)])
                    tt('dve', ost[hb][:, t0:t0 + n], bank(si)[:, :n], rec[:, t0:t0 + n], ALU.mult, [('ps', si), ('rec', si)], [('ost', hb, si)])
                    P.dma('sp', OT[h * 128:h * 128 + 128, t0:t0 + n], ost[hb][:, t0:t0 + n], reads=[('ost', hb, si)], writes=[uk()])
            A.reset(mM)
            P.barrier()

            P.stage = 'G2b'
            A.reset(pers)
            VC = VW // 128
            mgT = A.alloc([KC, T], BF16)
            mWO = A.mark()
            ogT = A.alloc([VC, T], BF16)
            mF2 = A.mark()
            qbl = [A.alloc([2, GH, 128], BF16) for _ in range(2)]
            ofl = [A.alloc([2, VW], F32) for _ in range(2)]
            ggl = [A.alloc([VW], BF16) for _ in range(2)]
            osum = A.alloc([VW], F32)
            osq = A.alloc([VW], F32)
            ssh = A.alloc([GH], F32)
            rsh = A.alloc([GH], F32)
            ogb = A.alloc([VW], BF16)
            for i in range(NT):
                t0 = 128 * i
                ib = i % 2
                P.dma('sp', qbl[ib][:, 0], qbf.rearrange("(h p) t -> p h t", p=128)[:, :, t0:t0 + 128], reads=['qb_d'], writes=[('qbl', ib)])
                P.dma('sp', qbl[ib][:, 1], qbb.rearrange("(h p) t -> p h t", p=128)[:, :, t0:t0 + 128], reads=['qb_d'], writes=[('qbl', ib)])
                P.dma('sp', ofl[ib][:, 0, :], o_f[t0:t0 + 128, :], reads=['o_d'], writes=[('ofl', ib)])
                P.dma('sp', ofl[ib][:, 1, :], o_b[t0:t0 + 128, :], reads=['o_d'], writes=[('ofl', ib)])
                P.dma('sp', ggl[ib], gg_s[t0:t0 + 128, :], reads=['gg_s'], writes=[('ggl', ib)])
                tt('pool', osum, ofl[ib][:, 0, :], ofl[ib][:, 1, :], ALU.add, [('ofl', ib)], ['osum'])
                ob0 = nextbank(VW // 512)
                for h in range(GH):
                    ob = (ob0 + (256 * h) // 512) % 8
                    oc = (256 * h) % 512
                    oap = bank(ob)[:, oc:oc + 256]
                    mm(oap, qbl[ib][:, 0, h, :], Sst[0][:, h, :], True, False, [('qbl', ib), ('Sst', 0)], [('ps', ob)])
                    mm(oap, qbl[ib][:, 1, h, :], Sst[1][:, h, :], False, True, [('qbl', ib), ('Sst', 1)], [('ps', ob)])
                for j in range(VW // 512):
                    ob = (ob0 + j) % 8
                    tt('dve', osum[:, 512 * j:512 * j + 512], osum[:, 512 * j:512 * j + 512], bank(ob), ALU.add, ['osum', ('ps', ob)], ['osum'])
                tt('pool', osq, osum, osum, ALU.mult, ['osum'], ['osq'])
                P.op('dve', lambda e: e.tensor_reduce(out=ssh, in_=osq.rearrange("p (h d) -> p h d", d=256), axis=AX.X, op=ALU.add),
                     ['osq'], ['ssh'])
                rstd_from_ss('dve', rsh, ssh, 256, ['ssh'], ['rsh'])
                for h in range(GH):
                    stt('dve', ogb[:, 256 * h:256 * h + 256], osum[:, 256 * h:256 * h + 256], rsh[:, h:h + 1],
                        ggl[ib][:, 256 * h:256 * h + 256], ALU.mult, ALU.mult, ['osum', 'rsh', ('ggl', ib)], [('ogb', h)])
                for q0 in range(0, VC, 8):
                    pb = nextbank()
                    pv = bankbf(pb)
                    nq = min(8, VC - q0)
                    for kk in range(nq):
                        k = q0 + kk
                        tr(pv[:, 128 * kk:128 * kk + 128], ogb[:, 128 * k:128 * k + 128], ident, [('ogb', k // 2)] + CK, [('ps', pb)])
                    cp('act', ogT[:, q0:q0 + nq, t0:t0 + 128], pv[:, :128 * nq].rearrange("p (k t) -> p k t", t=128), [('ps', pb)], [('ogT', i)])
            A.reset(mF2)
            P.barrier()

            P.stage = 'MG'
            OTs = A.alloc([MVW // 128, T], BF16)
            P.dma('sp', OTs, OT.rearrange("(k p) t -> p k t", p=128), reads=['OT'], writes=['OTs'])
            ws2 = WStream(max(VC, MVW // 128))
            gl = [A.alloc([2, T], BF16) for _ in range(2)]
            t1 = A.alloc([T], F32)
            mcnt = [0]
            OGK = [('ogT', i) for i in range(NT)]
            for j0 in range(0, D, 512):
                nb = min(512, D - j0)
                sA = ws2.load(w_gla_out, j0, nb, VC)
                sB = ws2.load(w_mla_out, j0, nb, MVW // 128)
                for jj in range(0, nb, 128):
                    cc = j0 + jj
                    gbf = mcnt[0] % 2
                    mcnt[0] += 1
                    P.dma('sp', gl[gbf][:, 0, :], sga[cc:cc + 128, :], reads=['ga'], writes=[('gl', gbf)])
                    P.dma('sp', gl[gbf][:, 1, :], sgb[cc:cc + 128, :], reads=['gb'], writes=[('gl', gbf)])
                    b0 = nextbank(nseg)
                    for si, (t0, n) in enumerate(TS):
                        b = (b0 + si) % 8
                        for k in range(VC):
                            mm(bank(b)[:, :n], ws2.buf[sA][:, k, jj:jj + 128], ogT[:, k, t0:t0 + n], k == 0, k == VC - 1,
                               ws2.rkeys(sA, k) + OGK, [('ps', b)])
                        tt('dve', t1[:, t0:t0 + n], bank(b)[:, :n], gl[gbf][:, 0, t0:t0 + n], ALU.mult, [('ps', b), ('gl', gbf)], [('t1', si)])
                    b0 = nextbank(nseg)
                    for si, (t0, n) in enumerate(TS):
                        b = (b0 + si) % 8
                        for k in range(MVW // 128):
                            mm(bank(b)[:, :n], ws2.buf[sB][:, k, jj:jj + 128], OTs[:, k, t0:t0 + n], k == 0, k == MVW // 128 - 1,
                               ws2.rkeys(sB, k) + ['OTs'], [('ps', b)])
                        tt('dve', gl[gbf][:, 1, t0:t0 + n], bank(b)[:, :n], gl[gbf][:, 1, t0:t0 + n], ALU.mult, [('ps', b), ('gl', gbf)], [('gl', gbf)])
                        tt('pool', mgT[:, cc // 128, t0:t0 + n], t1[:, t0:t0 + n], gl[gbf][:, 1, t0:t0 + n], ALU.add,
                           [('t1', si), ('gl', gbf)], [('mgT', si)])
            P.barrier()
            P.stage = 'WO'
            A.reset(mWO)
            ws3 = WStream(KC)
            xl = [A.alloc([512], F32) for _ in range(3)]
            wcnt = [0]

            def epi_wo(c0, nb, ti, b, ps):
                sb = wcnt[0] % 3
                wcnt[0] += 1
                t0 = 128 * ti
                P.dma('sp', xl[sb][:, :nb], x[t0:t0 + 128, c0:c0 + nb], writes=[('xl', sb)])
                tt('dve', xl[sb][:, :nb], xl[sb][:, :nb], ps, ALU.add, [('xl', sb), ('ps', b)], [('xl', sb)])
                P.dma('sp', h1[t0:t0 + 128, c0:c0 + nb], xl[sb][:, :nb], reads=[('xl', sb)], writes=[uk()])
            linear_b(ws3, w_o, KC, blocks_of(0, D), mgT, [('mgT', si) for si in range(nseg)], tilesO, epi_wo)
            P.barrier()

            P.stage = 'A2'
            A.reset(pers)
            hnT = A.alloc([KC, T], BF16)
            stage_norm([(h1[128 * i:128 * i + 128, :], 128, 128 * i) for i in range(NT)], ln2T, hnT)
            P.barrier()
            HK = [('nT', 128 * i) for i in range(NT)]

            P.stage = 'F1'
            mf1 = A.mark()
            ws4 = WStream(KC)
            ur = [A.alloc([T], F32) for _ in range(2)]
            ub = [A.alloc([T], BF16) for _ in range(2)]
            ucnt = [0]

            def epi_f1(cc, m, si, b, ps):
                t0, n = TS[si]
                sb = ucnt[0] % 2
                act(ur[sb][:, t0:t0 + n], ps, AF.Relu, [('ps', b)], [('ur', sb, si)])
                tt('dve', ub[sb][:, t0:t0 + n], ur[sb][:, t0:t0 + n], ur[sb][:, t0:t0 + n], ALU.mult, [('ur', sb, si)], [('ub', sb, si)])
                P.dma('sp', uT[cc:cc + 128, t0:t0 + n], ub[sb][:, t0:t0 + n], reads=[('ub', sb, si)], writes=[uk()])
                if si == nseg - 1:
                    ucnt[0] += 1
            linear_a(ws4, w_ff1, KC, blocks_of(0, DFF), hnT, HK, TS, epi_f1)
            P.barrier()

            P.stage = 'F2'
            A.reset(pers)
            yacc = A.alloc([NT, D], F32)
            for i in range(NT):
                P.dma('sp', yacc[:, i, :], h1[128 * i:128 * i + 128, :], reads=['h1'], writes=[('yacc', i)])
            FB = 8
            mUL = A.mark()
            ul = [A.alloc([FB, T], BF16) for _ in range(2)]
            w2b = [A.alloc([FB, 512], BF16) for _ in range(2)]
            w2n = [0]
            uTv = uT.rearrange("(f p) t -> p f t", p=128)
            w2v = w_ff2.rearrange("(f p) n -> p f n", p=128)
            for fb in range(0, FC, FB):
                nf = min(FB, FC - fb)
                us = (fb // FB) % 2
                P.dma('sp', ul[us][:, :nf, :], uTv[:, fb:fb + nf, :], reads=['uT'], writes=[('ul', us)])
                for c0 in range(0, D, 512):
                    nb = min(512, D - c0)
                    s = w2n[0] % 2
                    w2n[0] += 1
                    P.dma('pool', w2b[s][:, :nf, :nb], w2v[:, fb:fb + nf, c0:c0 + nb], writes=[('w2b', s)])
                    for i in range(NT):
                        b = nextbank()
                        for f in range(nf):
                            mm(bank(b)[:, :nb], ul[us][:, f, 128 * i:128 * i + 128], w2b[s][:, f, :nb], f == 0, f == nf - 1,
                               [('ul', us), ('w2b', s)], [('ps', b)])
                        tt('dve', yacc[:, i, c0:c0 + nb], yacc[:, i, c0:c0 + nb], bank(b)[:, :nb], ALU.add, [('yacc', i), ('ps', b)], [('yacc', i)])
            P.barrier()
            A.reset(mUL)
            finn = A.alloc([D], F32)
            P.dma('sp', finn, fin_in, writes=['finn'])
            fj = A.alloc([D], BF16)
            fst = A.alloc([2 * NT], F32)
            for i in range(NT):
                ss = fst[:, 2 * i:2 * i + 1]
                rs = fst[:, 2 * i + 1:2 * i + 2]
                memset('pool', ss, 0.0, [('fst', i)])
                act(fj, yacc[:, i, :], AF.Square, [('yacc', i), ('fst', i)], ['fj', ('fst', i)], accum_out=ss)
                rstd_from_ss('dve', rs, ss, D, [('fst', i)], [('frs', i)])
                stt('dve', yacc[:, i, :], yacc[:, i, :], rs, finn, ALU.mult, ALU.mult, [('yacc', i), ('frs', i), 'finn'], [('yacc', i)])
                P.dma('sp', out[128 * i:128 * i + 128, :], yacc[:, i, :], reads=[('yacc', i)], writes=[uk()])
        except _Stop:
            print('[kernel] stopped at barrier', stop)
        P.emit()
        print("[kernel] ops=%d waits=%d arena_peak=%d" % (len(P.all), P.n_waits, A.peak), flush=True)
        import os as _os3
        if _os3.environ.get('PRINTOPS'):
            for o in P.all[int(_os3.environ['PRINTOPS']):]:
                print('OP', o.idx, o.eng, 'dma' if o.dma else '', 'line', o.fn.__code__.co_firstlineno, sorted(o.deps), o.seq, flush=True)
    return nc


def _host_consts(c, core):
    T, NM, NC_ = c.T, c.NM, c.NCORES
    TT = T + NM
    idx = np.arange(128)
    s_ = idx[:, None]; t_ = idx[None, :]
    ones = np.ones((128, 128), np.float32)
    M_le = (s_ <= t_).astype(np.float32); M_gt = (s_ > t_).astype(np.float32)
    M_ge = (s_ >= t_).astype(np.float32); M_lt = (s_ < t_).astype(np.float32)
    pos = np.concatenate([NM + core * T + np.arange(T), np.arange(NM)]).astype(np.float32)
    inv_freq = (np.float32(10000.0) ** (-np.arange(0, 64, 2, dtype=np.float32) / np.float32(64))).astype(np.float32)
    ang = (pos[:, None] * inv_freq[None, :]).astype(np.float32)
    cos = np.cos(ang).astype(np.float32).T
    sin = np.sin(ang).astype(np.float32).T
    cos2 = np.zeros((128, TT), np.float32); sin2 = np.zeros((128, TT), np.float32)
    cos2[0:32] = cos; cos2[32:64] = cos; sin2[0:32] = sin; sin2[32:64] = sin
    r = np.arange(NC_)
    mf = np.broadcast_to((r == core).astype(np.float32)[None, :], (128, NC_))
    mb = np.broadcast_to((r > core).astype(np.float32)[None, :], (128, NC_))
    rot = np.zeros((128, 64), np.float32)
    for i in range(32):
        rot[i + 32, i] = -1.0
        rot[i, i + 32] = 1.0
    cbf = np.concatenate([np.eye(128, dtype=np.float32), ones, rot], axis=1).astype(ml_dtypes.bfloat16)
    return [ones, M_le, M_gt, M_ge, M_lt, cos2, sin2, mf, 1.0 - mf, mb, 1.0 - mb], cbf


def _run(c, inputs, dbg=False, stop=None, trace=False):
    T, NM, NC_ = c.T, c.NM, c.NCORES
    f = lambda a: np.ascontiguousarray(np.asarray(a, dtype=np.float32))
    x = f(inputs["x"])[0]
    gT = lambda v: f(v).reshape(-1, 128).T
    shared = {
        "meta": f(inputs["meta_tokens"]),
        "w_in": f(inputs["w_in"])[0],
        "wfa": np.concatenate([f(inputs["gla_wf"])[0], f(inputs["gla_bf"])], axis=0),
        "wba": np.concatenate([f(inputs["gla_wb"])[0], f(inputs["gla_bb"])], axis=0),
        "w_uq": f(inputs["w_uq"])[0], "w_ukv": f(inputs["w_ukv"])[0],
        "w_gla_out": f(inputs["w_gla_out"])[0], "w_mla_out": f(inputs["w_mla_out"])[0],
        "w_o": f(inputs["w_o"])[0], "w_ff1": f(inputs["w_ff1"])[0], "w_ff2": f(inputs["w_ff2"])[0],
        "glan": np.ascontiguousarray(np.broadcast_to(np.tile(f(inputs["gla_norm"])[0], c.GH)[None, :], (128, c.VW))),
        "finn": np.ascontiguousarray(np.broadcast_to(f(inputs["final_norm"])[None, :], (128, c.D))),
        "x_full": x,
    }
    posA = np.arange(NM + NC_ * T, dtype=np.float32)
    inv_freq = (np.float32(10000.0) ** (-np.arange(0, 64, 2, dtype=np.float32) / np.float32(64))).astype(np.float32)
    angA = (posA[:, None] * inv_freq[None, :]).astype(np.float32)
    cA = np.cos(angA).astype(np.float32).T
    sA = np.sin(angA).astype(np.float32).T
    shared["cosA"] = np.ascontiguousarray(np.concatenate([cA, cA], axis=0))
    shared["sinA"] = np.ascontiguousarray(np.concatenate([sA, sA], axis=0))
    tails = [gT(inputs["ln1"][0]), gT(inputs["ln2"][0]), gT(inputs["q_norm"][0]), gT(inputs["kv_norm"][0])]
    in_maps = []
    for core in range(NC_):
        parts, cbf = _host_consts(c, core)
        m = dict(shared)
        m["x"] = np.ascontiguousarray(x[core * T:(core + 1) * T])
        m["cf32"] = np.ascontiguousarray(np.concatenate(parts + tails, axis=1).astype(np.float32))
        m["cbf"] = np.ascontiguousarray(cbf)
        in_maps.append(m)
    nc = build_program(c, dbg, stop)
    res = run_bass_kernel_spmd(nc, in_maps, core_ids=list(range(NC_)), **({'trace': True} if trace else {}))
    outs = [np.asarray(res.results[i]["out"], dtype=np.float32) for i in range(NC_)]
    return np.concatenate(outs, axis=0)[None], res


def kernel(**inputs):
    c = Cfg()
    out, _ = _run(c, inputs)
    return out
```

```python
import numpy as np
import ml_dtypes
from contextlib import ExitStack
import concourse.bass as bass
import concourse.mybir as mybir
from concourse.bass_utils import run_bass_kernel_spmd

F32 = mybir.dt.float32
BF16 = mybir.dt.bfloat16
AF = mybir.ActivationFunctionType
ALU = mybir.AluOpType
AX = mybir.AxisListType

ENGS = ['pe', 'act', 'dve', 'pool', 'sp']
NDMA_SEM = 8
import os as _osx
POOL_COMPUTE = _osx.environ.get('POOLC', 'dve')
PROFILE = bool(_osx.environ.get('KPROFILE'))
EPS = 1e-6


class Cfg:
    def __init__(s, D=4096, SEQ=8192, NCORES=8, NM=16, GH=8, GR=16, MH=16, QR=1024, KVR=512, DFF=16384):
        s.D, s.SEQ, s.NCORES, s.NM, s.GH, s.GR, s.MH, s.QR, s.KVR, s.DFF = D, SEQ, NCORES, NM, GH, GR, MH, QR, KVR, DFF
        s.DK, s.DV, s.NOPE, s.ROPE, s.VD = 128, 256, 128, 64, 128
        s.T = SEQ // NCORES
        s.QKW = GH * s.DK
        s.VW = GH * s.DV
        s.MVW = MH * s.VD
        names = ['gq', 'gk', 'gv', 'gog', 'glrf', 'glrb', 'cq', 'ckv', 'krope', 'ga', 'gb']
        sizes = [s.QKW, s.QKW, s.VW, s.VW, GR, GR, QR, KVR, s.ROPE, D, D]
        s.off = {}
        o = 0
        for n, z in zip(names, sizes):
            s.off[n] = o
            o += z
        s.INW = o


class _Op:
    __slots__ = ('eng', 'fn', 'dma', 'deps', 'flag', 'seq', 'dslot', 'dval', 'idx', 'bar', 'cc', 'stage')


class Prog:
    def __init__(self, nc, st):
        self.nc = nc
        self.st = st
        self.all = []
        self.per = {e: [] for e in ENGS}
        self.last_w = {}
        self.readers = {}
        self.dma_n = {e: 0 for e in ENGS}
        self.sync_same = {'act': True, 'dve': True, 'pool': True, 'pe': False, 'sp': False}
        self.barriers = []
        self.dma_since_bar = []

    def sbuf(self, name, shape, dt):
        return self.st.enter_context(self.nc.sbuf_tensor(name, list(shape), dt))

    def psum(self, name, shape, dt):
        return self.st.enter_context(self.nc.psum_tensor(name, list(shape), dt))

    def op(self, eng, fn, reads=(), writes=(), dma=False, cc=False, nobar=False):
        if eng == 'pool' and not dma and not cc:
            eng = POOL_COMPUTE
        o = _Op()
        o.eng = eng; o.fn = fn; o.dma = dma or cc; o.flag = dma or cc; o.seq = None
        o.cc = cc
        o.stage = getattr(self, 'stage', 'S')
        o.idx = len(self.all)
        o.bar = 0 if nobar else len(self.barriers)
        psr_ = [k for k in reads if isinstance(k, tuple) and k[0] == 'ps']
        if psr_:
            reads = [k for k in reads if k not in psr_]
            writes = list(writes) + [k for k in psr_ if k not in writes]
        deps = set()
        for k in reads:
            w = self.last_w.get(k)
            if w is not None:
                deps.add(w)
        for k in writes:
            w = self.last_w.get(k)
            if w is not None:
                deps.add(w)
            for r in self.readers.get(k, ()):
                deps.add(r)
        o.deps = deps
        if cc:
            self.ccs = getattr(self, 'ccs', [])
            o.dslot = ('cc', len(self.ccs))
            o.dval = 1
            self.ccs.append(o.idx)
            self.dma_since_bar.append(o.idx)
        elif dma:
            n = self.dma_n[eng]
            o.dslot = n % NDMA_SEM
            o.dval = 16 * (n // NDMA_SEM + 1)
            self.dma_n[eng] = n + 1
            self.dma_since_bar.append(o.idx)
        for k in writes:
            self.last_w[k] = o.idx
            self.readers[k] = []
        for k in reads:
            self.readers.setdefault(k, []).append(o.idx)
        self.all.append(o)
        self.per[eng].append(o)
        if getattr(self, 'max_ops', None) and len(self.all) >= self.max_ops:
            raise _Stop()
        return o

    def barrier(self):
        deps = list(self.dma_since_bar)
        for e in ENGS:
            for o in reversed(self.per[e]):
                if not o.dma:
                    deps.append(o.idx)
                    break
        self.barriers.append(deps)
        self.dma_since_bar = []
        if getattr(self, 'stop_at', None) is not None and len(self.barriers) == self.stop_at:
            raise _Stop()

    def dma(self, eng, out, in_, reads=(), writes=(), nobar=False, **kw):
        return self.op(eng, lambda e: e.dma_start(out=out, in_=in_, **kw), reads, writes, dma=True, nobar=nobar)

    def emit(self):
        nc = self.nc
        ops = self.all
        for o in ops:
            for d in o.deps:
                od = ops[d]
                if od.dma:
                    continue
                if od.eng != o.eng or o.dma or self.sync_same[o.eng]:
                    od.flag = True
        for bl in self.barriers:
            for d in bl:
                if not ops[d].dma:
                    ops[d].flag = True
        for e in ENGS:
            c = 0
            for o in self.per[e]:
                if not o.dma and o.flag:
                    c += 1
                    o.seq = c
        csem = {e: self.st.enter_context(nc.semaphore("c_" + e)) for e in ENGS}
        dsem = {e: [self.st.enter_context(nc.semaphore("d_%s%d" % (e, i))) for i in range(NDMA_SEM)]
                for e in ENGS if self.dma_n[e] > 0}
        ccsem = [self.st.enter_context(nc.semaphore("cc%d" % i)) for i in range(len(getattr(self, 'ccs', [])))]
        final_waits = [(sm, 1) for sm in ccsem]
        for e in dsem:
            n = self.dma_n[e]
            for s in range(min(n, NDMA_SEM)):
                cnt = (n - 1 - s) // NDMA_SEM + 1
                final_waits.append((dsem[e][s], 16 * cnt))
        self.n_waits = 0

        def make(e):
            def f(engobj):
                waited = {}
                nbar = [0]

                def wait(sem, val, key):
                    if waited.get(key, 0) >= val:
                        return
                    waited[key] = val
                    engobj.wait_ge(sem, val)
                    self.n_waits += 1

                def wait_op(od, o):
                    if od.cc:
                        wait(ccsem[od.dslot[1]], 1, ('cc', od.dslot[1]))
                    elif od.dma:
                        wait(dsem[od.eng][od.dslot], od.dval, ('d', od.eng, od.dslot))
                    else:
                        if od.eng == e and o is not None and not (o.dma or self.sync_same[e]):
                            return
                        wait(csem[od.eng], od.seq, ('c', od.eng))

                for o in self.per[e]:
                    while nbar[0] < o.bar:
                        mx = {}
                        for d in self.barriers[nbar[0]]:
                            od = ops[d]
                            if od.cc:
                                mx[('cc', od.dslot[1])] = (ccsem[od.dslot[1]], 1)
                            elif od.dma:
                                key = ('d', od.eng, od.dslot)
                                if mx.get(key, (None, 0))[1] < od.dval:
                                    mx[key] = (dsem[od.eng][od.dslot], od.dval)
                            elif od.eng != e or e != 'pe':
                                key = ('c', od.eng)
                                if mx.get(key, (None, 0))[1] < od.seq:
                                    mx[key] = (csem[od.eng], od.seq)
                        for key, (sem, val) in mx.items():
                            wait(sem, val, key)
                        nbar[0] += 1
                    for d in sorted(o.deps):
                        wait_op(ops[d], o)
                    if o.dma and not o.cc and o.dval > 16:
                        wait(dsem[e][o.dslot], o.dval - 16, ('d', e, o.dslot))
                    if PROFILE:
                        with nc.named_scope(o.stage):
                            ins = o.fn(engobj)
                    else:
                        ins = o.fn(engobj)
                    if o.cc:
                        ins.then_inc(ccsem[o.dslot[1]])
                    elif o.dma:
                        ins.then_inc(dsem[e][o.dslot], 16)
                    elif o.flag:
                        ins.then_inc(csem[e], 1)
                if e == 'sp':
                    for (sem, val) in final_waits:
                        engobj.wait_ge(sem, val)
            return f

        with nc.Block() as block:
            block.sync(make('sp'))
            block.tensor(make('pe'))
            block.scalar(make('act'))
            block.vector(make('dve'))
            block.gpsimd(make('pool'))


class Arena:
    def __init__(self, P, nbytes):
        self.t = P.sbuf("arena", [128, nbytes // 2], BF16)
        self.n = nbytes
        self.top = 0
        self.peak = 0

    def alloc(self, shape, dt):
        ne = int(np.prod(shape))
        nb = ne * (4 if dt == F32 else 2)
        off = self.top
        self.top += (nb + 63) // 64 * 64
        assert self.top <= self.n, ("SBUF arena overflow", self.top, self.n)
        self.peak = max(self.peak, self.top)
        v = self.t[:, off // 2: off // 2 + nb // 2]
        if dt == F32:
            v = v.bitcast(F32)
        if len(shape) == 2:
            v = v.rearrange("p (a b) -> p a b", b=shape[1])
        elif len(shape) == 3:
            v = v.rearrange("p (a b c) -> p a b c", b=shape[1], c=shape[2])
        return v

    def mark(self):
        return self.top

    def reset(self, m):
        self.top = m


class _Stop(Exception):
    pass


def build_program(c, dbg=False, stop=None):
    nc = bass.Bass("TRN2", target_bir_lowering=False)
    D, T, NM, NC_ = c.D, c.T, c.NM, c.NCORES
    TT = T + NM
    KC = D // 128
    NT = T // 128
    GH, MH, QR, KVR, DFF = c.GH, c.MH, c.QR, c.KVR, c.DFF
    QKW, VW, MVW = c.QKW, c.VW, c.MVW
    QC, KVC, FC = QR // 128, KVR // 128, DFF // 128
    TS = [(h * 512, min(512, T - h * 512)) for h in range((T + 511) // 512)]

    def din(name, shape, dt=F32):
        return nc.dram_tensor(name, list(shape), dt, kind="ExternalInput").ap()

    def dscr(name, shape, dt):
        return nc.dram_tensor(name, list(shape), dt).ap()

    x = din("x", [T, D]); meta = din("meta", [NM, D])
    x_full = din("x_full", [NC_ * T, D])
    cosA = din("cosA", [64, NC_ * T + NM]); sinA = din("sinA", [64, NC_ * T + NM])
    w_in = din("w_in", [D, c.INW])
    wfa = din("wfa", [c.GR + 1, QKW]); wba = din("wba", [c.GR + 1, QKW])
    w_uq = din("w_uq", [QR, MH * 192]); w_ukv = din("w_ukv", [KVR, MH * 256])
    w_gla_out = din("w_gla_out", [VW, D]); w_mla_out = din("w_mla_out", [MVW, D])
    w_o = din("w_o", [D, D]); w_ff1 = din("w_ff1", [D, DFF]); w_ff2 = din("w_ff2", [DFF, D])
    NF = 128 * 5 + 2 * TT + 4 * NC_ + 2 * KC + QC + KVC
    cf_in = din("cf32", [128, NF])
    cb_in = din("cbf", [128, 320], BF16)
    glan_in = din("glan", [128, VW])
    fin_in = din("finn", [128, D])
    out = nc.dram_tensor("out", [T, D], F32, kind="ExternalOutput").ap()

    R1 = MH * 128 + 64 + MH * 128
    recv1 = nc.dram_tensor("recv1", [NC_ * R1, T], BF16)
    metaK = dscr("metaK", [MH * 128 + 64, NM], BF16)
    metaV = dscr("metaV", [NM, MVW], BF16)
    C2 = 2 * GH * 257
    GdA = [dscr("GdA%d" % i, [c.GR + 1, NC_ * T + NM], BF16) for i in range(2)]
    kall = dscr("kall", [NC_ * T + NM, QKW], BF16); vall = dscr("vall", [NC_ * T + NM, VW], BF16)
    QnT = dscr("QnT", [MH * 128, T], BF16); QpT = dscr("QpT", [MH * 64, T], BF16)
    q_s = dscr("q_s", [T, QKW], BF16); k_s = dscr("k_s", [TT, QKW], BF16); v_s = dscr("v_s", [TT, VW], BF16)
    gg_s = dscr("gg_s", [T, VW], BF16)
    sga = dscr("sga", [D, T], BF16); sgb = dscr("sgb", [D, T], BF16)
    o_f = dscr("o_f", [T, VW], F32); o_b = dscr("o_b", [T, VW], F32)
    qbf = dscr("qbf", [QKW, T], BF16); qbb = dscr("qbb", [QKW, T], BF16)
    OT = dscr("OT", [MVW, T], BF16)
    h1 = dscr("h1", [T, D], F32)
    uT = dscr("uT", [DFF, T], BF16)

    with ExitStack() as st:
        P = Prog(nc, st)
        A = Arena(P, 211968)
        P.stop_at = stop if isinstance(stop, int) else None
        import os as _os2
        P.max_ops = int(_os2.environ.get('MAXOPS', '0'))

        def chk(name):
            if stop == name:
                raise _Stop()
        try:
            PS = P.psum("ps", [128, 4096], F32)

            def bank(b, n=512):
                return PS[:, 512 * b: 512 * b + n]

            def bankbf(b):
                return PS[:, 512 * b: 512 * b + 512].bitcast(BF16)

            psr = [0]
            ukc = [0]

            def uk():
                ukc[0] += 1
                return ('uk', ukc[0])

            def nextbank(n=1):
                b = psr[0]
                psr[0] = (psr[0] + n) % 8
                return b

            def mm(out_, lhsT, rhs, start, stop, reads, writes):
                P.op('pe', lambda e: e.matmul(out_, lhsT=lhsT, rhs=rhs, start=start, stop=stop), reads, writes)

            def tr(out_, in_, ident_, reads, writes):
                P.op('pe', lambda e: e.transpose(out_, in_, ident_), reads, writes)

            def act(out_, in_, func, reads, writes, scale=1.0, bias=0.0, accum_out=None, eng='act'):
                if accum_out is None:
                    P.op('act', lambda e: e.activation(out=out_, in_=in_, func=func, bias=bias, scale=scale), reads, writes)
                else:
                    P.op('act', lambda e: e.activation(out=out_, in_=in_, func=func, bias=bias, scale=scale,
                                                       accum_out=accum_out), reads, writes)

            def amul(out_, in_, mulv, reads, writes):
                P.op('act', lambda e: e.mul(out=out_, in_=in_, mul=mulv), reads, writes)

            def tt(eng, out_, in0, in1, op, reads, writes):
                P.op(eng, lambda e: e.tensor_tensor(out=out_, in0=in0, in1=in1, op=op), reads, writes)

            def ts(eng, out_, in0, s1, s2, op0, op1, reads, writes):
                if op1 is None:
                    P.op(eng, lambda e: e.tensor_scalar(out=out_, in0=in0, scalar1=s1, scalar2=None, op0=op0), reads, writes)
                else:
                    P.op(eng, lambda e: e.tensor_scalar(out=out_, in0=in0, scalar1=s1, scalar2=s2, op0=op0, op1=op1), reads, writes)

            def stt(eng, out_, in0, scalar, in1, op0, op1, reads, writes):
                P.op(eng, lambda e: e.scalar_tensor_tensor(out=out_, in0=in0, scalar=scalar, in1=in1, op0=op0, op1=op1),
                     reads, writes)

            def cp(eng, out_, in_, reads, writes):
                if eng == 'act':
                    P.op('act', lambda e: e.copy(out=out_, in_=in_), reads, writes)
                else:
                    P.op(eng, lambda e: e.tensor_copy(out=out_, in_=in_), reads, writes)

            def memset(eng, ap, val, writes):
                P.op(eng, lambda e: e.memset(ap, val), (), writes)

            def rstd_from_ss(eng, dst, ss, n, reads, writes):
                ts(eng, dst, ss, 1.0 / n, EPS, ALU.mult, ALU.add, reads, writes)
                P.op('act', lambda e: e.sqrt(out=dst, in_=dst), writes, writes)
                P.op('dve', lambda e: e.reciprocal(out=dst, in_=dst), writes, writes)

            cf = A.alloc([NF], F32)
            cb = A.alloc([320], BF16)
            P.dma('sp', cf, cf_in, writes=['cf'])
            P.dma('sp', cb, cb_in, writes=['cb'])
            o_ = [0]

            def take(n):
                v = cf[:, o_[0]: o_[0] + n]
                o_[0] += n
                return v
            ones_f = take(128); M_le = take(128); M_gt = take(128); M_ge = take(128); M_lt = take(128)
            cos2 = take(TT); sin2 = take(TT)
            oh = take(NC_); _u1 = take(NC_); _u2 = take(NC_); _u3 = take(NC_)
            ln1T = take(KC); ln2T = take(KC); qnT = take(QC); kvnT = take(KVC)
            ident = cb[:, 0:128]; ones_b = cb[:, 128:256]; rot = cb[:, 256:320]
            CK = ['cf', 'cb']
            GR = c.GR
            Gd = [A.alloc([TT], BF16) for _ in range(2)]
            wga = [A.alloc([QKW], BF16) for _ in range(2)]
            for di in range(2):
                P.dma('pool', wga[di][:GR + 1, :], (wfa, wba)[di], writes=[('wga', di)])
            Sst = [A.alloc([GH, 256], BF16) for _ in range(2)]
            pers = A.mark()

            class WStream:
                def __init__(self, kcmax, exempt=False):
                    self.buf = [A.alloc([kcmax, 512], BF16) for _ in range(2)]
                    self.n = 0
                    self.exempt = exempt

                def load(self, Wap, c0, nb, kc):
                    s = self.n % 2
                    self.n += 1
                    Wv = Wap.rearrange("(c p) n -> p c n", p=128)
                    g = 8
                    for q in range(0, kc, g):
                        qe = min(kc, q + g)
                        P.dma('pool', self.buf[s][:, q:qe, 0:nb], Wv[:, q:qe, c0:c0 + nb], writes=[('W', s, q // g)], nobar=self.exempt)
                    return s

                def rkeys(self, s, k):
                    return [('W', s, k // 8)]

            def linear_a(ws, Wap, kc, blocks, actT, akeys, segs, epi):
                nxt = ws.load(Wap, blocks[0][0], blocks[0][1], kc)
                for bi, (c0, nb) in enumerate(blocks):
                    s = nxt
                    if bi + 1 < len(blocks):
                        nxt = ws.load(Wap, blocks[bi + 1][0], blocks[bi + 1][1], kc)
                    for j0 in range(0, nb, 128):
                        m = min(128, nb - j0)
                        b0 = nextbank(len(segs))
                        for si, (t0, n) in enumerate(segs):
                            b = (b0 + si) % 8
                            for k in range(kc):
                                mm(bank(b)[:m, :n], ws.buf[s][:, k, j0:j0 + m], actT[:, k, t0:t0 + n], k == 0, k == kc - 1,
                                   ws.rkeys(s, k) + akeys, [('ps', b)])
                            epi(c0 + j0, m, si, b, bank(b)[:m, :n])

            def linear_b(ws, Wap, kc, blocks, actT, akeys, tiles, epi):
                nxt = ws.load(Wap, blocks[0][0], blocks[0][1], kc)
                for bi, (c0, nb) in enumerate(blocks):
                    s = nxt
                    if bi + 1 < len(blocks):
                        nxt = ws.load(Wap, blocks[bi + 1][0], blocks[bi + 1][1], kc)
                    for ti, (t0, np_) in enumerate(tiles):
                        b = nextbank()
                        for k in range(kc):
                            mm(bank(b)[:np_, :nb], actT[:, k, t0:t0 + np_], ws.buf[s][:, k, 0:nb], k == 0, k == kc - 1,
                               ws.rkeys(s, k) + akeys, [('ps', b)])
                        epi(c0, nb, ti, b, bank(b)[:np_, :nb])

            def blocks_of(c0, n, step=512):
                return [(c0 + i, min(step, n - i)) for i in range(0, n, step)]

            def stage_norm(srcs, gT, dstT):
                m0 = A.mark()
                xin = [A.alloc([D], F32) for _ in range(2)]
                xs = [A.alloc([D], BF16) for _ in range(2)]
                stt_ = A.alloc([8], F32)
                for i, (src, np_, tok0) in enumerate(srcs):
                    b = i % 2
                    ss = stt_[:, 4 * b: 4 * b + 1]
                    rs = stt_[:, 4 * b + 1: 4 * b + 2]
                    P.dma('sp', xin[b][:np_], src, writes=[('xin', b)])
                    memset('dve', ss, 0.0, [('nst', b)])
                    act(xs[b][:np_], xin[b][:np_], AF.Square, [('xin', b), ('nst', b)], [('xs', b), ('nst', b)], accum_out=ss[:np_])
                    rstd_from_ss('dve', rs[:np_], ss[:np_], D, [('nst', b)], [('nrs', b)])
                    ts('dve', xs[b][:np_], xin[b][:np_], rs[:np_], None, ALU.mult, None, [('xin', b), ('nrs', b)], [('xs', b)])
                    for q0 in range(0, KC, 8):
                        pb = nextbank()
                        pv = bankbf(pb)
                        for kk in range(min(8, KC - q0)):
                            k = q0 + kk
                            tr(pv[:, 128 * kk: 128 * kk + np_], xs[b][:np_, 128 * k: 128 * k + 128], ident[:np_, :np_],
                               [('xs', b)] + CK, [('ps', pb)])
                        for kk in range(min(8, KC - q0)):
                            k = q0 + kk
                            src_ = pv[:, 128 * kk: 128 * kk + np_]
                            dst_ = dstT[:, k, tok0: tok0 + np_]
                            if kk % 2 == 0:
                                amul(dst_, src_, gT[:, k: k + 1], [('ps', pb)] + CK, [('nT', tok0)])
                            else:
                                ts('dve', dst_, src_, gT[:, k: k + 1], None, ALU.mult, None, [('ps', pb)] + CK, [('nT', tok0)])
                A.reset(m0)

            def make_gla(G, full=True):
                class _NS:
                    pass
                g = _NS()
                qk_in = [A.alloc([2, QKW], BF16) for _ in range(2)]
                v_in = [A.alloc([VW], BF16) for _ in range(2)]
                xg_e = A.alloc([QKW], F32)
                sp_t = A.alloc([QKW], F32)
                Rsum = A.alloc([QKW], F32)
                Ep = A.alloc([QKW], F32); Em = A.alloc([QKW], F32); Gk = A.alloc([QKW], F32); Pb = A.alloc([QKW], F32)
                qt_ = A.alloc([QKW], BF16); kt_ = A.alloc([QKW], BF16); kh_ = A.alloc([QKW], BF16); qb_ = A.alloc([QKW], BF16)
                qtT = A.alloc([GH, 128], BF16); ktT = A.alloc([GH, 128], BF16); qbT = A.alloc([GH, 128], BF16)
                attm = [A.alloc([128], BF16) for _ in range(2)]
                Sloc = A.alloc([GH, 256], F32)
                Sbf = A.alloc([GH, 256], BF16)
                a_h = A.alloc([GH], F32)
                o_st = [A.alloc([VW], F32) for _ in range(2)] if full else None
                ocnt = [0]

                alt = {'xg_e': (xg_e, A.alloc([QKW], F32)), 'sp_t': (sp_t, A.alloc([QKW], F32)), 'Gk': (Gk, A.alloc([QKW], F32)),
                       'kh_': (kh_, A.alloc([QKW], BF16)), 'a_h': (a_h, A.alloc([GH], F32))}
                g.par = 0

                def gla_gates(di, t0, np_, first, state_only=False):
                    Mi, Mc = ((M_le, M_gt), (M_ge, M_lt))[di]
                    gap_, gkeys_ = G(di, t0, np_)
                    pp = g.par if state_only else 0
                    xg_e = alt['xg_e'][pp]; sp_t = alt['sp_t'][pp]; Gk = alt['Gk'][pp]
                    sfx = 'B' if pp else ''
                    for h0 in range(0, QKW, 512):
                        nb = min(512, QKW - h0)
                        b = nextbank()
                        mm(bank(b)[:np_, :nb], gap_, wga[di][:GR + 1, h0:h0 + nb], True, True,
                           gkeys_ + [('wga', di)], [('ps', b)])
                        act(xg_e[:np_, h0:h0 + nb], bank(b)[:np_, :nb], AF.Exp, [('ps', b)], [('xg_e' + sfx)], scale=-1.0)
                    act(sp_t[:np_, :], xg_e[:np_, :], AF.Ln, [('xg_e' + sfx)], [('sp_t' + sfx)], bias=1.0)
                    for h0 in range(0, QKW, 512):
                        nb = min(512, QKW - h0)
                        b = nextbank()
                        if not state_only:
                            mm(bank(b)[:np_, :nb], Mi[:np_, :np_], sp_t[:np_, h0:h0 + nb], True, True, [('sp_t' + sfx)] + CK, [('ps', b)])
                            act(Ep[:np_, h0:h0 + nb], bank(b)[:np_, :nb], AF.Exp, [('ps', b)], ['Ep'], scale=-1.0 / 16)
                            act(Em[:np_, h0:h0 + nb], bank(b)[:np_, :nb], AF.Exp, [('ps', b)], ['Em'], scale=1.0 / 16)
                            b = nextbank()
                        mm(bank(b)[:np_, :nb], Mc[:np_, :np_], sp_t[:np_, h0:h0 + nb], True, True, [('sp_t' + sfx)] + CK, [('ps', b)])
                        act(Gk[:np_, h0:h0 + nb], bank(b)[:np_, :nb], AF.Exp, [('ps', b)], [('Gk' + sfx)], scale=-1.0 / 16)
                        if not first:
                            b = nextbank()
                            mm(bank(b)[:np_, :nb], ones_f[:, :np_], Rsum[:, h0:h0 + nb], True, True, ['Rsum'] + CK, [('ps', b)])
                            act(Pb[:np_, h0:h0 + nb], bank(b)[:np_, :nb], AF.Exp, [('ps', b)], ['Pb'], scale=-1.0 / 16)

                def gla_state_update(di, np_, kin, vin, kkeys, vkeys, bf=True):
                    pp = g.par if not bf else 0
                    sp_t = alt['sp_t'][pp]; Gk = alt['Gk'][pp]; kh_ = alt['kh_'][pp]; a_h = alt['a_h'][pp]
                    sfx = 'B' if pp else ''
                    tt('dve', kh_[:np_, :], kin, Gk[:np_, :], ALU.mult, kkeys + [('Gk' + sfx)], [('kh_' + sfx)])
                    ba = nextbank()
                    for h in range(GH):
                        mm(bank(ba)[:, h:h + 1], sp_t[:np_, 128 * h:128 * h + 128], ones_f[:np_, 0:1], True, True, [('sp_t' + sfx)] + CK, [('ps', ba)])
                    act(a_h, bank(ba)[:, :GH], AF.Exp, [('ps', ba)], [('a_h' + sfx)], scale=-1.0 / 16)
                    for h in range(GH):
                        b = nextbank()
                        mm(bank(b)[:, :256], kh_[:np_, 128 * h:128 * h + 128], vin[:np_, 256 * h:256 * h + 256], True, True,
                           [('kh_' + sfx)] + vkeys, [('ps', b)])
                        stt('dve', Sloc[:, h, :], Sloc[:, h, :], a_h[:, h:h + 1], bank(b)[:, :256], ALU.mult, ALU.add,
                            [('ps', b), ('a_h' + sfx), ('Sloc', h)], [('Sloc', h)])
                        if bf:
                            cp('act', Sbf[:, h, :], Sloc[:, h, :], [('Sloc', h)], [('Sbf', h)])

                for _n, _v in (('qk_in', qk_in), ('v_in', v_in), ('xg_e', xg_e), ('sp_t', sp_t), ('Rsum', Rsum), ('Ep', Ep), ('Em', Em), ('Gk', Gk), ('Pb', Pb), ('qt_', qt_), ('kt_', kt_), ('kh_', kh_), ('qb_', qb_), ('qtT', qtT), ('ktT', ktT), ('qbT', qbT), ('attm', attm), ('Sloc', Sloc), ('Sbf', Sbf), ('a_h', a_h), ('o_st', o_st), ('ocnt', ocnt)):
                    setattr(g, _n, _v)
                g.gates = gla_gates
                g.state_update = gla_state_update
                return g

            own_tiles = [(x[128 * i: 128 * i + 128, :], 128, 128 * i) for i in range(NT)]
            NTOK = NC_ * T
            r1 = recv1.ap()
            mB = A.mark()
            ws = WStream(KC, exempt=True)

            P.stage = 'glob'
            mG = A.mark()
            gstg = [A.alloc([512], BF16) for _ in range(2)]
            for i_ in range(2):
                memset('dve', gstg[i_][:GR + 1, :], 1.0, [('gstg', i_)])
            gsc = [0]
            xg = A.alloc([KC, TT], BF16)
            mG2 = A.mark()

            def stage_kv(r, with_meta):
                segsM = TS + ([(T, NM)] if with_meta else [])
                tilesM = [(128 * i, 128) for i in range(NT)] + ([(T, NM)] if with_meta else [])
                nsg = len(segsM)
                XK_ = [('nT', 128 * i) for i in range(NT)] + [('nT', T)]
                ckvg = A.alloc([KVC, TT], BF16)
                sqkv = A.alloc([KVC, TT], BF16)
                krp = A.alloc([TT], F32)
                krb = A.alloc([TT], BF16)
                kro = A.alloc([TT], BF16)
                csr = A.alloc([TT], F32)
                snr = A.alloc([TT], F32)
                tmp = A.alloc([512], F32)
                P.dma('sp', csr[:64, 0:T], cosA[:, NM + r * T: NM + (r + 1) * T], writes=['csr'])
                P.dma('sp', snr[:64, 0:T], sinA[:, NM + r * T: NM + (r + 1) * T], writes=['snr'])
                if with_meta:
                    P.dma('sp', csr[:64, T:TT], cosA[:, 0:NM], writes=['csr2'])
                    P.dma('sp', snr[:64, T:TT], sinA[:, 0:NM], writes=['snr2'])
                CS = ['csr', 'snr', 'csr2', 'snr2']

                def epi_ckv(cc, m, si, b, ps):
                    t0, n = segsM[si]
                    if cc < KVR:
                        k = cc // 128
                        amul(ckvg[:, k, t0:t0 + n], ps, kvnT[:, k:k + 1], [('ps', b)] + CK, [('ckvg', si)])
                        act(sqkv[:, k, t0:t0 + n], ps, AF.Square, [('ps', b)], [('sqkv', si)])
                    else:
                        cp('act', krp[:64, t0:t0 + n], ps, [('ps', b)], [('krp', si)])
                        cp('dve', krb[:64, t0:t0 + n], ps, [('ps', b)], [('krb', si)])
                linear_a(ws, w_in[:, c.off['ckv']: c.off['ckv'] + KVR + 64], KC,
                         ([(0, 512), (512, 64)] if KVR == 512 else blocks_of(0, KVR + 64)), xg, XK_, segsM, epi_ckv)
                rkv_bc = A.alloc([TT], F32)
                rkv_tm = A.alloc([NT + 1], F32)
                for si, (t0, n) in enumerate(segsM):
                    b = nextbank()
                    for k in range(KVC):
                        mm(bank(b)[:, :n], ones_b, sqkv[:, k, t0:t0 + n], k == 0, k == KVC - 1, [('sqkv', si)] + CK, [('ps', b)])
                    rstd_from_ss('dve', rkv_bc[:, t0:t0 + n], bank(b)[:, :n], KVR, [('ps', b)], [('rkv_bc', si)])
                bq = nextbank()
                for ti, (t0, np_) in enumerate(tilesM):
                    si = min(t0 // 512, len(TS) - 1) if t0 < T else len(TS)
                    for k in range(KVC):
                        mm(bank(bq)[:np_, ti:ti + 1], sqkv[:, k, t0:t0 + np_], ones_b[:, 0:1], k == 0, k == KVC - 1,
                           [('sqkv', si)] + CK, [('ps', bq)])
                rstd_from_ss('dve', rkv_tm[:, :NT], bank(bq)[:, :NT], KVR, [('ps', bq)], ['rkv_tm'])
                if with_meta:
                    rstd_from_ss('dve', rkv_tm[:NM, NT:NT + 1], bank(bq)[:NM, NT:NT + 1], KVR, [('ps', bq)], ['rkv_tm2'])
                for si, (t0, n) in enumerate(segsM):
                    b = nextbank()
                    mm(bank(b)[:64, :n], rot[:64, :], krb[:64, t0:t0 + n], True, True, [('krb', si)] + CK, [('ps', b)])
                    tt('dve', tmp[:64, :n], bank(b)[:64, :n], snr[:64, t0:t0 + n], ALU.mult, [('ps', b)] + CS, ['rtmp'])
                    tt('pool', krp[:64, t0:t0 + n], krp[:64, t0:t0 + n], csr[:64, t0:t0 + n], ALU.mult, [('krp', si)] + CS, [('krp', si)])
                    tt('dve', kro[:64, t0:t0 + n], tmp[:64, :n], krp[:64, t0:t0 + n], ALU.add, ['rtmp', ('krp', si)], [('kro', si), 'rtmp'])
                for si, (t0, n) in enumerate(TS):
                    P.dma('sp', r1[r * R1 + MH * 128: r * R1 + MH * 128 + 64, t0:t0 + n], kro[:64, t0:t0 + n], reads=[('kro', si)], writes=[uk()])
                if with_meta:
                    P.dma('sp', metaK[MH * 128: MH * 128 + 64, :], kro[:64, T:TT], reads=[('kro', len(TS))], writes=[uk()])
                kst = [A.alloc([TT], BF16) for _ in range(2)]
                kcnt = [0]

                def epi_kn(cc, m, si, b, ps):
                    t0, n = segsM[si]
                    h = cc // 256
                    sb = kcnt[0] % 2
                    tt('dve', kst[sb][:, t0:t0 + n], ps, rkv_bc[:, t0:t0 + n], ALU.mult, [('ps', b), ('rkv_bc', si)], [('kst', sb, si)])
                    if si < len(TS):
                        P.dma('sp', r1[r * R1 + h * 128: r * R1 + h * 128 + 128, t0:t0 + n], kst[sb][:, t0:t0 + n], reads=[('kst', sb, si)], writes=[uk()])
                    else:
                        P.dma('sp', metaK[h * 128: h * 128 + 128, :], kst[sb][:, T:TT], reads=[('kst', sb, si)], writes=[uk()])
                    if si == nsg - 1:
                        kcnt[0] += 1
                linear_a(ws, w_ukv, KVC, [(256 * h, 128) for h in range(MH)], ckvg, [('ckvg', i) for i in range(nsg)], segsM, epi_kn)
                vst = [A.alloc([512], BF16) for _ in range(2)]
                vcnt = [0]
                w_ukv_v = w_ukv.rearrange("k (h two d) -> k h two d", two=2, d=128)
                sendV = r1[r * R1 + MH * 128 + 64: (r + 1) * R1, :].rearrange("(h p) (kt d) -> p h kt d", p=128, d=128)
                for h0 in range(0, MH, 4):
                    nh = min(4, MH - h0)
                    s = ws.n % 2
                    ws.n += 1
                    for hh in range(nh):
                        P.dma('pool', ws.buf[s][:, 0:KVC, 128 * hh: 128 * hh + 128],
                              w_ukv_v[:, h0 + hh, 1, :].rearrange("(c p) d -> p c d", p=128), writes=[('W', s, 0)], nobar=True)
                    for ti, (t0, np_) in enumerate(tilesM):
                        b = nextbank()
                        for k in range(KVC):
                            mm(bank(b)[:np_, :128 * nh], ckvg[:, k, t0:t0 + np_], ws.buf[s][:, k, 0:128 * nh], k == 0, k == KVC - 1,
                               [('W', s, 0)] + [('ckvg', i) for i in range(nsg)], [('ps', b)])
                        sb = vcnt[0] % 2
                        vcnt[0] += 1
                        ts('dve', vst[sb][:np_, :128 * nh], bank(b)[:np_, :128 * nh], rkv_tm[:np_, ti:ti + 1], None, ALU.mult, None,
                           [('ps', b), 'rkv_tm', 'rkv_tm2'], [('vst', sb)])
                        if t0 < T:
                            P.dma('sp', sendV[:, h0:h0 + nh, ti, :], vst[sb][:, :128 * nh].rearrange("p (h d) -> p h d", d=128),
                                  reads=[('vst', sb)], writes=[uk()])
                        else:
                            P.dma('sp', metaV[:, 128 * h0: 128 * (h0 + nh)], vst[sb][:NM, :128 * nh], reads=[('vst', sb)], writes=[uk()])

            def stage_kvg(r, with_meta):
                tiles_ = [(128 * i, 128) for i in range(NT)] + ([(T, NM)] if with_meta else [])
                segs_ = TS + ([(T, NM)] if with_meta else [])
                XK_ = [('nT', 128 * i) for i in range(NT)] + [('nT', T)]
                pst_ = [A.alloc([512], BF16) for _ in range(2)]
                pc_ = [0]

                def gcol(t0):
                    return r * T + t0 if t0 < T else NTOK

                def epi_kvg(c0, nb, ti, b, ps):
                    sb = pc_[0] % 2
                    pc_[0] += 1
                    t0, np_ = tiles_[ti]
                    g0 = gcol(t0)
                    cp('act' if pc_[0] % 2 else 'dve', pst_[sb][:np_, :nb], ps, [('ps', b)], [('pstg', sb)])
                    if c0 < QKW:
                        P.dma('sp', kall[g0:g0 + np_, c0:c0 + nb], pst_[sb][:np_, :nb], reads=[('pstg', sb)], writes=[uk()])
                    else:
                        P.dma('sp', vall[g0:g0 + np_, c0 - QKW:c0 - QKW + nb], pst_[sb][:np_, :nb], reads=[('pstg', sb)], writes=[uk()])
                linear_b(ws, w_in[:, QKW: 2 * QKW + VW], KC, blocks_of(0, QKW) + blocks_of(QKW, VW), xg, XK_, tiles_, epi_kvg)
                for di, nm_ in enumerate(('glrf', 'glrb')):
                    def epi_gg(cc, m, si, b, ps, di=di):
                        t0, n = segs_[si]
                        g0 = gcol(t0)
                        sb = gsc[0] % 2
                        gsc[0] += 1
                        cp('act', gstg[sb][:GR, :n], ps, [('ps', b)], [('gstg', sb)])
                        P.dma('sp', GdA[di][:, g0:g0 + n], gstg[sb][:GR + 1, :n], reads=[('gstg', sb)], writes=[uk()])
                    linear_a(ws, w_in[:, c.off[nm_]: c.off[nm_] + GR], KC, [(0, GR)], xg, XK_, segs_, epi_gg)

            for r in range(NC_):
                wm = (r == 0)
                stage_norm([(x_full[r * T + 128 * i: r * T + 128 * i + 128, :], 128, 128 * i) for i in range(NT)]
                           + ([(meta[:, :], NM, T)] if wm else []), ln1T, xg)
                P.barrier()
                P.stage = 'glob_kv'
                stage_kv(r, wm)
                P.stage = 'glob_kvg'
                stage_kvg(r, wm)
                P.stage = 'glob_norm'
                A.reset(mG2)
                P.barrier()

            P.stage = 'gstate'
            A.reset(mG)
            gtl = [A.alloc([128], BF16) for _ in range(2)]
            gtc = [0]

            def G_glob(di, t0, np_):
                ib = gtc[0] % 2
                gtc[0] += 1
                P.dma('sp', gtl[ib][:GR + 1, :np_], GdA[di][:, t0:t0 + np_], writes=[('gtl', ib)])
                return gtl[ib][:GR + 1, :np_], [('gtl', ib)]
            gl_ = make_gla(G_glob, full=False)
            Sacc = A.alloc([2, GH, 256], F32)
            memset('dve', Sacc, 0.0, ['Sacc'])
            kv_ld = [0]

            def load_kv(g0, np_):
                ib = kv_ld[0] % 2
                kv_ld[0] += 1
                P.dma('sp', gl_.qk_in[ib][:np_, 1, :], kall[g0:g0 + np_, :], writes=[('qk_in', ib, 1)])
                P.dma('sp', gl_.v_in[ib][:np_, :], vall[g0:g0 + np_, :], writes=[('v_in', ib)])
                return ib
            for di in range(2):
                memset('dve', gl_.Sloc, 0.0, [('Sloc', h) for h in range(GH)])
                if di == 0:
                    ib = load_kv(NTOK, NM)
                    gl_.gates(0, NTOK, NM, True, state_only=True)
                    gl_.state_update(0, NM, gl_.qk_in[ib][:NM, 1, :], gl_.v_in[ib], [('qk_in', ib, 1)], [('v_in', ib)], bf=False)
                for r in (range(NC_) if di == 0 else range(NC_ - 1, -1, -1)):
                    stt('dve', Sacc[:, di].rearrange("p h c -> p (h c)"), gl_.Sloc.rearrange("p h c -> p (h c)"), oh[:, r:r + 1],
                        Sacc[:, di].rearrange("p h c -> p (h c)"), ALU.mult, ALU.add, [('Sloc', h) for h in range(GH)] + ['Sacc'] + CK, ['Sacc'])
                    if (di == 0 and r == NC_ - 1) or (di == 1 and r == 0):
                        break
                    for i in (range(NT) if di == 0 else range(NT - 1, -1, -1)):
                        g0 = r * T + 128 * i
                        ib = load_kv(g0, 128)
                        gl_.par ^= 1
                        gl_.gates(di, g0, 128, True, state_only=True)
                        gl_.state_update(di, 128, gl_.qk_in[ib][:, 1, :], gl_.v_in[ib], [('qk_in', ib, 1)], [('v_in', ib)], bf=False)
            for di in range(2):
                cp('dve', Sst[di], Sacc[:, di], ['Sacc'], [('Sst', di)])
            P.barrier()
            A.reset(mG)

            P.stage = 'A'
            xnT = A.alloc([KC, TT], BF16)
            stage_norm(own_tiles + [(meta[:, :], NM, T)], ln1T, xnT)
            XK = [('nT', 128 * i) for i in range(NT)] + [('nT', T)]
            segsM = TS + [(T, NM)]
            P.barrier()

            P.stage = 'B2'
            m1 = A.mark()
            cqg = A.alloc([QC, T], BF16)
            sqq = A.alloc([QC, T], BF16)

            def epi_cq(cc, m, si, b, ps):
                t0, n = TS[si]
                k = cc // 128
                amul(cqg[:, k, t0:t0 + n], ps, qnT[:, k:k + 1], [('ps', b)] + CK, [('cqg', si)])
                act(sqq[:, k, t0:t0 + n], ps, AF.Square, [('ps', b)], [('sqq', si)])
            linear_a(ws, w_in[:, c.off['cq']: c.off['cq'] + QR], KC, blocks_of(0, QR), xnT, XK, TS, epi_cq)
            rq_bc = A.alloc([T], F32)
            for si, (t0, n) in enumerate(TS):
                b = nextbank()
                for k in range(QC):
                    mm(bank(b)[:, :n], ones_b, sqq[:, k, t0:t0 + n], k == 0, k == QC - 1, [('sqq', si)] + CK, [('ps', b)])
                rstd_from_ss('dve', rq_bc[:, t0:t0 + n], bank(b)[:, :n], QR, [('ps', b)], [('rq_bc', si)])
            qst = [A.alloc([T], BF16) for _ in range(2)]
            qpf = [A.alloc([T], F32)]
            qpb = [A.alloc([T], BF16)]
            qtm = [A.alloc([T], F32)]
            qcnt = [0]
            SC = 192.0 ** -0.5

            def epi_q(cc, m, si, b, ps):
                t0, n = TS[si]
                h = cc // 192
                sb = qcnt[0] % 2
                CQ = [('cqg', i) for i in range(len(TS))]
                if m == 128:
                    stt('dve', qst[sb][:, t0:t0 + n], ps, SC, rq_bc[:, t0:t0 + n], ALU.mult, ALU.mult,
                        [('ps', b), ('rq_bc', si)], [('qst', sb, si)])
                    P.dma('sp', QnT[h * 128: h * 128 + 128, t0:t0 + n], qst[sb][:, t0:t0 + n], reads=[('qst', sb, si)], writes=[uk()])
                else:
                    stt('dve', qpf[0][:64, t0:t0 + n], ps, SC, rq_bc[:64, t0:t0 + n], ALU.mult, ALU.mult,
                        [('ps', b), ('rq_bc', si)], [('qpf', 0, si)])
                    cp('act', qpb[0][:64, t0:t0 + n], qpf[0][:64, t0:t0 + n], [('qpf', 0, si)], [('qpb', 0, si)])
                    b2 = nextbank()
                    mm(bank(b2)[:64, :n], rot[:64, :], qpb[0][:64, t0:t0 + n], True, True, [('qpb', 0, si)] + CK, [('ps', b2)])
                    tt('dve', qtm[0][:64, t0:t0 + n], bank(b2)[:64, :n], sin2[:64, t0:t0 + n], ALU.mult, [('ps', b2)] + CK, [('qtm', 0, si)])
                    tt('pool', qpf[0][:64, t0:t0 + n], qpf[0][:64, t0:t0 + n], cos2[:64, t0:t0 + n], ALU.mult,
                       [('qpf', 0, si)] + CK, [('qpf', 0, si)])
                    tt('dve', qpb[0][:64, t0:t0 + n], qtm[0][:64, t0:t0 + n], qpf[0][:64, t0:t0 + n], ALU.add,
                       [('qtm', 0, si), ('qpf', 0, si)], [('qpb', 0, si)])
                    P.dma('sp', QpT[h * 64: h * 64 + 64, t0:t0 + n], qpb[0][:64, t0:t0 + n], reads=[('qpb', 0, si)], writes=[uk()])
                    if si == len(TS) - 1:
                        qcnt[0] += 1
            qblocks = []
            for h in range(MH):
                qblocks += [(192 * h, 128), (192 * h + 128, 64)]
            linear_a(ws, w_uq, QC, qblocks, cqg, [('cqg', i) for i in range(len(TS))], TS, epi_q)
            A.reset(m1)
            P.barrier()

            P.stage = 'B4'
            m1 = A.mark()
            glan = A.alloc([VW], F32)
            P.dma('sp', glan, glan_in, writes=['glan'])
            tilesO = [(128 * i, 128) for i in range(NT)]
            gst = [A.alloc([512], F32) for _ in range(2)]
            gsb = [A.alloc([512], BF16) for _ in range(2)]
            gcnt = [0]

            def epi_gog(c0, nb, ti, b, ps):
                sb = gcnt[0] % 2
                gcnt[0] += 1
                act(gst[sb][:, :nb], ps, AF.Silu, [('ps', b)], [('gst', sb)])
                tt('dve', gsb[sb][:, :nb], gst[sb][:, :nb], glan[:, c0:c0 + nb], ALU.mult, [('gst', sb), 'glan'], [('gsb', sb)])
                P.dma('sp', gg_s[128 * ti: 128 * ti + 128, c0:c0 + nb], gsb[sb][:, :nb], reads=[('gsb', sb)], writes=[uk()])
            linear_b(ws, w_in[:, c.off['gog']: c.off['gog'] + VW], KC, blocks_of(0, VW), xnT, XK, tilesO, epi_gog)
            A.reset(m1)
            P.barrier()

            P.stage = 'B5'
            m1 = A.mark()
            sst = [A.alloc([T], BF16) for _ in range(2)]
            scnt = [0]
            for nm_, dst in (('ga', sga), ('gb', sgb)):
                def epi_sg(cc, m, si, b, ps, dst=dst):
                    t0, n = TS[si]
                    sb = scnt[0] % 2
                    act(sst[sb][:, t0:t0 + n], ps, AF.Sigmoid, [('ps', b)], [('sst', sb, si)])
                    P.dma('sp', dst[cc:cc + 128, t0:t0 + n], sst[sb][:, t0:t0 + n], reads=[('sst', sb, si)], writes=[uk()])
                    if si == len(TS) - 1:
                        scnt[0] += 1
                linear_a(ws, w_in[:, c.off[nm_]: c.off[nm_] + D], KC, blocks_of(0, D), xnT, XK, TS, epi_sg)
            A.reset(m1)
            P.barrier()

            P.stage = 'B3'
            m1 = A.mark()
            pst = [A.alloc([512], BF16) for _ in range(2)]
            pcnt = [0]
            QSC = 128.0 ** -0.5
            tilesAll = tilesO + [(T, NM)]

            def epi_qkv(c0, nb, ti, b, ps):
                sb = pcnt[0] % 2
                pcnt[0] += 1
                t0, np_ = tilesAll[ti]
                if c0 < QKW:
                    if t0 >= T:
                        return
                    amul(pst[sb][:np_, :nb], ps, QSC, [('ps', b)], [('pst', sb)])
                    P.dma('sp', q_s[t0:t0 + np_, c0:c0 + nb], pst[sb][:np_, :nb], reads=[('pst', sb)], writes=[uk()])
                elif c0 < 2 * QKW:
                    cp('act' if pcnt[0] % 2 else 'dve', pst[sb][:np_, :nb], ps, [('ps', b)], [('pst', sb)])
                    P.dma('sp', k_s[t0:t0 + np_, c0 - QKW:c0 - QKW + nb], pst[sb][:np_, :nb], reads=[('pst', sb)], writes=[uk()])
                else:
                    cp('act' if pcnt[0] % 2 else 'dve', pst[sb][:np_, :nb], ps, [('ps', b)], [('pst', sb)])
                    P.dma('sp', v_s[t0:t0 + np_, c0 - 2 * QKW:c0 - 2 * QKW + nb], pst[sb][:np_, :nb], reads=[('pst', sb)], writes=[uk()])
            linear_b(ws, w_in[:, 0: 2 * QKW + VW], KC, blocks_of(0, QKW) + blocks_of(QKW, QKW) + blocks_of(2 * QKW, VW), xnT, XK, tilesAll, epi_qkv)
            for di, nm_ in enumerate(('glrf', 'glrb')):
                memset('dve', Gd[di][:GR + 1, :], 1.0, [('Gd', di)])

                def epi_g(cc, m, si, b, ps, di=di):
                    t0, n = segsM[si]
                    cp('act', Gd[di][:GR, t0:t0 + n], ps, [('ps', b)], [('Gd', di)])
                linear_a(ws, w_in[:, c.off[nm_]: c.off[nm_] + GR], KC, [(0, GR)], xnT, XK, segsM, epi_g)
            P.barrier()
            A.reset(pers)

            P.stage = 'G1'
            m1 = A.mark()
            NTG = NT
            gl = make_gla(lambda di, t0, np_: (Gd[di][:GR + 1, t0:t0 + np_], [('Gd', di)]))
            qk_in = gl.qk_in; v_in = gl.v_in; xg_e = gl.xg_e; sp_t = gl.sp_t; Rsum = gl.Rsum; Ep = gl.Ep; Em = gl.Em; Gk = gl.Gk; Pb = gl.Pb; qt_ = gl.qt_; kt_ = gl.kt_; kh_ = gl.kh_; qb_ = gl.qb_; qtT = gl.qtT; ktT = gl.ktT; qbT = gl.qbT; attm = gl.attm; Sloc = gl.Sloc; Sbf = gl.Sbf; a_h = gl.a_h; o_st = gl.o_st; ocnt = gl.ocnt
            gla_gates = gl.gates; gla_state_update = gl.state_update
            for di in range(2):
                memset('dve', Sloc, 0.0, [('Sloc', h) for h in range(GH)])
                memset('pool', Sbf, 0.0, [('Sbf', h) for h in range(GH)])
                memset('pool', Rsum, 0.0, ['Rsum'])
                order = list(range(NTG)) if di == 0 else list(range(NTG - 1, -1, -1))
                mask = (M_le, M_ge)[di]
                qbdst = (qbf, qbb)[di]
                odst = (o_f, o_b)[di]
                for oi, i in enumerate(order):
                    t0 = 128 * i
                    ib = oi % 2
                    first = (oi == 0)
                    P.dma('sp', qk_in[ib][:, 0, :], q_s[t0:t0 + 128, :], reads=['q_s'], writes=[('qk_in', ib, 0)])
                    P.dma('sp', qk_in[ib][:, 1, :], k_s[t0:t0 + 128, :], reads=['k_s'], writes=[('qk_in', ib, 1)])
                    P.dma('sp', v_in[ib], v_s[t0:t0 + 128, :], reads=['v_s'], writes=[('v_in', ib)])
                    gla_gates(di, t0, 128, first)
                    qin = qk_in[ib][:, 0, :]
                    kin = qk_in[ib][:, 1, :]
                    tt('dve', qt_, qin, Ep, ALU.mult, [('qk_in', ib, 0), 'Ep'], ['qt_'])
                    tt('pool', kt_, kin, Em, ALU.mult, [('qk_in', ib, 1), 'Em'], ['kt_'])
                    if first:
                        cp('pool', qb_, qt_, ['qt_'], ['qb_'])
                    else:
                        tt('pool', qb_, qt_, Pb, ALU.mult, ['qt_', 'Pb'], ['qb_'])
                    for (src, dstT_, key) in ((qt_, qtT, 'qtT'), (kt_, ktT, 'ktT'), (qb_, qbT, 'qbT')):
                        pb = nextbank()
                        pv = bankbf(pb)
                        for h in range(GH):
                            tr(pv[:, 128 * h:128 * h + 128], src[:, 128 * h:128 * h + 128], ident, [key[:-1] + '_'] + CK, [('ps', pb)])
                        cp('act', dstT_, pv[:, :128 * GH].rearrange("p (h t) -> p h t", t=128), [('ps', pb)], [key])
                    P.dma('sp', qbdst.rearrange("(h p) t -> p h t", p=128)[:, :, t0:t0 + 128], qbT, reads=['qbT'], writes=[uk()])
                    ob0 = nextbank(VW // 512)
                    osb = ocnt[0] % 2
                    ocnt[0] += 1
                    nob = VW // 512
                    for h in range(GH):
                        b = (ob0 + nob + (h % (8 - nob))) % 8
                        mm(bank(b)[:, :128], ktT[:, h, :], qtT[:, h, :], True, True, ['ktT', 'qtT'], [('ps', b)])
                        ab = h % 2
                        tt('dve', attm[ab], bank(b)[:, :128], mask, ALU.mult, [('ps', b)] + CK, [('attm', ab)])
                        ob = (ob0 + (256 * h) // 512) % 8
                        oc = (256 * h) % 512
                        oap = bank(ob)[:, oc:oc + 256]
                        mm(oap, attm[ab], v_in[ib][:, 256 * h:256 * h + 256], True, False, [('attm', ab), ('v_in', ib)], [('ps', ob)])
                        mm(oap, qtT[:, h, :], Sbf[:, h, :], False, True, ['qtT', ('Sbf', h)], [('ps', ob)])
                    for j in range(VW // 512):
                        ob = (ob0 + j) % 8
                        cp('act' if j % 2 else 'dve', o_st[osb][:, 512 * j:512 * j + 512], bank(ob), [('ps', ob)], [('o_st', osb)])
                    P.dma('sp', odst[t0:t0 + 128, :], o_st[osb], reads=[('o_st', osb)], writes=[uk()])
                    gla_state_update(di, 128, kin, v_in[ib], [('qk_in', ib, 1)], [('v_in', ib)])
                    tt('pool', Rsum, Rsum, sp_t, ALU.add, ['Rsum', 'sp_t'], ['Rsum'])
            A.reset(m1)
            P.barrier()

            P.stage = 'M'
            A.reset(pers)
            mM = A.mark()
            Qn = A.alloc([MH, T], BF16)
            Qp = A.alloc([MH, T], BF16)
            P.dma('sp', Qn, QnT.rearrange("(h p) t -> p h t", p=128), reads=['QnT'], writes=['Qn'])
            P.dma('sp', Qp[:64], QpT.rearrange("(h p) t -> p h t", p=64), reads=['QpT'], writes=['Qp'])
            NKT = NC_ * NT
            NK = NKT * 128 + NM
            Kpe = A.alloc([NK], BF16)
            r1 = recv1.ap()
            for r in range(NC_):
                P.dma('sp', Kpe[:64, r * T:(r + 1) * T], r1[r * R1 + MH * 128: r * R1 + MH * 128 + 64, :], reads=['recv1'], writes=['Kpe'])
            P.dma('sp', Kpe[:64, NKT * 128:NK], metaK[MH * 128:MH * 128 + 64, :], reads=['metaK'], writes=['Kpe'])
            Kh = [A.alloc([NK], BF16) for _ in range(2)]
            Vh = [A.alloc([NKT + 1, 128], BF16) for _ in range(2)]
            Pt = [A.alloc([T], BF16) for _ in range(3)]
            rec = A.alloc([T], F32)
            ost = [A.alloc([T], BF16) for _ in range(2)]
            nseg = len(TS)
            assert nseg <= 2
            def load_head(h):
                hb = h % 2
                for r in range(NC_):
                    P.dma('sp', Kh[hb][:, r * T:(r + 1) * T], r1[r * R1 + h * 128: r * R1 + h * 128 + 128, :], reads=['recv1'], writes=[('Kh', hb, r)])
                    vsrc = r1[r * R1 + MH * 128 + 64 + h * 128: r * R1 + MH * 128 + 64 + (h + 1) * 128, :]
                    P.dma('sp', Vh[hb][:, r * NT:(r + 1) * NT, :], vsrc.rearrange("p (kt d) -> p kt d", d=128), reads=['recv1'], writes=[('Vh', hb, r)])
                P.dma('sp', Kh[hb][:, NKT * 128:NK], metaK[h * 128:h * 128 + 128, :], reads=['metaK'], writes=[('Kh', hb, NC_)])
                P.dma('sp', Vh[hb][:NM, NKT, :], metaV[:, h * 128:h * 128 + 128], reads=['metaV'], writes=[('Vh', hb, NC_)])
            load_head(0)
            for h in range(MH):
                hb = h % 2
                if h + 1 < MH:
                    load_head(h + 1)
                def emit_S(kt):
                    nk = 128 if kt < NKT else NM
                    k0 = kt * 128
                    sbk = 4 + 2 * (kt % 2)
                    for si, (t0, n) in enumerate(TS):
                        b = sbk + si
                        mm(bank(b)[:nk, :n], Kh[hb][:, k0:k0 + nk], Qn[:, h, t0:t0 + n], True, False, [('Kh', hb, min(kt // NT, NC_)), 'Qn'], [('ps', b)])
                        mm(bank(b)[:nk, :n], Kpe[:64, k0:k0 + nk], Qp[:64, h, t0:t0 + n], False, True, ['Kpe', 'Qp'], [('ps', b)])

                emit_S(0)
                for kt in range(NKT + 1):
                    nk = 128 if kt < NKT else NM
                    sbk = 4 + 2 * (kt % 2)
                    pi = kt % 3
                    if kt + 1 <= NKT:
                        emit_S(kt + 1)
                    P.op('act', (lambda e, o_=Pt[pi][:nk, :T], i_=PS[:nk, 512 * sbk:512 * sbk + T]:
                                 e.activation(out=o_, in_=i_, func=AF.Exp, bias=0.0, scale=1.0)),
                         [('ps', sbk + si) for si in range(nseg)], [('Pt', pi)])
                    for si, (t0, n) in enumerate(TS):
                        mm(bank(si)[:, :n], Vh[hb][:nk, kt, :], Pt[pi][:nk, t0:t0 + n], kt == 0, kt == NKT, [('Vh', hb, min(kt // NT, NC_)), ('Pt', pi)], [('ps', si)])
                        mm(bank(2 + si)[:, :n], ones_b[:nk, :], Pt[pi][:nk, t0:t0 + n], kt == 0, kt == NKT, [('Pt', pi)] + CK, [('ps', 2 + si)])
                for si, (t0, n) in enumerate(TS):
                    P.op('dve', (lambda e, o_=rec[:, t0:t0 + n], i_=bank(2 + si)[:, :n]: e.reciprocal(out=o_, in_=i_)),
                         [('ps', 2 + si)], [('rec', si)])
                    tt('dve', ost[hb][:, t0:t0 + n], bank(si)[:, :n], rec[:, t0:t0 + n], ALU.mult, [('ps', si), ('rec', si)], [('ost', hb, si)])
                    P.dma('sp', OT[h * 128:h * 128 + 128, t0:t0 + n], ost[hb][:, t0:t0 + n], reads=[('ost', hb, si)], writes=[uk()])
            A.reset(mM)
            P.barrier()

            P.stage = 'G2b'
            A.reset(pers)
            VC = VW // 128
            mgT = A.alloc([KC, T], BF16)
            mWO = A.mark()
            ogT = A.alloc([VC, T], BF16)
            mF2 = A.mark()
            qbl = [A.alloc([2, GH, 128], BF16) for _ in range(2)]
            ofl = [A.alloc([2, VW], F32) for _ in range(2)]
            ggl = [A.alloc([VW], BF16) for _ in range(2)]
            osum = A.alloc([VW], F32)
            osq = A.alloc([VW], F32)
            ssh = A.alloc([GH], F32)
            rsh = A.alloc([GH], F32)
            ogb = A.alloc([VW], BF16)
            for i in range(NT):
                t0 = 128 * i
                ib = i % 2
                P.dma('sp', qbl[ib][:, 0], qbf.rearrange("(h p) t -> p h t", p=128)[:, :, t0:t0 + 128], reads=['qb_d'], writes=[('qbl', ib)])
                P.dma('sp', qbl[ib][:, 1], qbb.rearrange("(h p) t -> p h t", p=128)[:, :, t0:t0 + 128], reads=['qb_d'], writes=[('qbl', ib)])
                P.dma('sp', ofl[ib][:, 0, :], o_f[t0:t0 + 128, :], reads=['o_d'], writes=[('ofl', ib)])
                P.dma('sp', ofl[ib][:, 1, :], o_b[t0:t0 + 128, :], reads=['o_d'], writes=[('ofl', ib)])
                P.dma('sp', ggl[ib], gg_s[t0:t0 + 128, :], reads=['gg_s'], writes=[('ggl', ib)])
                tt('pool', osum, ofl[ib][:, 0, :], ofl[ib][:, 1, :], ALU.add, [('ofl', ib)], ['osum'])
                ob0 = nextbank(VW // 512)
                for h in range(GH):
                    ob = (ob0 + (256 * h) // 512) % 8
                    oc = (256 * h) % 512
                    oap = bank(ob)[:, oc:oc + 256]
                    mm(oap, qbl[ib][:, 0, h, :], Sst[0][:, h, :], True, False, [('qbl', ib), ('Sst', 0)], [('ps', ob)])
                    mm(oap, qbl[ib][:, 1, h, :], Sst[1][:, h, :], False, True, [('qbl', ib), ('Sst', 1)], [('ps', ob)])
                for j in range(VW // 512):
                    ob = (ob0 + j) % 8
                    tt('dve', osum[:, 512 * j:512 * j + 512], osum[:, 512 * j:512 * j + 512], bank(ob), ALU.add, ['osum', ('ps', ob)], ['osum'])
                tt('pool', osq, osum, osum, ALU.mult, ['osum'], ['osq'])
                P.op('dve', lambda e: e.tensor_reduce(out=ssh, in_=osq.rearrange("p (h d) -> p h d", d=256), axis=AX.X, op=ALU.add),
                     ['osq'], ['ssh'])
                rstd_from_ss('dve', rsh, ssh, 256, ['ssh'], ['rsh'])
                for h in range(GH):
                    stt('dve', ogb[:, 256 * h:256 * h + 256], osum[:, 256 * h:256 * h + 256], rsh[:, h:h + 1],
                        ggl[ib][:, 256 * h:256 * h + 256], ALU.mult, ALU.mult, ['osum', 'rsh', ('ggl', ib)], [('ogb', h)])
                for q0 in range(0, VC, 8):
                    pb = nextbank()
                    pv = bankbf(pb)
                    nq = min(8, VC - q0)
                    for kk in range(nq):
                        k = q0 + kk
                        tr(pv[:, 128 * kk:128 * kk + 128], ogb[:, 128 * k:128 * k + 128], ident, [('ogb', k // 2)] + CK, [('ps', pb)])
                    cp('act', ogT[:, q0:q0 + nq, t0:t0 + 128], pv[:, :128 * nq].rearrange("p (k t) -> p k t", t=128), [('ps', pb)], [('ogT', i)])
            A.reset(mF2)
            P.barrier()

            P.stage = 'MG'
            OTs = A.alloc([MVW // 128, T], BF16)
            P.dma('sp', OTs, OT.rearrange("(k p) t -> p k t", p=128), reads=['OT'], writes=['OTs'])
            ws2 = WStream(max(VC, MVW // 128))
            gl = [A.alloc([2, T], BF16) for _ in range(2)]
            t1 = A.alloc([T], F32)
            mcnt = [0]
            OGK = [('ogT', i) for i in range(NT)]
            for j0 in range(0, D, 512):
                nb = min(512, D - j0)
                sA = ws2.load(w_gla_out, j0, nb, VC)
                sB = ws2.load(w_mla_out, j0, nb, MVW // 128)
                for jj in range(0, nb, 128):
                    cc = j0 + jj
                    gbf = mcnt[0] % 2
                    mcnt[0] += 1
                    P.dma('sp', gl[gbf][:, 0, :], sga[cc:cc + 128, :], reads=['ga'], writes=[('gl', gbf)])
                    P.dma('sp', gl[gbf][:, 1, :], sgb[cc:cc + 128, :], reads=['gb'], writes=[('gl', gbf)])
                    b0 = nextbank(nseg)
                    for si, (t0, n) in enumerate(TS):
                        b = (b0 + si) % 8
                        for k in range(VC):
                            mm(bank(b)[:, :n], ws2.buf[sA][:, k, jj:jj + 128], ogT[:, k, t0:t0 + n], k == 0, k == VC - 1,
                               ws2.rkeys(sA, k) + OGK, [('ps', b)])
                        tt('dve', t1[:, t0:t0 + n], bank(b)[:, :n], gl[gbf][:, 0, t0:t0 + n], ALU.mult, [('ps', b), ('gl', gbf)], [('t1', si)])
                    b0 = nextbank(nseg)
                    for si, (t0, n) in enumerate(TS):
                        b = (b0 + si) % 8
                        for k in range(MVW // 128):
                            mm(bank(b)[:, :n], ws2.buf[sB][:, k, jj:jj + 128], OTs[:, k, t0:t0 + n], k == 0, k == MVW // 128 - 1,
                               ws2.rkeys(sB, k) + ['OTs'], [('ps', b)])
                        tt('dve', gl[gbf][:, 1, t0:t0 + n], bank(b)[:, :n], gl[gbf][:, 1, t0:t0 + n], ALU.mult, [('ps', b), ('gl', gbf)], [('gl', gbf)])
                        tt('pool', mgT[:, cc // 128, t0:t0 + n], t1[:, t0:t0 + n], gl[gbf][:, 1, t0:t0 + n], ALU.add,
                           [('t1', si), ('gl', gbf)], [('mgT', si)])
            P.barrier()
            P.stage = 'WO'
            A.reset(mWO)
            ws3 = WStream(KC)
            xl = [A.alloc([512], F32) for _ in range(3)]
            wcnt = [0]

            def epi_wo(c0, nb, ti, b, ps):
                sb = wcnt[0] % 3
                wcnt[0] += 1
                t0 = 128 * ti
                P.dma('sp', xl[sb][:, :nb], x[t0:t0 + 128, c0:c0 + nb], writes=[('xl', sb)])
                tt('dve', xl[sb][:, :nb], xl[sb][:, :nb], ps, ALU.add, [('xl', sb), ('ps', b)], [('xl', sb)])
                P.dma('sp', h1[t0:t0 + 128, c0:c0 + nb], xl[sb][:, :nb], reads=[('xl', sb)], writes=[uk()])
            linear_b(ws3, w_o, KC, blocks_of(0, D), mgT, [('mgT', si) for si in range(nseg)], tilesO, epi_wo)
            P.barrier()

            P.stage = 'A2'
            A.reset(pers)
            hnT = A.alloc([KC, T], BF16)
            stage_norm([(h1[128 * i:128 * i + 128, :], 128, 128 * i) for i in range(NT)], ln2T, hnT)
            P.barrier()
            HK = [('nT', 128 * i) for i in range(NT)]

            P.stage = 'F1'
            mf1 = A.mark()
            ws4 = WStream(KC)
            ur = [A.alloc([T], F32) for _ in range(2)]
            ub = [A.alloc([T], BF16) for _ in range(2)]
            ucnt = [0]

            def epi_f1(cc, m, si, b, ps):
                t0, n = TS[si]
                sb = ucnt[0] % 2
                act(ur[sb][:, t0:t0 + n], ps, AF.Relu, [('ps', b)], [('ur', sb, si)])
                tt('dve', ub[sb][:, t0:t0 + n], ur[sb][:, t0:t0 + n], ur[sb][:, t0:t0 + n], ALU.mult, [('ur', sb, si)], [('ub', sb, si)])
                P.dma('sp', uT[cc:cc + 128, t0:t0 + n], ub[sb][:, t0:t0 + n], reads=[('ub', sb, si)], writes=[uk()])
                if si == nseg - 1:
                    ucnt[0] += 1
            linear_a(ws4, w_ff1, KC, blocks_of(0, DFF), hnT, HK, TS, epi_f1)
            P.barrier()

            P.stage = 'F2'
            A.reset(pers)
            yacc = A.alloc([NT, D], F32)
            for i in range(NT):
                P.dma('sp', yacc[:, i, :], h1[128 * i:128 * i + 128, :], reads=['h1'], writes=[('yacc', i)])
            FB = 8
            mUL = A.mark()
            ul = [A.alloc([FB, T], BF16) for _ in range(2)]
            w2b = [A.alloc([FB, 512], BF16) for _ in range(2)]
            w2n = [0]
            uTv = uT.rearrange("(f p) t -> p f t", p=128)
            w2v = w_ff2.rearrange("(f p) n -> p f n", p=128)
            for fb in range(0, FC, FB):
                nf = min(FB, FC - fb)
                us = (fb // FB) % 2
                P.dma('sp', ul[us][:, :nf, :], uTv[:, fb:fb + nf, :], reads=['uT'], writes=[('ul', us)])
                for c0 in range(0, D, 512):
                    nb = min(512, D - c0)
                    s = w2n[0] % 2
                    w2n[0] += 1
                    P.dma('pool', w2b[s][:, :nf, :nb], w2v[:, fb:fb + nf, c0:c0 + nb], writes=[('w2b', s)])
                    for i in range(NT):
                        b = nextbank()
                        for f in range(nf):
                            mm(bank(b)[:, :nb], ul[us][:, f, 128 * i:128 * i + 128], w2b[s][:, f, :nb], f == 0, f == nf - 1,
                               [('ul', us), ('w2b', s)], [('ps', b)])
                        tt('dve', yacc[:, i, c0:c0 + nb], yacc[:, i, c0:c0 + nb], bank(b)[:, :nb], ALU.add, [('yacc', i), ('ps', b)], [('yacc', i)])
            P.barrier()
            A.reset(mUL)
            finn = A.alloc([D], F32)
            P.dma('sp', finn, fin_in, writes=['finn'])
            fj = A.alloc([D], BF16)
            fst = A.alloc([2 * NT], F32)
            for i in range(NT):
                ss = fst[:, 2 * i:2 * i + 1]
                rs = fst[:, 2 * i + 1:2 * i + 2]
                memset('pool', ss, 0.0, [('fst', i)])
                act(fj, yacc[:, i, :], AF.Square, [('yacc', i), ('fst', i)], ['fj', ('fst', i)], accum_out=ss)
                rstd_from_ss('dve', rs, ss, D, [('fst', i)], [('frs', i)])
                stt('dve', yacc[:, i, :], yacc[:, i, :], rs, finn, ALU.mult, ALU.mult, [('yacc', i), ('frs', i), 'finn'], [('yacc', i)])
                P.dma('sp', out[128 * i:128 * i + 128, :], yacc[:, i, :], reads=[('yacc', i)], writes=[uk()])
        except _Stop:
            print('[kernel] stopped at barrier', stop)
        P.emit()
        print("[kernel] ops=%d waits=%d arena_peak=%d" % (len(P.all), P.n_waits, A.peak), flush=True)
        import os as _os3
        if _os3.environ.get('PRINTOPS'):
            for o in P.all[int(_os3.environ['PRINTOPS']):]:
                print('OP', o.idx, o.eng, 'dma' if o.dma else '', 'line', o.fn.__code__.co_firstlineno, sorted(o.deps), o.seq, flush=True)
    return nc


def _host_consts(c, core):
    T, NM, NC_ = c.T, c.NM, c.NCORES
    TT = T + NM
    idx = np.arange(128)
    s_ = idx[:, None]; t_ = idx[None, :]
    ones = np.ones((128, 128), np.float32)
    M_le = (s_ <= t_).astype(np.float32); M_gt = (s_ > t_).astype(np.float32)
    M_ge = (s_ >= t_).astype(np.float32); M_lt = (s_ < t_).astype(np.float32)
    pos = np.concatenate([NM + core * T + np.arange(T), np.arange(NM)]).astype(np.float32)
    inv_freq = (np.float32(10000.0) ** (-np.arange(0, 64, 2, dtype=np.float32) / np.float32(64))).astype(np.float32)
    ang = (pos[:, None] * inv_freq[None, :]).astype(np.float32)
    cos = np.cos(ang).astype(np.float32).T
    sin = np.sin(ang).astype(np.float32).T
    cos2 = np.zeros((128, TT), np.float32); sin2 = np.zeros((128, TT), np.float32)
    cos2[0:32] = cos; cos2[32:64] = cos; sin2[0:32] = sin; sin2[32:64] = sin
    r = np.arange(NC_)
    mf = np.broadcast_to((r == core).astype(np.float32)[None, :], (128, NC_))
    mb = np.broadcast_to((r > core).astype(np.float32)[None, :], (128, NC_))
    rot = np.zeros((128, 64), np.float32)
    for i in range(32):
        rot[i + 32, i] = -1.0
        rot[i, i + 32] = 1.0
    cbf = np.concatenate([np.eye(128, dtype=np.float32), ones, rot], axis=1).astype(ml_dtypes.bfloat16)
    return [ones, M_le, M_gt, M_ge, M_lt, cos2, sin2, mf, 1.0 - mf, mb, 1.0 - mb], cbf


def _run(c, inputs, dbg=False, stop=None, trace=False):
    T, NM, NC_ = c.T, c.NM, c.NCORES
    f = lambda a: np.ascontiguousarray(np.asarray(a, dtype=np.float32))
    x = f(inputs["x"])[0]
    gT = lambda v: f(v).reshape(-1, 128).T
    shared = {
        "meta": f(inputs["meta_tokens"]),
        "w_in": f(inputs["w_in"])[0],
        "wfa": np.concatenate([f(inputs["gla_wf"])[0], f(inputs["gla_bf"])], axis=0),
        "wba": np.concatenate([f(inputs["gla_wb"])[0], f(inputs["gla_bb"])], axis=0),
        "w_uq": f(inputs["w_uq"])[0], "w_ukv": f(inputs["w_ukv"])[0],
        "w_gla_out": f(inputs["w_gla_out"])[0], "w_mla_out": f(inputs["w_mla_out"])[0],
        "w_o": f(inputs["w_o"])[0], "w_ff1": f(inputs["w_ff1"])[0], "w_ff2": f(inputs["w_ff2"])[0],
        "glan": np.ascontiguousarray(np.broadcast_to(np.tile(f(inputs["gla_norm"])[0], c.GH)[None, :], (128, c.VW))),
        "finn": np.ascontiguousarray(np.broadcast_to(f(inputs["final_norm"])[None, :], (128, c.D))),
        "x_full": x,
    }
    posA = np.arange(NM + NC_ * T, dtype=np.float32)
    inv_freq = (np.float32(10000.0) ** (-np.arange(0, 64, 2, dtype=np.float32) / np.float32(64))).astype(np.float32)
    angA = (posA[:, None] * inv_freq[None, :]).astype(np.float32)
    cA = np.cos(angA).astype(np.float32).T
    sA = np.sin(angA).astype(np.float32).T
    shared["cosA"] = np.ascontiguousarray(np.concatenate([cA, cA], axis=0))
    shared["sinA"] = np.ascontiguousarray(np.concatenate([sA, sA], axis=0))
    tails = [gT(inputs["ln1"][0]), gT(inputs["ln2"][0]), gT(inputs["q_norm"][0]), gT(inputs["kv_norm"][0])]
    in_maps = []
    for core in range(NC_):
        parts, cbf = _host_consts(c, core)
        m = dict(shared)
        m["x"] = np.ascontiguousarray(x[core * T:(core + 1) * T])
        m["cf32"] = np.ascontiguousarray(np.concatenate(parts + tails, axis=1).astype(np.float32))
        m["cbf"] = np.ascontiguousarray(cbf)
        in_maps.append(m)
    nc = build_program(c, dbg, stop)
    res = run_bass_kernel_spmd(nc, in_maps, core_ids=list(range(NC_)), **({'trace': True} if trace else {}))
    outs = [np.asarray(res.results[i]["out"], dtype=np.float32) for i in range(NC_)]
    return np.concatenate(outs, axis=0)[None], res


def kernel(**inputs):
    c = Cfg()
    out, _ = _run(c, inputs)
    return out
```
